# Optimizing a Trainium2 kernel written in Bass

```python
import jax, jax.numpy as jnp
from jax import lax
import numpy as np


D_MODEL = 1024
BATCH = 4
SEQ = 8192
DEPTH = 2

HEAD_DIM = 64
CONV_CH = D_MODEL // 4
CONV_WIDTH = 31
ATT_WIDTH = 3 * D_MODEL // 8
N_ATT_HEADS = ATT_WIDTH // HEAD_DIM
DILATION_PAIRS = ((128, 1), (512, 4), (2048, 16))
ATT_BLOCK = 128
ALIBI_MAX_EXP = 8.0
MASK_VALUE = -1e30
REC_WIDTH = 3 * D_MODEL // 8
REC_KEY_DIM = 64
REC_VAL_DIM = 64
N_REC_HEADS = REC_WIDTH // REC_VAL_DIM
REC_KEY_WIDTH = N_REC_HEADS * REC_KEY_DIM
REC_CHUNK = 64
F_TINY = 1e-30
MIX_WIDTH = CONV_CH + ATT_WIDTH + REC_WIDTH
IN_SPLITS = (CONV_CH, CONV_CH, ATT_WIDTH, ATT_WIDTH, ATT_WIDTH,
             REC_KEY_WIDTH, REC_KEY_WIDTH, REC_KEY_WIDTH, REC_WIDTH, REC_WIDTH)
IN_COLS = sum(IN_SPLITS)
D_FF = ((8 * D_MODEL // 3 + 127) // 128) * 128
FFN_CONV_WIDTH = 3
N_MOD = 6
EPS = 1e-6

kernel_name = "hybrid_conv_dilattn_hgrn2_encoder"


def _rmsnorm(x, w):
    xf = x.astype(jnp.float32)
    y = xf * lax.rsqrt(jnp.mean(xf * xf, axis=-1, keepdims=True) + EPS)
    return (y * w.astype(jnp.float32)).astype(x.dtype)


def _layernorm(x, w, b):
    xf = x.astype(jnp.float32)
    mu = jnp.mean(xf, axis=-1, keepdims=True)
    var = jnp.mean(jnp.square(xf - mu), axis=-1, keepdims=True)
    y = (xf - mu) * lax.rsqrt(var + EPS)
    return (y * w.astype(jnp.float32) + b.astype(jnp.float32)).astype(x.dtype)


def _depthwise_conv(x, w, b=None):
    k_w, ch = w.shape
    y = lax.conv_general_dilated(
        x, w[:, None, :].astype(x.dtype), window_strides=(1,),
        padding=[(k_w // 2, k_w // 2)],
        dimension_numbers=('NWC', 'WIO', 'NWC'), feature_group_count=ch)
    if b is not None:
        y = y + b.astype(x.dtype)
    return y


def _dilated_window_attention(q, k, v, slopes, window, dilation):
    B, S, H, Dh = q.shape
    half = window // (2 * dilation)
    L = S // dilation

    def to_sub(t):
        return t.reshape(B, L, dilation, H, Dh).transpose(0, 2, 1, 3, 4).reshape(B * dilation, L, H, Dh)

    qs, ks, vs = to_sub(q), to_sub(k), to_sub(v)
    blk = min(ATT_BLOCK, L)
    nb = -(-L // blk)
    Lp = nb * blk
    span = blk + 2 * half
    qs = jnp.pad(qs, ((0, 0), (0, Lp - L), (0, 0), (0, 0)))
    pad_k = ((0, 0), (half, half + Lp - L), (0, 0), (0, 0))
    ks = jnp.pad(ks, pad_k)
    vs = jnp.pad(vs, pad_k)
    idx = (np.arange(nb) * blk)[:, None] + np.arange(span)[None, :]
    kb = ks[:, idx]
    vb = vs[:, idx]
    qb = qs.reshape(B * dilation, nb, blk, H, Dh)
    rel = np.arange(span)[None, :] - half - np.arange(blk)[:, None]
    key_pos = idx - half
    valid = ((np.abs(rel) <= half)[None]
             & (key_pos[:, None, :] >= 0) & (key_pos[:, None, :] < L))
    dist = jnp.asarray(np.abs(rel) * dilation, jnp.float32)
    scores = jnp.einsum('bnqhd,bnkhd->bnhqk', qb, kb) * (Dh ** -0.5)
    scores = scores - slopes[:, None, None] * dist
    scores = jnp.where(jnp.asarray(valid)[:, None], scores, MASK_VALUE)
    lse = jax.nn.logsumexp(scores, axis=-1)
    p = jnp.exp(scores - lse[..., None])
    o = jnp.einsum('bnhqk,bnkhd->bnqhd', p, vb).reshape(B * dilation, Lp, H, Dh)[:, :L]
    lse = lse.transpose(0, 1, 3, 2).reshape(B * dilation, Lp, H)[:, :L]

    def from_sub(t):
        t5 = t.reshape((B, dilation, L) + t.shape[2:])
        t5 = jnp.moveaxis(t5, 1, 2)
        return t5.reshape((B, S) + t5.shape[3:])

    return from_sub(o), from_sub(lse)


def _mixture_of_dilated_attention(q, k, v):
    slopes = jnp.asarray(2.0 ** (-ALIBI_MAX_EXP * np.arange(1, N_ATT_HEADS + 1) / N_ATT_HEADS), jnp.float32)
    outs, lses = [], []
    for window, dilation in DILATION_PAIRS:
        o_g, l_g = _dilated_window_attention(q, k, v, slopes, window, dilation)
        outs.append(o_g)
        lses.append(l_g)
    wts = jax.nn.softmax(jnp.stack(lses, 0), axis=0)
    return jnp.einsum('gbsh,gbshd->bshd', wts, jnp.stack(outs, 0))


def _hgrn2_scan(q, k, v, logf):
    B, S, H, dk = q.shape
    dv = v.shape[-1]
    n = S // REC_CHUNK

    def chunks(t):
        return t.reshape(B, n, REC_CHUNK, H, t.shape[-1]).transpose(1, 0, 3, 2, 4)

    qc, kc, vc = chunks(q), chunks(k), chunks(v)
    bc = jnp.cumsum(chunks(logf), axis=3)
    lower = jnp.asarray(np.tril(np.ones((REC_CHUNK, REC_CHUNK), bool)))[:, :, None]

    def step(state, inp):
        qt, kt, vt, bt = inp
        diff = bt[:, :, :, None, :] - bt[:, :, None, :, :]
        decay = jnp.where(lower, jnp.exp(jnp.where(lower, diff, 0.0)), 0.0)
        scores = jnp.einsum('bhtk,bhsk,bhtsk->bhts', qt, kt, decay)
        o = (jnp.einsum('bhts,bhsv->bhtv', scores, vt)
             + jnp.einsum('bhtk,bhkv->bhtv', qt * jnp.exp(bt), state))
        b_last = bt[:, :, -1:, :]
        state = (jnp.exp(b_last[:, :, 0, :])[..., None] * state
                 + jnp.einsum('bhsk,bhsv->bhkv', kt * jnp.exp(b_last - bt), vt))
        return state, o

    state0 = jnp.zeros((B, H, dk, dv), q.dtype)
    _, o = lax.scan(step, state0, (qc, kc, vc, bc))
    return o.transpose(1, 0, 3, 2, 4).reshape(B, S, H, dv)


def _hgrn2_gate(z, lb):
    f = lb + (1.0 - lb) * jax.nn.sigmoid(z)
    logf = jnp.log(jnp.maximum(f, F_TINY))
    k = (1.0 - lb) * jax.nn.sigmoid(-z)
    return logf, k


def _token_mixers(h, w_in, conv_w, conv_b, ln_w, ln_b, lb_fwd, lb_bwd, rec_norm_w, w_out):
    B, S, _ = h.shape
    f32 = jnp.float32
    proj = h @ w_in.astype(h.dtype)
    cuts = [int(c) for c in np.cumsum(IN_SPLITS)[:-1]]
    (a_val, a_gate, q_att, k_att, v_att,
     q_rec, z_fwd, z_bwd, i_rec, g_rec) = jnp.split(proj, cuts, axis=-1)

    a = a_val * jax.nn.sigmoid(a_gate)
    a = _depthwise_conv(a, conv_w, conv_b)
    a = jax.nn.silu(_layernorm(a, ln_w, ln_b))

    def heads(t, d):
        return t.astype(f32).reshape(B, S, -1, d)
    att = _mixture_of_dilated_attention(heads(q_att, HEAD_DIM), heads(k_att, HEAD_DIM), heads(v_att, HEAD_DIM))
    att = att.reshape(B, S, ATT_WIDTH).astype(h.dtype)

    qr = heads(jax.nn.silu(q_rec.astype(f32)), REC_KEY_DIM)
    vr = heads(i_rec, REC_VAL_DIM)
    logf_f, k_f = _hgrn2_gate(z_fwd.astype(f32), lb_fwd)
    logf_b, k_b = _hgrn2_gate(z_bwd.astype(f32), lb_bwd)
    o_f = _hgrn2_scan(qr, heads(k_f, REC_KEY_DIM), vr, heads(logf_f, REC_KEY_DIM))
    flip = lambda t: jnp.flip(t, axis=1)
    o_b = flip(_hgrn2_scan(flip(qr), flip(heads(k_b, REC_KEY_DIM)), flip(vr), flip(heads(logf_b, REC_KEY_DIM))))
    o_r = o_f + o_b
    o_r = o_r * lax.rsqrt(jnp.mean(o_r * o_r, axis=-1, keepdims=True) + EPS)
    o_r = o_r.reshape(B, S, REC_WIDTH) * rec_norm_w.astype(f32)
    rec = (o_r * jax.nn.silu(g_rec.astype(f32))).astype(h.dtype)

    mixed = jnp.concatenate([a, att, rec], axis=-1)
    return mixed @ w_out.astype(h.dtype)


def _conv_ffn(h, w_up, conv_w, w_down):
    u = h @ w_up.astype(h.dtype)
    u = _depthwise_conv(u, conv_w)
    gate, val = jnp.split(u, 2, axis=-1)
    return (jax.nn.gelu(gate, approximate=False) * val) @ w_down.astype(h.dtype)


def setup_inputs(seed: int = 0) -> dict:
    key = jax.random.key(seed)
    ks = jax.random.split(key, 18)
    f32 = jnp.float32
    D = D_MODEL

    def nrm(k, shape, scale):
        return jax.random.normal(k, shape, f32) * scale

    return {
        "x": nrm(ks[0], (BATCH, SEQ, D), 1.0),
        "c": nrm(ks[1], (BATCH, D), 1.0),
        "w_ada": nrm(ks[2], (DEPTH, D, N_MOD * D), 0.5 * D ** -0.5),
        "b_ada": nrm(ks[3], (DEPTH, N_MOD * D), 0.02),
        "norm1_w": 1.0 + nrm(ks[4], (DEPTH, D), 0.05),
        "w_in": nrm(ks[5], (DEPTH, D, IN_COLS), D ** -0.5),
        "conv_a_w": nrm(ks[6], (DEPTH, CONV_WIDTH, CONV_CH), CONV_WIDTH ** -0.5),
        "conv_a_b": nrm(ks[7], (DEPTH, CONV_CH), 0.02),
        "ln_a_w": 1.0 + nrm(ks[8], (DEPTH, CONV_CH), 0.05),
        "ln_a_b": nrm(ks[9], (DEPTH, CONV_CH), 0.02),
        "lb_gamma": nrm(ks[10], (DEPTH, 2, REC_KEY_WIDTH), 1.0),
        "rec_norm_w": 1.0 + nrm(ks[11], (DEPTH, REC_WIDTH), 0.05),
        "w_out": nrm(ks[12], (DEPTH, MIX_WIDTH, D), MIX_WIDTH ** -0.5),
        "norm2_w": 1.0 + nrm(ks[13], (DEPTH, D), 0.05),
        "w_up": nrm(ks[14], (DEPTH, D, 2 * D_FF), D ** -0.5),
        "conv_f_w": nrm(ks[15], (DEPTH, FFN_CONV_WIDTH, 2 * D_FF), FFN_CONV_WIDTH ** -0.5),
        "w_down": nrm(ks[16], (DEPTH, D_FF, D), D_FF ** -0.5),
        "final_norm_w": 1.0 + nrm(ks[17], (D,), 0.05),
    }


def reference(x, c, w_ada, b_ada, norm1_w, w_in, conv_a_w, conv_a_b, ln_a_w, ln_a_b,
              lb_gamma, rec_norm_w, w_out, norm2_w, w_up, conv_f_w, w_down, final_norm_w):
    p = jax.nn.softmax(lb_gamma.astype(jnp.float32), axis=0)
    lower_bounds = jnp.cumsum(p, axis=0) - p[0:1]
    cond = jax.nn.silu(c)
    for l in range(DEPTH):
        mod = (cond @ w_ada[l] + b_ada[l]).astype(x.dtype)[:, None, :]
        sh1, sc1, g1, sh2, sc2, g2 = jnp.split(mod, N_MOD, axis=-1)
        h = _rmsnorm(x, norm1_w[l]) * (1.0 + sc1) + sh1
        x = x + g1 * _token_mixers(h, w_in[l], conv_a_w[l], conv_a_b[l], ln_a_w[l], ln_a_b[l],
                                   lower_bounds[l, 0], lower_bounds[l, 1], rec_norm_w[l], w_out[l])
        h = _rmsnorm(x, norm2_w[l]) * (1.0 + sc2) + sh2
        x = x + g2 * _conv_ffn(h, w_up[l], conv_f_w[l], w_down[l])
    return _rmsnorm(x, final_norm_w)
```

```python
import numpy as np
import concourse.bass as bass
import concourse.mybir as mybir

from contextlib import ExitStack
import ml_dtypes


from concourse.bass_utils import run_bass_kernel_spmd

EPS = 1e-6
F_TINY = 1e-30
NPBF = ml_dtypes.bfloat16


F32 = mybir.dt.float32
BF16 = mybir.dt.bfloat16
AF = mybir.ActivationFunctionType
ALU = mybir.AluOpType

N_DMA_SEM = 8


class Sched:
    def __init__(self, nc, same_engine_sync=True):
        self.nc = nc
        self.ops = []
        self.same_engine_sync = same_engine_sync
        self.eng = {
            "pe": nc.tensor, "dve": nc.vector, "act": nc.scalar,
            "pool": nc.gpsimd, "sp": nc.sync,
        }

    def op(self, eng, fn, reads=(), writes=(), kind="c", fence=False):
        self.ops.append(dict(eng=eng, fn=fn, reads=tuple(reads), writes=tuple(writes), kind=kind, fence=fence))

    def dma(self, eng, out, in_, reads=(), writes=(), **kw):
        e = self.eng[eng]
        self.op(eng, lambda: e.dma_start(out=out, in_=in_, **kw), reads, writes, kind="d")

    def emit(self, stack):
        nc = self.nc
        ops = self.ops
        n = len(ops)
        last_w = {}
        readers = {}
        deps = [set() for _ in range(n)]
        for i, o in enumerate(ops):
            for k in o["reads"]:
                if k in last_w:
                    deps[i].add(last_w[k])
            for k in o["writes"]:
                if k in last_w:
                    deps[i].add(last_w[k])
                for r in readers.get(k, ()):
                    if r != i:
                        deps[i].add(r)
            for k in o["reads"]:
                readers.setdefault(k, []).append(i)
            for k in o["writes"]:
                last_w[k] = i
                readers[k] = []
        need = [[] for _ in range(n)]
        signal = [False] * n
        last_on = {}
        for i, o in enumerate(ops):
            if o.get("fence") and o["eng"] in last_on:
                p = last_on[o["eng"]]
                need[i].append(p)
                signal[p] = True
            if o["kind"] == "c":
                last_on[o["eng"]] = i
        for i, o in enumerate(ops):
            for p in deps[i]:
                po = ops[p]
                if po["kind"] == "d":
                    need[i].append(p)
                    continue
                if po["eng"] == o["eng"]:
                    if o["kind"] == "c" and (o["eng"] == "pe" or not self.same_engine_sync):
                        continue
                need[i].append(p)
                signal[p] = True
        comp_sem = {}
        for e in ("pe", "dve", "act", "pool"):
            comp_sem[e] = stack.enter_context(nc.semaphore("s_" + e))
        dma_sems = {}
        for e in ("sp", "pool", "act"):
            dma_sems[e] = [stack.enter_context(nc.semaphore("d_%s_%d" % (e, j))) for j in range(N_DMA_SEM)]
        cnt = {e: 0 for e in comp_sem}
        dcnt = {e: 0 for e in dma_sems}
        semval = [None] * n
        waited = {}
        sem_objs = {}

        def do_wait(engname, sem, val):
            key = (engname, id(sem))
            if waited.get(key, 0) >= val:
                return
            waited[key] = val
            self.eng[engname].wait_ge(sem, val)

        for i, o in enumerate(ops):
            e = o["eng"]
            wl = {}
            for p in need[i]:
                s, v = semval[p]
                if id(s) not in wl or wl[id(s)][1] < v:
                    wl[id(s)] = (s, v)
            if o["kind"] == "d":
                j = dcnt[e]
                sem = dma_sems[e][j % N_DMA_SEM]
                if j >= N_DMA_SEM:
                    prev = 16 * (j // N_DMA_SEM)
                    if id(sem) not in wl or wl[id(sem)][1] < prev:
                        wl[id(sem)] = (sem, prev)
            for s, v in wl.values():
                do_wait(e, s, v)
            ins = o["fn"]()
            if o["kind"] == "d":
                dcnt[e] = j + 1
                v = 16 * (j // N_DMA_SEM + 1)
                ins.then_inc(sem, 16)
                semval[i] = (sem, v)
            elif signal[i]:
                cnt[e] += 1
                ins.then_inc(comp_sem[e], 1)
                semval[i] = (comp_sem[e], cnt[e])
        for e, sems in dma_sems.items():
            for j, s in enumerate(sems):
                tot = dcnt[e]
                k = (tot - j + N_DMA_SEM - 1) // N_DMA_SEM if tot > j else 0
                if k > 0:
                    nc.sync.wait_ge(s, 16 * k)
        return dict(n_ops=n, cnt=cnt, dcnt=dcnt)
TOK = 4096
TT = 512
NT = TOK // TT
D = 1024
KC = D // 128
INC = 3584


class Ctx:
    def __init__(self, same_engine_sync=True):
        self.nc = bass.Bass("TRN2", target_bir_lowering=False)
        self.st = ExitStack()
        self.S = Sched(self.nc, same_engine_sync=same_engine_sync)
        self.nps = 0

    def sb(self, name, shape, dt):
        return self.st.enter_context(self.nc.sbuf_tensor(name, list(shape), dt))

    def ps(self, name, shape=(128, 512), dt=F32):
        return self.st.enter_context(self.nc.psum_tensor(name, list(shape), dt))

    def din(self, name, shape, dt):
        return self.nc.dram_tensor(name, list(shape), dt, kind="ExternalInput").ap()

    def dout(self, name, shape, dt):
        return self.nc.dram_tensor(name, list(shape), dt, kind="ExternalOutput").ap()

    def finish(self):
        info = self.S.emit(self.st)
        self.st.close()
        return self.nc, info


def emit_mod(C, wada, bada_t, ct, col0, nchunk, modT, tagp, wtiles):
    nc, S = C.nc, C.S
    csb = C.sb(tagp + "c", [128, KC], F32)
    csl = C.sb(tagp + "cs", [128, KC], F32)
    bsb = C.sb(tagp + "b", [128, 48], F32)
    S.dma("sp", csb[:], ct, writes=[tagp + "c"])
    S.dma("sp", bsb[:], bada_t, writes=[tagp + "b"])
    S.op("act", lambda: nc.scalar.activation(out=csl[:], in_=csb[:], func=AF.Silu), reads=[tagp + "c"], writes=[tagp + "cs"])
    WB = 256
    pm = C.ps(tagp + "pm", [128, 64], F32)
    wv = wada.rearrange("(k p) n -> p k n", p=128)
    ngrp = (nchunk * 128) // WB
    for g in range(ngrp):
        tt_, key = wtiles[g % len(wtiles)]
        t = tt_[:, :, 0:WB]
        S.dma("sp", t, wv[:, :, col0 + g * WB: col0 + (g + 1) * WB], writes=[key])
        for jj in range(WB // 128):
            j = g * (WB // 128) + jj
            for k in range(KC):
                S.op("pe", (lambda t=t, jj=jj, j=j, k=k: nc.tensor.matmul(
                    pm[:, j:j + 1], t[:, k, jj * 128:(jj + 1) * 128], csl[:, k:k + 1],
                    start=(k == 0), stop=(k == KC - 1))),
                    reads=[key, tagp + "cs"], writes=[(tagp + "pm", j)])
    jb = col0 // 128
    S.op("dve", lambda: nc.vector.tensor_tensor(out=modT[:, 0:nchunk], in0=pm[:, 0:nchunk], in1=bsb[:, jb:jb + nchunk], op=ALU.add),
         reads=[(tagp + "pm", j) for j in range(nchunk)] + [tagp + "b"], writes=[tagp + "mod"])


def emit_rstd(C, S, nc, ps_stat, rstd, key_ps, key_rstd, ncols, dim, tmp):
    S.op("act", lambda: nc.scalar.activation(out=tmp[:, :ncols], in_=ps_stat[:, :ncols], func=AF.Sqrt, scale=1.0 / dim, bias=C.eps_t[:, 0:1]),
         reads=[key_ps], writes=[key_rstd + "_t"])
    S.op("dve", lambda: nc.vector.reciprocal(out=rstd[:, :ncols], in_=tmp[:, :ncols]), reads=[key_rstd + "_t"], writes=[key_rstd])


def build_A(layer):
    C = Ctx()
    nc, S = C.nc, C.S
    xT = C.din("xT", [D, TOK], F32)
    ct = C.din("ct", [128, KC], F32)
    wada = C.din("wada", [D, 6144], F32)
    bada = C.din("bada", [128, 48], F32)
    n1w = C.din("n1w", [128, KC], F32)
    win = C.din("win", [D, INC], F32)
    lbg = C.din("lbg", [128, 2 * 768], F32)
    o_aT = C.dout("o_aT", [256, TOK], BF16)
    o_qkT = C.dout("o_qkT", [768, TOK], BF16)
    o_gT = C.dout("o_gT", [384, TOK], F32)
    o_v = C.dout("o_v", [TOK, 384], BF16)
    o_qr = C.dout("o_qr", [TOK, 384], F32)
    o_lf = C.dout("o_lf", [TOK, 768], F32)
    o_kk = C.dout("o_kk", [TOK, 768], F32)
    o_ir = C.dout("o_ir", [TOK, 384], BF16)

    C.eps_t = C.sb("eps", [128, 1], F32)
    S.op("dve", lambda: nc.vector.memset(C.eps_t[:], EPS), writes=["eps"])
    ones = C.sb("ones", [128, 128], BF16)
    S.op("dve", lambda: nc.vector.memset(ones[:], 1.0), writes=["ones"])

    wsb = C.sb("wsb", [128, KC, INC], BF16)
    wv = win.rearrange("(k p) n -> p k n", p=128)
    for k in range(KC):
        S.dma("pool", wsb[:, k, :], wv[:, k, :], writes=[("wsb", k)])
    WKEYS = [("wsb", k) for k in range(KC)]

    modT = C.sb("modT", [128, 16], F32)
    xt = [C.sb("xt%d" % i, [128, KC, TT], F32) for i in range(2)]
    emit_mod(C, wada, bada, ct, 0, 16, modT, "m_", [(xt[0], "xt0"), (xt[1], "xt1")])
    nw = C.sb("nw", [128, KC], F32)
    S.dma("sp", nw[:], n1w, writes=["nw"])
    scl = C.sb("scl", [128, KC], F32)
    S.op("dve", lambda: nc.vector.scalar_tensor_tensor(out=scl[:], in0=modT[:, 8:16], scalar=1.0, in1=nw[:], op0=ALU.add, op1=ALU.mult),
         reads=["m_mod", "nw"], writes=["scl"])

    lbt = C.sb("lbt", [128, 768], F32)
    oml = C.sb("oml", [128, 768], F32)
    if layer == 0:
        S.op("dve", lambda: nc.vector.memset(lbt[:], 0.0), writes=["lbt"])
        S.op("dve", lambda: nc.vector.memset(oml[:], 1.0), writes=["oml"])
    else:
        lg = C.sb("lg", [128, 2 * 768], F32)
        S.dma("sp", lg[:], lbg, writes=["lg"])
        ex = C.sb("lbex", [128, 2 * 768], F32)
        S.op("act", lambda: nc.scalar.activation(out=ex[:], in_=lg[:], func=AF.Exp), reads=["lg"], writes=["lbex"])
        sm = C.sb("lbsm", [128, 768], F32)
        S.op("dve", lambda: nc.vector.tensor_tensor(out=sm[:], in0=ex[:, 0:768], in1=ex[:, 768:1536], op=ALU.add), reads=["lbex"], writes=["lbsm"])
        S.op("dve", lambda: nc.vector.reciprocal(out=sm[:], in_=sm[:]), reads=["lbsm"], writes=["lbsm"])
        S.op("dve", lambda: nc.vector.tensor_tensor(out=lbt[:], in0=ex[:, 768:1536], in1=sm[:], op=ALU.mult), reads=["lbex", "lbsm"], writes=["lbt"])
        S.op("dve", lambda: nc.vector.tensor_scalar(out=oml[:], in0=lbt[:], scalar1=-1.0, scalar2=1.0, op0=ALU.mult, op1=ALU.add), reads=["lbt"], writes=["oml"])

    hs = [C.sb("hs%d" % i, [128, KC, TT], BF16) for i in range(2)]
    sq = C.sb("sq", [128, KC, TT], BF16)
    rstd = C.sb("rstd", [128, TT], F32)
    rtmp = C.sb("rtmp", [128, TT], F32)
    ps_stat = C.ps("ps_stat")
    psf = [C.ps("psf%d" % i) for i in range(3)]
    pst = [C.ps("pst%d" % i) for i in range(3)]
    NST = 3
    stf_b = [C.sb("stfb%d" % i, [128, TT], BF16) for i in range(NST)]
    stf_f = [C.sb("stff%d" % i, [128, TT], F32) for i in range(NST)]
    sg = [C.sb("sg%d" % i, [128, TT], F32) for i in range(2)]
    st_v = [C.sb("stv%d" % i, [128, 384], BF16) for i in range(2)]
    st_qr = [C.sb("stqr%d" % i, [128, 384], F32) for i in range(2)]
    st_s = [C.sb("sts%d" % i, [128, 768], F32) for i in range(2)]
    st_lf = [C.sb("stlf%d" % i, [128, 768], F32) for i in range(2)]
    st_kk = [C.sb("stkk%d" % i, [128, 768], F32) for i in range(2)]
    st_ir = [C.sb("stir%d" % i, [128, 384], BF16) for i in range(2)]
    xv = xT.rearrange("(k p) t -> p k t", p=128)
    cnt = {"f": 0, "t": 0, "st": 0, "tm": 0}

    def prep(i):
        b = i % 2
        x = xt[b]
        S.dma("sp", x[:], xv[:, :, i * TT:(i + 1) * TT], writes=["xt%d" % b])
        S.op("act", lambda: nc.scalar.activation(out=sq[:], in_=x[:], func=AF.Square), reads=["xt%d" % b], writes=["sq"])
        for k in range(KC):
            S.op("pe", (lambda k=k: nc.tensor.matmul(ps_stat[:], ones[:], sq[:, k, :], start=(k == 0), stop=(k == KC - 1))),
                 reads=["sq", "ones"], writes=["ps_stat"])
        emit_rstd(C, S, nc, ps_stat, rstd, "ps_stat", "rstd", TT, D, rtmp)
        for k in range(KC):
            S.op("dve", (lambda k=k: nc.vector.scalar_tensor_tensor(out=x[:, k, :], in0=x[:, k, :], scalar=scl[:, k:k + 1], in1=rstd[:],
                                                                    op0=ALU.mult, op1=ALU.mult)),
                 reads=["xt%d" % b, "scl", "rstd"], writes=["xt%d" % b])
        for k in range(KC):
            S.op("act", (lambda k=k: nc.scalar.activation(out=hs[b][:, k, :], in_=x[:, k, :], func=AF.Identity, bias=modT[:, k:k + 1])),
                 reads=["xt%d" % b, "m_mod"], writes=["hs%d" % b])

    def fm_group(i, colchunk):
        b = i % 2
        p = cnt["f"] % 3
        cnt["f"] += 1
        pt = psf[p]
        for k in range(KC):
            S.op("pe", (lambda k=k: nc.tensor.matmul(pt[:], wsb[:, k, colchunk * 128:(colchunk + 1) * 128], hs[b][:, k, :],
                                                     start=(k == 0), stop=(k == KC - 1))),
                 reads=WKEYS + ["hs%d" % b], writes=["psf%d" % p])
        return pt, "psf%d" % p

    def main(i):
        b = i % 2
        t0 = i * TT
        for c in range(2):
            pg, kg = fm_group(i, 2 + c)
            s = cnt["st"] % 2
            cnt["st"] += 1
            S.op("act", (lambda pg=pg, s=s: nc.scalar.activation(out=sg[s][:], in_=pg[:], func=AF.Sigmoid)), reads=[kg], writes=["sg%d" % s])
            pv, kv = fm_group(i, c)
            q = cnt["tm"] % NST
            cnt["tm"] += 1
            S.op("dve", (lambda pv=pv, s=s, q=q: nc.vector.tensor_tensor(out=stf_b[q][:], in0=pv[:], in1=sg[s][:], op=ALU.mult)),
                 reads=[kv, "sg%d" % s], writes=["stfb%d" % q])
            S.dma("pool", o_aT[c * 128:(c + 1) * 128, t0:t0 + TT], stf_b[q][:], reads=["stfb%d" % q])
        for c in range(6):
            pq, kq = fm_group(i, 4 + c)
            q = cnt["tm"] % NST
            cnt["tm"] += 1
            S.op("act", (lambda pq=pq, q=q: nc.scalar.copy(out=stf_b[q][:], in_=pq[:])), reads=[kq], writes=["stfb%d" % q])
            S.dma("pool", o_qkT[c * 128:(c + 1) * 128, t0:t0 + TT], stf_b[q][:], reads=["stfb%d" % q])
        for c in range(3):
            pq, kq = fm_group(i, 25 + c)
            q = cnt["tm"] % NST
            cnt["tm"] += 1
            S.op("act", (lambda pq=pq, q=q: nc.scalar.activation(out=stf_f[q][:], in_=pq[:], func=AF.Silu)), reads=[kq], writes=["stff%d" % q])
            S.dma("pool", o_gT[c * 128:(c + 1) * 128, t0:t0 + TT], stf_f[q][:], reads=["stff%d" % q])
        def sub(su):
            r0 = t0 + su * 128
            sbi = cnt["t"] % 2
            cnt["t"] += 1

            def tm_group(col0, ncol):
                p = cnt["tm"] % 3
                cnt["tm"] += 1
                pt = pst[p]
                for k in range(KC):
                    S.op("pe", (lambda k=k, pt=pt: nc.tensor.matmul(pt[:, :ncol], hs[b][:, k, su * 128:(su + 1) * 128], wsb[:, k, col0:col0 + ncol],
                                                                    start=(k == 0), stop=(k == KC - 1))),
                         reads=WKEYS + ["hs%d" % b], writes=["pst%d" % p])
                return pt, "pst%d" % p
            pt, kp = tm_group(1280, 384)
            S.op("act", (lambda pt=pt: nc.scalar.copy(out=st_v[sbi][:], in_=pt[:, :384])), reads=[kp], writes=["stv%d" % sbi])
            S.dma("pool", o_v[r0:r0 + 128, :], st_v[sbi][:], reads=["stv%d" % sbi])
            pt, kp = tm_group(1664, 384)
            S.op("act", (lambda pt=pt: nc.scalar.activation(out=st_qr[sbi][:], in_=pt[:, :384], func=AF.Silu)), reads=[kp], writes=["stqr%d" % sbi])
            S.dma("pool", o_qr[r0:r0 + 128, :], st_qr[sbi][:], reads=["stqr%d" % sbi])
            for dd in range(2):
                pt, kp = tm_group(2048 + dd * 384, 384)
                sl = slice(dd * 384, (dd + 1) * 384)
                S.op("act", (lambda pt=pt, sl=sl: nc.scalar.activation(out=st_s[sbi][:, sl], in_=pt[:, :384], func=AF.Sigmoid)),
                     reads=[kp], writes=[("sts%d" % sbi, dd)])
            ks = "sts%d" % sbi
            S.op("dve", lambda: nc.vector.tensor_tensor(out=st_s[sbi][:], in0=st_s[sbi][:], in1=oml[:], op=ALU.mult),
                 reads=[(ks, 0), (ks, 1), "oml"], writes=[(ks, 0), (ks, 1)])
            S.op("dve", lambda: nc.vector.scalar_tensor_tensor(out=st_s[sbi][:], in0=st_s[sbi][:], scalar=F_TINY, in1=lbt[:], op0=ALU.max, op1=ALU.add),
                 reads=[(ks, 0), (ks, 1), "lbt"], writes=[(ks, 0), (ks, 1)])
            S.op("act", lambda: nc.scalar.activation(out=st_lf[sbi][:], in_=st_s[sbi][:], func=AF.Ln), reads=[(ks, 0), (ks, 1)], writes=["stlf%d" % sbi])
            S.op("dve", lambda: nc.vector.tensor_scalar(out=st_kk[sbi][:], in0=st_s[sbi][:], scalar1=-1.0, scalar2=1.0, op0=ALU.mult, op1=ALU.add),
                 reads=[(ks, 0), (ks, 1)], writes=["stkk%d" % sbi])
            S.dma("pool", o_lf[r0:r0 + 128, :], st_lf[sbi][:], reads=["stlf%d" % sbi])
            S.dma("pool", o_kk[r0:r0 + 128, :], st_kk[sbi][:], reads=["stkk%d" % sbi])
            pt, kp = tm_group(2816, 384)
            S.op("act", (lambda pt=pt: nc.scalar.copy(out=st_ir[sbi][:], in_=pt[:, :384])), reads=[kp], writes=["stir%d" % sbi])
            S.dma("pool", o_ir[r0:r0 + 128, :], st_ir[sbi][:], reads=["stir%d" % sbi])

        for su in range(TT // 128):
            sub(su)

    prep(0)
    for i in range(NT):
        if i + 1 < NT:
            prep(i + 1)
        main(i)
    return C.finish()
NPAD = TOK + 2
CT = 256
CSTEP = CT - 2
NCT = (TOK + CSTEP - 1) // CSTEP
DFF = 2816
NJ = DFF // 128


def build_CD(final):
    C = Ctx()
    nc, S = C.nc, C.S
    xT = C.din("xT", [D, NPAD], F32)
    acT = C.din("acT", [256, NPAD], BF16)
    atT = C.din("atT", [384, NPAD], BF16)
    ofT = C.din("ofT", [384, NPAD], F32)
    obT = C.din("obT", [384, NPAD], F32)
    gT = C.din("gT", [384, NPAD], F32)
    rnw = C.din("rnw", [128, 3], F32)
    ct = C.din("ct", [128, KC], F32)
    wada = C.din("wada", [D, 6144], F32)
    bada = C.din("bada", [128, 48], F32)
    n2w = C.din("n2w", [128, KC], F32)
    fnw = C.din("fnw", [128, KC], F32)
    wout = C.din("wout", [D, D], F32)
    wup = C.din("wup", [D, 2 * DFF], F32)
    wdown = C.din("wdown", [DFF, D], F32)
    cfw = C.din("cfw", [128, 44, 3], F32)
    flags = C.din("flags", [128, 2], F32)
    bdm = C.din("bdm", [128, 128], F32)
    o_xT = C.dout("o_xT", [D, TOK], F32)

    C.eps_t = C.sb("eps", [128, 1], F32)
    S.op("dve", lambda: nc.vector.memset(C.eps_t[:], EPS), writes=["eps"])
    ones = C.sb("ones", [128, 128], BF16)
    S.op("dve", lambda: nc.vector.memset(ones[:], 1.0), writes=["ones"])
    bd = C.sb("bd", [128, 128], BF16)
    S.dma("pool", bd[:], bdm, writes=["bd"])

    xt = C.sb("xt", [128, KC, CT], F32)
    modT = C.sb("modT", [128, 32], F32)
    emit_mod(C, wada, bada, ct, 2048, 32, modT, "m_", [(xt, "xt")])
    nw = C.sb("nw", [128, KC], F32)
    S.dma("sp", nw[:], n2w, writes=["nw"])
    scl = C.sb("scl", [128, KC], F32)
    S.op("dve", lambda: nc.vector.scalar_tensor_tensor(out=scl[:], in0=modT[:, 16:24], scalar=1.0, in1=nw[:], op0=ALU.add, op1=ALU.mult),
         reads=["m_mod", "nw"], writes=["scl"])
    fw = C.sb("fw", [128, KC], F32)
    S.dma("sp", fw[:], fnw, writes=["fw"])
    rw = C.sb("rw", [128, 3], F32)
    S.dma("sp", rw[:], rnw, writes=["rw"])
    cw = C.sb("cw", [128, 44, 3], F32)
    S.dma("sp", cw[:], cfw, writes=["cw"])
    fl = C.sb("fl", [128, 2], F32)
    S.dma("sp", fl[:], flags, writes=["fl"])

    wo = C.sb("wo", [128, KC, D], BF16)
    wu = C.sb("wu", [128, KC, 2 * DFF], BF16)
    wd = C.sb("wd", [128, NJ, D], BF16)
    wov = wout.rearrange("(k p) n -> p k n", p=128)
    wuv = wup.rearrange("(k p) n -> p k n", p=128)
    wdv = wdown.rearrange("(k p) n -> p k n", p=128)
    for k in range(KC):
        S.dma("pool", wo[:, k, :], wov[:, k, :], writes=[("wo", k)])
    for k in range(KC):
        S.dma("pool", wu[:, k, :], wuv[:, k, :], writes=[("wu", k)])
    for k in range(NJ):
        S.dma("pool", wd[:, k, :], wdv[:, k, :], writes=[("wd", k)])
    WO = [("wo", k) for k in range(KC)]
    WU = [("wu", k) for k in range(KC)]
    WD = [("wd", k) for k in range(NJ)]

    mx = C.sb("mx", [128, KC, CT], BF16)
    oft = C.sb("oft", [128, 3, CT], F32)
    obt = C.sb("obt", [128, 3, CT], F32)
    gt = C.sb("gt", [128, 3, CT], F32)
    h2 = C.sb("h2", [128, KC, CT], BF16)
    sq = C.sb("sq", [128, KC, CT], BF16)
    gv = C.sb("gv", [128, NJ, CT], BF16)
    rstd = C.sb("rstd", [128, CT], F32)
    rtmp = C.sb("rtmp", [128, CT], F32)
    tmpf = [C.sb("tmpf%d" % i, [128, CT], F32) for i in range(2)]
    ug = [C.sb("ug%d" % i, [128, CT], F32) for i in range(4)]
    c1 = [C.sb("c1%d" % i, [128, CT], F32) for i in range(4)]
    ost = [C.sb("ost%d" % i, [128, CT], F32) for i in range(2)]
    ps_stat = C.ps("ps_stat")
    psr = C.ps("psr")
    psm = [C.ps("psm%d" % i) for i in range(5)]
    cnt = {"p": 0, "u": 0, "g": 0, "t": 0, "o": 0}

    xv = xT.rearrange("(k p) t -> p k t", p=128)
    acv = acT.rearrange("(k p) t -> p k t", p=128)
    atv = atT.rearrange("(k p) t -> p k t", p=128)
    ofv = ofT.rearrange("(k p) t -> p k t", p=128)
    obv = obT.rearrange("(k p) t -> p k t", p=128)
    gv_ = gT.rearrange("(k p) t -> p k t", p=128)

    def newps():
        p = cnt["p"] % 5
        cnt["p"] += 1
        return psm[p], "psm%d" % p

    def do_tile(ti):
        c0 = ti * CSTEP
        w = min(CT, NPAD - c0)
        nout = w - 2
        cs = slice(c0, c0 + w)
        S.dma("sp", xt[:, :, :w], xv[:, :, cs], writes=["xt"])
        S.dma("sp", mx[:, 0:2, :w], acv[:, :, cs], writes=[("mx", 0)])
        S.dma("sp", mx[:, 2:5, :w], atv[:, :, cs], writes=[("mx", 1)])
        S.dma("sp", oft[:, :, :w], ofv[:, :, cs], writes=["oft"])
        S.dma("sp", obt[:, :, :w], obv[:, :, cs], writes=["obt"])
        S.dma("sp", gt[:, :, :w], gv_[:, :, cs], writes=["gt"])
        S.op("dve", lambda: nc.vector.tensor_tensor(out=oft[:, :, :w], in0=oft[:, :, :w], in1=obt[:, :, :w], op=ALU.add),
             reads=["oft", "obt"], writes=["oft"])
        S.op("act", lambda: nc.scalar.activation(out=sq[:, 0:3, :w], in_=oft[:, :, :w], func=AF.Square), reads=["oft"], writes=["sq"])
        for c in range(3):
            S.op("pe", (lambda c=c: nc.tensor.matmul(psr[:, :w], bd[:], sq[:, c, :w], start=True, stop=True)), reads=["sq", "bd"], writes=["psr"])
            emit_rstd(C, S, nc, psr, rstd, "psr", "rstd", w, 64, rtmp)
            S.op("dve", (lambda c=c: nc.vector.scalar_tensor_tensor(out=oft[:, c, :w], in0=oft[:, c, :w], scalar=rw[:, c:c + 1], in1=rstd[:, :w],
                                                                    op0=ALU.mult, op1=ALU.mult)),
                 reads=["oft", "rw", "rstd"], writes=["oft"])
            S.op("dve", (lambda c=c: nc.vector.tensor_tensor(out=mx[:, 5 + c, :w], in0=oft[:, c, :w], in1=gt[:, c, :w], op=ALU.mult)),
                 reads=["oft", "gt"], writes=[("mx", 2)])
        MX = [("mx", 0), ("mx", 1), ("mx", 2)]
        for oc in range(KC):
            pt, kp = newps()
            for k in range(KC):
                S.op("pe", (lambda k=k, pt=pt, oc=oc: nc.tensor.matmul(pt[:, :w], wo[:, k, oc * 128:(oc + 1) * 128], mx[:, k, :w],
                                                                      start=(k == 0), stop=(k == KC - 1))),
                     reads=WO + MX, writes=[kp])
            S.op("dve", (lambda pt=pt, oc=oc: nc.vector.scalar_tensor_tensor(out=xt[:, oc, :w], in0=pt[:, :w], scalar=modT[:, oc:oc + 1], in1=xt[:, oc, :w],
                                                                            op0=ALU.mult, op1=ALU.add)),
                 reads=[kp, "m_mod", "xt"], writes=["xt"])
        S.op("act", lambda: nc.scalar.activation(out=sq[:, :, :w], in_=xt[:, :, :w], func=AF.Square), reads=["xt"], writes=["sq"])
        for k in range(KC):
            S.op("pe", (lambda k=k: nc.tensor.matmul(ps_stat[:, :w], ones[:], sq[:, k, :w], start=(k == 0), stop=(k == KC - 1))),
                 reads=["sq", "ones"], writes=["ps_stat"])
        emit_rstd(C, S, nc, ps_stat, rstd, "ps_stat", "rstd", w, D, rtmp)
        for k in range(KC):
            q = cnt["t"] % 2
            cnt["t"] += 1
            S.op("dve", (lambda k=k, q=q: nc.vector.scalar_tensor_tensor(out=tmpf[q][:, :w], in0=xt[:, k, :w], scalar=scl[:, k:k + 1], in1=rstd[:, :w],
                                                                        op0=ALU.mult, op1=ALU.mult)),
                 reads=["xt", "scl", "rstd"], writes=["tmpf%d" % q])
            S.op("act", (lambda k=k, q=q: nc.scalar.activation(out=h2[:, k, :w], in_=tmpf[q][:, :w], func=AF.Identity, bias=modT[:, 8 + k:9 + k])),
                 reads=["tmpf%d" % q, "m_mod"], writes=["h2"])
        for j in range(NJ):
            res = []
            for part in range(2):
                col = j + part * NJ
                pt, kp = newps()
                for k in range(KC):
                    S.op("pe", (lambda k=k, pt=pt, col=col: nc.tensor.matmul(pt[:, :w], wu[:, k, col * 128:(col + 1) * 128], h2[:, k, :w],
                                                                            start=(k == 0), stop=(k == KC - 1))),
                         reads=WU + ["h2"], writes=[kp])
                u = cnt["u"] % 4
                cnt["u"] += 1
                S.op("act", (lambda pt=pt, u=u: nc.scalar.copy(out=ug[u][:, :w], in_=pt[:, :w])), reads=[kp], writes=["ug%d" % u])
                S.op("act", (lambda pt=pt, u=u, col=col: nc.scalar.activation(out=c1[u][:, :w], in_=pt[:, :w], func=AF.Copy, scale=cw[:, col, 1:2])),
                     reads=[kp, "cw"], writes=["c1%d" % u])
                if ti == 0:
                    S.op("dve", (lambda u=u: nc.vector.tensor_scalar(out=ug[u][:, 0:1], in0=ug[u][:, 0:1], scalar1=fl[:, 0:1], scalar2=None, op0=ALU.mult)),
                         reads=["ug%d" % u, "fl"], writes=["ug%d" % u])
                if ti == NCT - 1:
                    S.op("dve", (lambda u=u: nc.vector.tensor_scalar(out=ug[u][:, w - 1:w], in0=ug[u][:, w - 1:w], scalar1=fl[:, 1:2], scalar2=None, op0=ALU.mult)),
                         reads=["ug%d" % u, "fl"], writes=["ug%d" % u])
                S.op("dve", (lambda u=u, col=col: nc.vector.scalar_tensor_tensor(out=c1[u][:, 1:w - 1], in0=ug[u][:, 0:w - 2], scalar=cw[:, col, 0:1],
                                                                                in1=c1[u][:, 1:w - 1], op0=ALU.mult, op1=ALU.add)),
                     reads=["ug%d" % u, "c1%d" % u, "cw"], writes=["c1%d" % u])
                S.op("dve", (lambda u=u, col=col: nc.vector.scalar_tensor_tensor(out=c1[u][:, 1:w - 1], in0=ug[u][:, 2:w], scalar=cw[:, col, 2:3],
                                                                                in1=c1[u][:, 1:w - 1], op0=ALU.mult, op1=ALU.add)),
                     reads=["ug%d" % u, "c1%d" % u, "cw"], writes=["c1%d" % u])
                res.append(u)
            ugate, uval = res
            g = cnt["g"] % 2
            cnt["g"] += 1
            S.op("act", (lambda ugate=ugate: nc.scalar.activation(out=c1[ugate][:, 1:w - 1], in_=c1[ugate][:, 1:w - 1], func=AF.Gelu)),
                 reads=["c1%d" % ugate], writes=["c1%d" % ugate])
            S.op("dve", (lambda uval=uval, ugate=ugate, j=j: nc.vector.tensor_tensor(out=gv[:, j, 1:w - 1], in0=c1[ugate][:, 1:w - 1], in1=c1[uval][:, 1:w - 1], op=ALU.mult)),
                 reads=["c1%d" % ugate, "c1%d" % uval], writes=[("gv", j)])
        GV = [("gv", j) for j in range(NJ)]
        for oc in range(KC):
            pt, kp = newps()
            for j in range(NJ):
                S.op("pe", (lambda j=j, pt=pt, oc=oc: nc.tensor.matmul(pt[:, 1:w - 1], wd[:, j, oc * 128:(oc + 1) * 128], gv[:, j, 1:w - 1],
                                                                      start=(j == 0), stop=(j == NJ - 1))),
                     reads=WD + GV, writes=[kp])
            if not final:
                o = cnt["o"] % 2
                cnt["o"] += 1
                S.op("dve", (lambda pt=pt, oc=oc, o=o: nc.vector.scalar_tensor_tensor(out=ost[o][:, 1:w - 1], in0=pt[:, 1:w - 1], scalar=modT[:, 24 + oc:25 + oc],
                                                                                     in1=xt[:, oc, 1:w - 1], op0=ALU.mult, op1=ALU.add)),
                     reads=[kp, "m_mod", "xt"], writes=["ost%d" % o])
                S.dma("pool", o_xT[oc * 128:(oc + 1) * 128, c0:c0 + nout], ost[o][:, 1:w - 1], reads=["ost%d" % o])
            else:
                S.op("dve", (lambda pt=pt, oc=oc: nc.vector.scalar_tensor_tensor(out=xt[:, oc, 1:w - 1], in0=pt[:, 1:w - 1], scalar=modT[:, 24 + oc:25 + oc],
                                                                                in1=xt[:, oc, 1:w - 1], op0=ALU.mult, op1=ALU.add)),
                     reads=[kp, "m_mod", "xt"], writes=["xt"])
        if final:
            S.op("act", lambda: nc.scalar.activation(out=sq[:, :, 1:w - 1], in_=xt[:, :, 1:w - 1], func=AF.Square), reads=["xt"], writes=["sq"])
            for k in range(KC):
                S.op("pe", (lambda k=k: nc.tensor.matmul(ps_stat[:, 1:w - 1], ones[:], sq[:, k, 1:w - 1], start=(k == 0), stop=(k == KC - 1))),
                     reads=["sq", "ones"], writes=["ps_stat"])
            emit_rstd(C, S, nc, ps_stat, rstd, "ps_stat", "rstd", w, D, rtmp)
            for k in range(KC):
                o = cnt["o"] % 2
                cnt["o"] += 1
                S.op("dve", (lambda k=k, o=o: nc.vector.scalar_tensor_tensor(out=ost[o][:, 1:w - 1], in0=xt[:, k, 1:w - 1], scalar=fw[:, k:k + 1], in1=rstd[:, 1:w - 1],
                                                                            op0=ALU.mult, op1=ALU.mult)),
                     reads=["xt", "fw", "rstd"], writes=["ost%d" % o])
                S.dma("pool", o_xT[k * 128:(k + 1) * 128, c0:c0 + nout], ost[o][:, 1:w - 1], reads=["ost%d" % o])

    for ti in range(NCT):
        do_tile(ti)
    return C.finish()
CK = 31
CH = CK // 2


def build_Bc():
    C = Ctx()
    nc, S = C.nc, C.S
    aT = C.din("aT", [256, TOK + 2 * CH], BF16)
    cwd = C.din("cw_d", [128, 2, CK], F32)
    cbd = C.din("cb_d", [128, 2], F32)
    lwd = C.din("lw_d", [128, 2], F32)
    lbd = C.din("lb_d", [128, 2], F32)
    idd = C.din("ident_d", [128, 128], F32)
    o_ac = C.dout("o_ac", [256, TOK], BF16)

    C.eps_t = C.sb("eps", [128, 1], F32)
    S.op("dve", lambda: nc.vector.memset(C.eps_t[:], EPS), writes=["eps"])
    onesf = C.sb("onesf", [128, 128], F32)
    S.op("dve", lambda: nc.vector.memset(onesf[:], 1.0), writes=["onesf"])
    idt = C.sb("idt", [128, 128], F32)
    S.dma("sp", idt[:], idd, writes=["idt"])
    cw = C.sb("cw", [128, 2, CK], F32)
    S.dma("sp", cw[:], cwd, writes=["cw"])
    cb = C.sb("cb", [128, 2], F32)
    S.dma("sp", cb[:], cbd, writes=["cb"])
    lw = C.sb("lw", [128, 2], F32)
    S.dma("sp", lw[:], lwd, writes=["lw"])
    lb = C.sb("lb", [128, 2], F32)
    S.dma("sp", lb[:], lbd, writes=["lb"])
    dg = C.sb("dg", [128, 2, CK, 128], BF16)
    for c in range(2):
        for k in range(CK):
            S.op("dve", (lambda c=c, k=k: nc.vector.tensor_scalar(out=dg[:, c, k, :], in0=idt[:], scalar1=cw[:, c, k:k + 1], scalar2=None, op0=ALU.mult)),
                 reads=["idt", "cw"], writes=["dg"])
    at = [C.sb("at%d" % i, [128, 2, TT + 2 * CH], BF16) for i in range(2)]
    y = C.sb("y", [128, 2, TT], F32)
    ysq = C.sb("ysq", [128, 2, TT], F32)
    mean = C.sb("mean", [128, TT], F32)
    msq = C.sb("msq", [128, TT], F32)
    var = C.sb("var", [128, TT], F32)
    rstd = C.sb("rstd", [128, TT], F32)
    tmp = C.sb("tmp", [128, TT], F32)
    ob = [C.sb("ob%d" % i, [128, TT], BF16) for i in range(2)]
    pc = [C.ps("pc%d" % i) for i in range(2)]
    ps1 = C.ps("ps1")
    ps2 = C.ps("ps2")
    av = aT.rearrange("(c p) t -> p c t", p=128)
    cnt = {"o": 0}

    def load(i):
        b = i % 2
        S.dma("sp", at[b][:], av[:, :, i * TT:i * TT + TT + 2 * CH], writes=["at%d" % b])

    def do_tile(i):
        b = i % 2
        for c in range(2):
            for k in range(CK):
                S.op("pe", (lambda c=c, k=k: nc.tensor.matmul(pc[c][:], dg[:, c, k, :], at[b][:, c, k:k + TT], start=(k == 0), stop=(k == CK - 1))),
                     reads=["dg", "at%d" % b], writes=["pc%d" % c])
            S.op("act", (lambda c=c: nc.scalar.activation(out=y[:, c, :], in_=pc[c][:], func=AF.Identity, bias=cb[:, c:c + 1])),
                 reads=["pc%d" % c, "cb"], writes=[("y", c)])
            S.op("act", (lambda c=c: nc.scalar.activation(out=ysq[:, c, :], in_=y[:, c, :], func=AF.Square)), reads=[("y", c)], writes=[("ysq", c)])
        for c in range(2):
            S.op("pe", (lambda c=c: nc.tensor.matmul(ps1[:], onesf[:], y[:, c, :], start=(c == 0), stop=(c == 1))), reads=[("y", c), "onesf"], writes=["ps1"])
        for c in range(2):
            S.op("pe", (lambda c=c: nc.tensor.matmul(ps2[:], onesf[:], ysq[:, c, :], start=(c == 0), stop=(c == 1))), reads=[("ysq", c), "onesf"], writes=["ps2"])
        S.op("dve", lambda: nc.vector.tensor_scalar(out=mean[:], in0=ps1[:], scalar1=1.0 / 256, scalar2=None, op0=ALU.mult), reads=["ps1"], writes=["mean"])
        S.op("dve", lambda: nc.vector.tensor_tensor(out=msq[:], in0=mean[:], in1=mean[:], op=ALU.mult), reads=["mean"], writes=["msq"])
        S.op("dve", lambda: nc.vector.scalar_tensor_tensor(out=var[:], in0=ps2[:], scalar=1.0 / 256, in1=msq[:], op0=ALU.mult, op1=ALU.subtract),
             reads=["ps2", "msq"], writes=["var"])
        S.op("act", lambda: nc.scalar.activation(out=tmp[:], in_=var[:], func=AF.Sqrt, bias=C.eps_t[:, 0:1]), reads=["var", "eps"], writes=["tmp"])
        S.op("dve", lambda: nc.vector.reciprocal(out=rstd[:], in_=tmp[:]), reads=["tmp"], writes=["rstd"])
        for c in range(2):
            S.op("dve", (lambda c=c: nc.vector.tensor_tensor(out=y[:, c, :], in0=y[:, c, :], in1=mean[:], op=ALU.subtract)),
                 reads=[("y", c), "mean"], writes=[("y", c)])
            S.op("dve", (lambda c=c: nc.vector.scalar_tensor_tensor(out=y[:, c, :], in0=y[:, c, :], scalar=lw[:, c:c + 1], in1=rstd[:], op0=ALU.mult, op1=ALU.mult)),
                 reads=[("y", c), "lw", "rstd"], writes=[("y", c)])
            o = cnt["o"] % 2
            cnt["o"] += 1
            S.op("act", (lambda c=c, o=o: nc.scalar.activation(out=ob[o][:], in_=y[:, c, :], func=AF.Silu, bias=lb[:, c:c + 1])),
                 reads=[("y", c), "lb"], writes=["ob%d" % o])
            S.dma("pool", o_ac[c * 128:(c + 1) * 128, i * TT:(i + 1) * TT], ob[o][:], reads=["ob%d" % o])

    load(0)
    for i in range(NT):
        if i + 1 < NT:
            load(i + 1)
        do_tile(i)
    return C.finish()
SEQ = 8192
DILS = (1, 4, 16)
NU = 3


def build_Ba():
    C = Ctx()
    nc, S = C.nc, C.S
    qd, kd, vd = {}, {}, {}
    for r in DILS:
        L = SEQ // r
        qd[r] = C.din("q%d" % r, [NU, 64, r * L], BF16)
        kd[r] = C.din("k%d" % r, [NU, 64, r * (L + 128)], BF16)
        vd[r] = C.din("v%d" % r, [NU, 128, r * (L // 128 + 1), 64], BF16)
    Ed = C.din("E", [NU, 3, 128, 256], BF16)
    o_at = C.dout("o_at", [NU, 64, SEQ], BF16)

    ones = C.sb("ones", [128, 64], BF16)
    S.op("dve", lambda: nc.vector.memset(ones[:], 1.0), writes=["ones"])
    qs = [C.sb("qs%d" % i, [64, SEQ], BF16) for i in range(2)]
    ks = [C.sb("ks%d" % i, [64, SEQ + 128 * 16], BF16) for i in range(2)]
    vs = [C.sb("vs%d" % i, [128, SEQ // 128 + 16, 64], BF16) for i in range(2)]
    Et = [C.sb("Et%d" % i, [128, 256], BF16) for i in range(2)]
    acc = C.sb("acc", [64, 2, SEQ], F32)
    pt = [C.sb("pt%d" % i, [128, 256], BF16) for i in range(3)]
    ost = [C.sb("ost%d" % i, [64, 2048], BF16) for i in range(2)]
    pss = [C.ps("pss%d" % i, [128, 256], F32) for i in range(3)]
    pso = [C.ps("pso%d" % i, [64, 256], F32) for i in range(3)]
    cnt = {"l": 0, "p": 0, "o": 0}

    def load(u, bi):
        r = DILS[bi]
        L = SEQ // r
        b = cnt["l"] % 2
        cnt["l"] += 1
        S.dma("sp", qs[b][:, :], qd[r][u, :, :], writes=["qs%d" % b])
        S.dma("sp", ks[b][:, :r * (L + 128)], kd[r][u, :, :], writes=["ks%d" % b])
        S.dma("sp", vs[b][:, :r * (L // 128 + 1), :], vd[r][u, :, :, :], writes=["vs%d" % b])
        S.dma("sp", Et[b][:], Ed[u, bi, :, :], writes=["Et%d" % b])
        return b

    def branch(u, bi, b):
        r = DILS[bi]
        L = SEQ // r
        nb_n = L // 128
        for rho in range(r):
            for nb in range(nb_n):
                p = cnt["p"] % 3
                cnt["p"] += 1
                q0 = rho * L + nb * 128
                for c in range(2):
                    k0 = rho * (L + 128) + (nb + c) * 128
                    S.op("pe", (lambda c=c, k0=k0, q0=q0, p=p: nc.tensor.matmul(pss[p][:, c * 128:(c + 1) * 128], ks[b][:, k0:k0 + 128], qs[b][:, q0:q0 + 128],
                                                                                   start=True, stop=True)),
                         reads=["ks%d" % b, "qs%d" % b], writes=["pss%d" % p])
                S.op("act", (lambda p=p: nc.scalar.activation(out=pt[p][:], in_=pss[p][:], func=AF.Exp, scale=0.125)), reads=["pss%d" % p], writes=["pt%d" % p])
                S.op("dve", (lambda p=p: nc.vector.tensor_tensor(out=pt[p][:], in0=pt[p][:], in1=Et[b][:], op=ALU.mult)),
                     reads=["pt%d" % p, "Et%d" % b], writes=["pt%d" % p])
                if nb == 0:
                    S.op("dve", (lambda p=p: nc.vector.memset(pt[p][0:64, 0:128], 0.0)), writes=["pt%d" % p])
                if nb == nb_n - 1:
                    S.op("dve", (lambda p=p: nc.vector.memset(pt[p][64:128, 128:256], 0.0)), writes=["pt%d" % p])
                for c in range(2):
                    vt = rho * (nb_n + 1) + nb + c
                    S.op("pe", (lambda c=c, vt=vt, p=p: nc.tensor.matmul(pso[p][:, 0:128], vs[b][:, vt, :], pt[p][:, c * 128:(c + 1) * 128], start=(c == 0), stop=(c == 1))),
                         reads=["vs%d" % b, "pt%d" % p], writes=[("pso%d" % p, 0)])
                for c in range(2):
                    S.op("pe", (lambda c=c, p=p: nc.tensor.matmul(pso[p][:, 128:256], ones[:], pt[p][:, c * 128:(c + 1) * 128], start=(c == 0), stop=(c == 1))),
                         reads=["ones", "pt%d" % p], writes=[("pso%d" % p, 1)])
                t0 = rho + r * 128 * nb
                sl = slice(t0, t0 + 127 * r + 1, r) if r > 1 else slice(t0, t0 + 128)
                pv = pso[p][:, :].rearrange("p (a t) -> p a t", a=2)
                if bi == 0:
                    S.op("act", (lambda pv=pv, sl=sl: nc.scalar.copy(out=acc[:, :, sl], in_=pv)), reads=[("pso%d" % p, 0), ("pso%d" % p, 1)], writes=["acc"])
                else:
                    S.op("dve", (lambda pv=pv, sl=sl: nc.vector.tensor_tensor(out=acc[:, :, sl], in0=acc[:, :, sl], in1=pv, op=ALU.add)),
                         reads=[("pso%d" % p, 0), ("pso%d" % p, 1), "acc"], writes=["acc"])

    def finish_unit(u):
        for g in range(SEQ // 2048):
            o = cnt["o"] % 2
            cnt["o"] += 1
            sl = slice(g * 2048, (g + 1) * 2048)
            S.op("dve", (lambda sl=sl: nc.vector.reciprocal(out=acc[:, 1, sl], in_=acc[:, 1, sl])), reads=["acc"], writes=["acc"])
            S.op("dve", (lambda o=o, sl=sl: nc.vector.tensor_tensor(out=ost[o][:], in0=acc[:, 0, sl], in1=acc[:, 1, sl], op=ALU.mult)),
                 reads=["acc"], writes=["ost%d" % o])
            S.dma("pool", o_at[u, :, sl], ost[o][:], reads=["ost%d" % o])

    seqs = [(u, bi) for u in range(NU) for bi in range(3)]
    bnext = load(*seqs[0])
    for idx, (u, bi) in enumerate(seqs):
        bcur = bnext
        if idx + 1 < len(seqs):
            bnext = load(*seqs[idx + 1])
        branch(u, bi, bcur)
        if bi == 2:
            finish_unit(u)
    return C.finish()
RTILES = SEQ // 128
NCH = SEQ // 64


def rec_consts():
    s = np.arange(128)[:, None]
    t = np.arange(128)[None, :]
    same = (s // 64) == (t // 64)
    tri = same & (s <= t)
    refm = same & ((s % 64) <= 31)
    dm = tri.astype(np.float32) - refm.astype(np.float32)
    mask = tri.astype(np.uint32)
    ind = np.zeros((128, 6), np.float32)
    sl = np.arange(128)
    for ch in range(2):
        inch = (sl // 64) == ch
        ind[:, 3 * ch + 0] = inch & ((sl % 64) > 31)
        ind[:, 3 * ch + 1] = inch
        ind[:, 3 * ch + 2] = inch & ((sl % 64) <= 31)
    return dm, mask, ind


def build_Br(NU=NU, SEQ=SEQ, dbg=9):
    RTILES = SEQ // 128
    NCH = SEQ // 64
    C = Ctx()
    nc, S = C.nc, C.S
    qd = C.din("rq", [NU, SEQ, 128], F32)
    kd = C.din("rk", [NU, SEQ, 128], F32)
    ld = C.din("rl", [NU, SEQ, 128], F32)
    vd = C.din("rv", [NU, SEQ, 128], BF16)
    dmd = C.din("dm", [128, 128], F32)
    mkd = C.din("mask", [128, 128], mybir.dt.uint32)
    idd = C.din("ind", [128, 6], F32)
    ied = C.din("identb", [128, 128], BF16)
    o_r = C.dout("o_r", [NU, 128, SEQ], F32)

    dm = C.sb("dm_s", [128, 128], F32)
    S.dma("sp", dm[:], dmd, writes=["dm"])
    mk = C.sb("mk_s", [128, 128], mybir.dt.uint32)
    S.dma("sp", mk[:], mkd, writes=["mk"])
    ind = C.sb("ind_s", [128, 6], F32)
    S.dma("sp", ind[:], idd, writes=["ind"])
    idb = C.sb("idb", [128, 128], BF16)
    S.dma("sp", idb[:], ied, writes=["idb"])

    NB = 3
    qt = [C.sb("qt%d" % i, [128, 128], F32) for i in range(NB)]
    kt = [C.sb("kt%d" % i, [128, 128], F32) for i in range(NB)]
    lt = [C.sb("lt%d" % i, [128, 128], F32) for i in range(NB)]
    vt = [C.sb("vt%d" % i, [128, 128], BF16) for i in range(NB)]
    e1 = [C.sb("e1%d" % i, [128, 128], F32) for i in range(2)]
    e2 = [C.sb("e2%d" % i, [128, 128], F32) for i in range(2)]
    qtl = [C.sb("qtl%d" % i, [128, 128], BF16) for i in range(2)]
    ktl = [C.sb("ktl%d" % i, [128, 128], BF16) for i in range(2)]
    kfm = [C.sb("kfm%d" % i, [128, 128], BF16) for i in range(2)]
    qfm = C.sb("qfm", [128, SEQ], BF16)
    EX = C.sb("EX", [128, RTILES, 6], F32)
    AT = C.sb("AT", [128, RTILES, 2, 128], BF16)
    U = C.sb("U", [128, 64, NCH], F32)
    Sc = C.sb("Sc", [128, 64, NCH], F32)
    Sp = C.sb("Sp", [128, NCH, 64], BF16)
    Gc = C.sb("Gc", [128, NCH], F32)
    ER = C.sb("ER", [128, NCH], F32)
    ot = [C.sb("ot%d" % i, [64, 2, 128], F32) for i in range(2)]
    pdb = C.ps("pdb", [128, 128], F32)
    pd = [pdb[:, :], pdb[:, :]]
    pstb = C.ps("pstb", [128, 8], F32)
    pst = pstb[:, :]
    ptq = C.ps("ptq", [128, 128], BF16)
    ptk = C.ps("ptk", [128, 128], BF16)
    pab = C.ps("pab", [128, 2, 128], F32)
    pa = [pab[:, :, :], pab[:, :, :]]
    pub = C.ps("pub", [128, 2, 128], F32)
    pu = [pub[:, :, :], pub[:, :, :]]
    pob = C.ps("pob", [64, 2, 128], F32)
    po = [pob[:, :, :], pob[:, :, :]]

    S.op("dve", lambda: nc.vector.memset(AT[:], 0.0), writes=["ATz"] + [("AT", m_) for m_ in range(RTILES)])

    def load1(u, m):
        b = m % NB
        r = slice(m * 128, (m + 1) * 128)
        S.dma("sp", qt[b][:], qd[u, r, :], writes=["qt%d" % b])
        S.dma("sp", kt[b][:], kd[u, r, :], writes=["kt%d" % b])
        S.dma("sp", lt[b][:], ld[u, r, :], writes=["lt%d" % b])
        S.dma("sp", vt[b][:], vd[u, r, :], writes=["vt%d" % b])

    def pass1(u, m):
        b = m % NB
        a = m % 2
        S.op("pe", lambda: nc.tensor.matmul(pd[a], dm[:], lt[b][:], start=True, stop=True), reads=["dm", "lt%d" % b], writes=["pd"])
        S.op("act", lambda: nc.scalar.activation(out=e1[a][:], in_=pd[a], func=AF.Exp), reads=["pd"], writes=["e1%d" % a])
        S.op("act", lambda: nc.scalar.activation(out=e2[a][:], in_=pd[a], func=AF.Exp, scale=-1.0), reads=["pd"], writes=["e2%d" % a])
        S.op("dve", lambda: nc.vector.tensor_tensor(out=qtl[a][:], in0=qt[b][:], in1=e1[a][:], op=ALU.mult), reads=["qt%d" % b, "e1%d" % a], writes=["qtl%d" % a])
        S.op("dve", lambda: nc.vector.tensor_tensor(out=ktl[a][:], in0=kt[b][:], in1=e2[a][:], op=ALU.mult), reads=["kt%d" % b, "e2%d" % a], writes=["ktl%d" % a])
        if dbg < 2:
            return
        S.op("pe", lambda: nc.tensor.transpose(ptq[:, :], qtl[a][:], idb[:]), reads=["qtl%d" % a, "idb"], writes=["ptq"])
        S.op("pe", lambda: nc.tensor.transpose(ptk[:, :], ktl[a][:], idb[:]), reads=["ktl%d" % a, "idb"], writes=["ptk"])
        S.op("act", lambda: nc.scalar.copy(out=qfm[:, m * 128:(m + 1) * 128], in_=ptq[:, :]), reads=["ptq"], writes=[("qfm", m)])
        S.op("dve", lambda: nc.vector.tensor_copy(out=kfm[a][:], in_=ptk[:, :]), reads=["ptk"], writes=["kfm%d" % a])
        if dbg < 3:
            return
        S.op("pe", lambda: nc.tensor.matmul(pst[:, 0:6], lt[b][:], ind[:], start=True, stop=True), reads=["lt%d" % b, "ind"], writes=["pst"])
        S.op("act", lambda: nc.scalar.activation(out=EX[:, m, :], in_=pst[:, 0:6], func=AF.Exp), reads=["pst"], writes=[("EX", m)])
        if dbg < 4:
            return
        for hh in range(2):
            hs_ = slice(64 * hh, 64 * hh + 64)
            S.op("pe", (lambda hh=hh, hs_=hs_: nc.tensor.matmul(pa[a][:, hh, :], kfm[a][hs_, :], qfm[hs_, m * 128:(m + 1) * 128], start=True, stop=True)),
                 reads=["kfm%d" % a, ("qfm", m)], writes=["pa"], fence=True)
        for hh in range(2):
            S.op("dve", (lambda hh=hh: nc.vector.copy_predicated(out=AT[:, m, hh, :], mask=mk[:], data=pa[a][:, hh, :])),
                 reads=["pa", "mk", "ATz"], writes=[("AT", m)])
        if dbg < 5:
            return
        for ch in range(2):
            cs_ = slice(64 * ch, 64 * ch + 64)
            S.op("pe", (lambda ch=ch, cs_=cs_: nc.tensor.matmul(pu[a][:, ch, :], ktl[a][cs_, :], vt[b][cs_, :], start=True, stop=True)),
                 reads=["ktl%d" % a, "vt%d" % b], writes=["pu"], fence=True)
            n = 2 * m + ch
            for hh in range(2):
                hs_ = slice(64 * hh, 64 * hh + 64)
                eng = "dve"
                if eng == "dve":
                    S.op("dve", (lambda ch=ch, hs_=hs_, n=n: nc.vector.tensor_scalar(out=U[hs_, :, n], in0=pu[a][hs_, ch, hs_], scalar1=EX[hs_, m, 3 * ch:3 * ch + 1],
                                                                                    scalar2=None, op0=ALU.mult)),
                         reads=["pu", ("EX", m)], writes=[("U", n, hh)])
                else:
                    S.op("act", (lambda ch=ch, hs_=hs_, n=n: nc.scalar.activation(out=U[hs_, :, n], in_=pu[a][hs_, ch, hs_], func=AF.Copy, scale=EX[hs_, m, 3 * ch:3 * ch + 1])),
                         reads=["pu", ("EX", m)], writes=[("U", n, hh)])

    def scan(u):
        allU = [("U", n, hh) for n in range(NCH) for hh in range(2)]
        allEX = [("EX", m) for m in range(RTILES)]
        exv = EX[:, :, :].rearrange("p m (c k) -> p (m c) k", c=2)
        S.op("dve", lambda: nc.vector.tensor_copy(out=Gc[:], in_=exv[:, :, 1]), reads=allEX, writes=["Gc"])
        S.op("dve", lambda: nc.vector.tensor_copy(out=ER[:], in_=exv[:, :, 2]), reads=allEX, writes=["ER"])
        for dv in range(64):
            S.op("dve", (lambda dv=dv: nc.vector.tensor_tensor_scan(out=Sc[:, dv, :], data0=Gc[:], data1=U[:, dv, :], initial=0.0, op0=ALU.mult, op1=ALU.add)),
                 reads=allU + ["Gc"], writes=[("Sc", dv)])
        S.op("dve", lambda: nc.vector.memset(Sp[:, 0, :], 0.0), writes=[("Sp", -1)])
        for dv in range(64):
            S.op("dve", (lambda dv=dv: nc.vector.tensor_tensor(out=Sp[:, 1:NCH, dv], in0=Sc[:, dv, 0:NCH - 1], in1=ER[:, 1:NCH], op=ALU.mult)),
                 reads=[("Sc", dv), "ER"], writes=[("Sp", dv)])

    def load2(u, m):
        b = m % NB
        S.dma("sp", vt[b][:], vd[u, m * 128:(m + 1) * 128, :], writes=["vt%d" % b])

    def pass2(u, m):
        b = m % NB
        a = m % 2
        allSp = [("Sp", dv) for dv in range(-1, 64)]
        for hh in range(2):
            hs_ = slice(64 * hh, 64 * hh + 64)
            for ch in range(2):
                cs_ = slice(64 * ch, 64 * ch + 64)
                n = 2 * m + ch
                S.op("pe", (lambda hh=hh, hs_=hs_, cs_=cs_: nc.tensor.matmul(po[a][:, hh, cs_], vt[b][cs_, hs_], AT[cs_, m, hh, cs_], start=True, stop=False)),
                     reads=["vt%d" % b, ("AT", m)], writes=["po"], fence=True)
                S.op("pe", (lambda hh=hh, hs_=hs_, cs_=cs_, n=n, ch=ch: nc.tensor.matmul(po[a][:, hh, cs_], Sp[hs_, n, :], qfm[hs_, m * 128 + 64 * ch: m * 128 + 64 * ch + 64],
                                                                                 start=False, stop=True)),
                     reads=allSp + [("qfm", m)], writes=["po"], fence=True)
        S.op("act", lambda: nc.scalar.copy(out=ot[a][:], in_=po[a]), reads=["po"], writes=["ot%d" % a])
        S.dma("pool", o_r[u, :, m * 128:(m + 1) * 128].rearrange("(h d) t -> d h t", h=2), ot[a][:], reads=["ot%d" % a])

    for u in range(NU):
        load1(u, 0)
        load1(u, 1)
        for m in range(RTILES):
            if m + 2 < RTILES:
                load1(u, m + 2)
            pass1(u, m)
        if dbg < 6:
            continue
        scan(u)
        if dbg < 7:
            continue
        load2(u, 0)
        load2(u, 1)
        for m in range(RTILES):
            if m + 2 < RTILES:
                load2(u, m + 2)
            pass2(u, m)
    return C.finish()
def att_tables():
    slopes = 2.0 ** (-8.0 * np.arange(1, 7) / 6)
    p = np.arange(128)[:, None, None]
    c = np.arange(2)[None, :, None]
    i = np.arange(128)[None, None, :]
    rel = 128 * c + p - 64 - i
    out = np.zeros((6, 3, 128, 256), np.float32)
    for h in range(6):
        for bi, r in enumerate((1, 4, 16)):
            e = np.where(np.abs(rel) <= 64, np.exp(-slopes[h] * r * np.abs(rel)), 0.0)
            out[h, bi] = e.reshape(128, 256)
    return out.astype(NPBF)


def att_layout(q_fm, k_fm, v_tm, r):
    T = q_fm.shape[1]
    L = T // r
    nb = L // 128
    qr = q_fm.reshape(64, L, r).transpose(0, 2, 1).reshape(64, r * L)
    kr = np.zeros((64, r, L + 128), k_fm.dtype)
    kr[:, :, 64:64 + L] = k_fm.reshape(64, L, r).transpose(0, 2, 1)
    vr = np.zeros((r, L + 128, 64), v_tm.dtype)
    vr[:, 64:64 + L] = v_tm.reshape(L, r, 64).transpose(1, 0, 2)
    vr = vr.reshape(r, nb + 1, 128, 64).transpose(2, 0, 1, 3).reshape(128, r * (nb + 1), 64)
    return np.ascontiguousarray(qr), np.ascontiguousarray(kr.reshape(64, r * (L + 128))), np.ascontiguousarray(vr)
_PROG_CACHE = {}


def _prog(name, fn, *a):
    return fn(*a)[0]


def _t128(v):
    return np.ascontiguousarray(np.asarray(v, np.float32).reshape(-1, 128).T)


def _run(nc, maps):
    res = run_bass_kernel_spmd(nc, maps, core_ids=list(range(8)))
    return res.results


def kernel(x, c, w_ada, b_ada, norm1_w, w_in, conv_a_w, conv_a_b, ln_a_w, ln_a_b,
           lb_gamma, rec_norm_w, w_out, norm2_w, w_up, conv_f_w, w_down, final_norm_w):
    f32 = np.float32
    x = np.asarray(x, f32)
    B, T, _ = x.shape
    HALF = T // 2
    cores = [(b, h) for b in range(B) for h in range(2)]
    xT = [np.ascontiguousarray(x[b].T) for b in range(B)]
    lbg_rep = np.ascontiguousarray(np.broadcast_to(np.asarray(lb_gamma, f32).reshape(1, -1), (128, 1536)))
    E_all = att_tables()
    dmc, maskc, indc = rec_consts()
    identf = np.eye(128, dtype=f32)
    identb = identf.astype(NPBF)
    bdm = np.kron(np.eye(2), np.ones((64, 64))).astype(f32)
    depth = w_in.shape[0]
    for l in range(depth):
        final = (l == depth - 1)
        maps = []
        for (b, h) in cores:
            maps.append({"xT": np.ascontiguousarray(xT[b][:, h * HALF:(h + 1) * HALF]), "ct": _t128(c[b]),
                         "wada": np.asarray(w_ada[l], f32), "bada": _t128(b_ada[l]), "n1w": _t128(norm1_w[l]),
                         "win": np.asarray(w_in[l], f32), "lbg": lbg_rep})
        ra = _run(_prog("A%d" % l, build_A, l), maps)

        def catT(name, b):
            return np.concatenate([ra[2 * b][name], ra[2 * b + 1][name]], axis=1)

        def catR(name, b):
            return np.concatenate([ra[2 * b][name], ra[2 * b + 1][name]], axis=0)
        aT = [catT("o_aT", b) for b in range(B)]
        qkT = [catT("o_qkT", b) for b in range(B)]
        gT = [catT("o_gT", b) for b in range(B)]
        vat = [catR("o_v", b) for b in range(B)]
        qr = [catR("o_qr", b) for b in range(B)]
        lf = [catR("o_lf", b) for b in range(B)]
        kk = [catR("o_kk", b) for b in range(B)]
        ir = [catR("o_ir", b) for b in range(B)]
        del ra
        maps = []
        cw = np.ascontiguousarray(np.asarray(conv_a_w[l], f32).T.reshape(2, 128, CK).transpose(1, 0, 2))

        def t2(v):
            return np.ascontiguousarray(np.asarray(v, f32).reshape(2, 128).T)
        for (b, h) in cores:
            pad = np.zeros((256, T + 2 * CH), aT[b].dtype)
            pad[:, CH:CH + T] = aT[b]
            maps.append({"aT": np.ascontiguousarray(pad[:, h * HALF:h * HALF + HALF + 2 * CH]), "cw_d": cw, "cb_d": t2(conv_a_b[l]),
                         "lw_d": t2(ln_a_w[l]), "lb_d": t2(ln_a_b[l]), "ident_d": identf})
        rc = _run(_prog("Bc", build_Bc), maps)
        acT = [np.concatenate([rc[2 * b]["o_ac"], rc[2 * b + 1]["o_ac"]], axis=1) for b in range(B)]
        del rc
        units = [(b, h) for b in range(B) for h in range(6)]
        maps = []
        for ci in range(8):
            us = units[3 * ci:3 * ci + 3]
            m = {"E": np.ascontiguousarray(np.stack([E_all[h] for (_, h) in us]))}
            for r in DILS:
                ql, kl, vl = [], [], []
                for (b, h) in us:
                    q_fm = qkT[b][h * 64:(h + 1) * 64, :]
                    k_fm = qkT[b][384 + h * 64:384 + (h + 1) * 64, :]
                    v_tm = vat[b][:, h * 64:(h + 1) * 64]
                    a_, b_, c_ = att_layout(np.ascontiguousarray(q_fm), np.ascontiguousarray(k_fm), np.ascontiguousarray(v_tm), r)
                    ql.append(a_)
                    kl.append(b_)
                    vl.append(c_)
                m["q%d" % r] = np.stack(ql)
                m["k%d" % r] = np.stack(kl)
                m["v%d" % r] = np.stack(vl)
            maps.append(m)
        rb = _run(_prog("Ba", build_Ba), maps)
        atT = [np.zeros((384, T), NPBF) for _ in range(B)]
        for ui, (b, h) in enumerate(units):
            atT[b][h * 64:(h + 1) * 64, :] = rb[ui // 3]["o_at"][ui % 3]
        del rb
        runits = [(b, d, hp) for b in range(B) for d in range(2) for hp in range(3)]
        maps = []
        for ci in range(8):
            us = runits[3 * ci:3 * ci + 3]
            ql, kl, ll, vl = [], [], [], []
            for (b, d, hp) in us:
                cs = slice(hp * 128, (hp + 1) * 128)
                cs2 = slice(d * 384 + hp * 128, d * 384 + (hp + 1) * 128)
                q_, k_, l_, v_ = qr[b][:, cs], kk[b][:, cs2], lf[b][:, cs2], ir[b][:, cs]
                if d == 1:
                    q_, k_, l_, v_ = q_[::-1], k_[::-1], l_[::-1], v_[::-1]
                ql.append(np.ascontiguousarray(q_))
                kl.append(np.ascontiguousarray(k_))
                ll.append(np.ascontiguousarray(l_))
                vl.append(np.ascontiguousarray(v_))
            maps.append({"rq": np.stack(ql), "rk": np.stack(kl), "rl": np.stack(ll), "rv": np.stack(vl),
                         "dm": dmc, "mask": maskc, "ind": indc, "identb": identb})
        rr = _run(_prog("Br", build_Br), maps)
        ofT = [np.zeros((384, T), f32) for _ in range(B)]
        obT = [np.zeros((384, T), f32) for _ in range(B)]
        for ui, (b, d, hp) in enumerate(runits):
            o = rr[ui // 3]["o_r"][ui % 3]
            if d == 0:
                ofT[b][hp * 128:(hp + 1) * 128, :] = o
            else:
                obT[b][hp * 128:(hp + 1) * 128, :] = o[:, ::-1]
        del rr
        cfw = np.ascontiguousarray(np.asarray(conv_f_w[l], f32).T.reshape(44, 128, 3).transpose(1, 0, 2))

        def padcols(a, h):
            p = np.zeros((a.shape[0], T + 2), a.dtype)
            p[:, 1:T + 1] = a
            return np.ascontiguousarray(p[:, h * HALF:h * HALF + HALF + 2])
        maps = []
        for (b, h) in cores:
            flags = np.ones((128, 2), f32)
            if h == 0:
                flags[:, 0] = 0.0
            if h == 1:
                flags[:, 1] = 0.0
            maps.append({"xT": padcols(xT[b], h), "acT": padcols(acT[b], h), "atT": padcols(atT[b], h), "ofT": padcols(ofT[b], h),
                         "obT": padcols(obT[b], h), "gT": padcols(gT[b], h), "rnw": _t128(rec_norm_w[l]), "ct": _t128(c[b]),
                         "wada": np.asarray(w_ada[l], f32), "bada": _t128(b_ada[l]), "n2w": _t128(norm2_w[l]), "fnw": _t128(final_norm_w),
                         "wout": np.asarray(w_out[l], f32), "wup": np.asarray(w_up[l], f32), "wdown": np.asarray(w_down[l], f32),
                         "cfw": cfw, "flags": flags, "bdm": bdm})
        rd = _run(_prog("CD%d" % final, build_CD, final), maps)
        xT = [np.concatenate([rd[2 * b]["o_xT"], rd[2 * b + 1]["o_xT"]], axis=1) for b in range(B)]
        del rd
    out = np.stack([np.ascontiguousarray(xT[b].T) for b in range(B)]).astype(f32)
    return out
```

```python
import numpy as np
import concourse.bass as bass
import concourse.mybir as mybir

from contextlib import ExitStack
import ml_dtypes


from concourse.bass_utils import run_bass_kernel_spmd

EPS = 1e-6
F_TINY = 1e-30
NPBF = ml_dtypes.bfloat16


F32 = mybir.dt.float32
BF16 = mybir.dt.bfloat16
AF = mybir.ActivationFunctionType
ALU = mybir.AluOpType

N_DMA_SEM = 8


class Sched:
    def __init__(self, nc, same_engine_sync=True):
        self.nc = nc
        self.ops = []
        self.same_engine_sync = same_engine_sync
        self.eng = {
            "pe": nc.tensor, "dve": nc.vector, "act": nc.scalar,
            "pool": nc.gpsimd, "sp": nc.sync,
        }

    def op(self, eng, fn, reads=(), writes=(), kind="c", fence=False):
        self.ops.append(dict(eng=eng, fn=fn, reads=tuple(reads), writes=tuple(writes), kind=kind, fence=fence))

    def dma(self, eng, out, in_, reads=(), writes=(), **kw):
        e = self.eng[eng]
        self.op(eng, lambda: e.dma_start(out=out, in_=in_, **kw), reads, writes, kind="d")

    def emit(self, stack):
        nc = self.nc
        ops = self.ops
        n = len(ops)
        last_w = {}
        readers = {}
        deps = [set() for _ in range(n)]
        for i, o in enumerate(ops):
            for k in o["reads"]:
                if k in last_w:
                    deps[i].add(last_w[k])
            for k in o["writes"]:
                if k in last_w:
                    deps[i].add(last_w[k])
                for r in readers.get(k, ()):
                    if r != i:
                        deps[i].add(r)
            for k in o["reads"]:
                readers.setdefault(k, []).append(i)
            for k in o["writes"]:
                last_w[k] = i
                readers[k] = []
        need = [[] for _ in range(n)]
        signal = [False] * n
        last_on = {}
        for i, o in enumerate(ops):
            if o.get("fence") and o["eng"] in last_on:
                p = last_on[o["eng"]]
                need[i].append(p)
                signal[p] = True
            if o["kind"] == "c":
                last_on[o["eng"]] = i
        for i, o in enumerate(ops):
            for p in deps[i]:
                po = ops[p]
                if po["kind"] == "d":
                    need[i].append(p)
                    continue
                if po["eng"] == o["eng"]:
                    if o["kind"] == "c" and (o["eng"] == "pe" or not self.same_engine_sync):
                        continue
                need[i].append(p)
                signal[p] = True
        comp_sem = {}
        for e in ("pe", "dve", "act", "pool"):
            comp_sem[e] = stack.enter_context(nc.semaphore("s_" + e))
        dma_sems = {}
        for e in ("sp", "pool", "act"):
            dma_sems[e] = [stack.enter_context(nc.semaphore("d_%s_%d" % (e, j))) for j in range(N_DMA_SEM)]
        cnt = {e: 0 for e in comp_sem}
        dcnt = {e: 0 for e in dma_sems}
        semval = [None] * n
        waited = {}
        sem_objs = {}

        def do_wait(engname, sem, val):
            key = (engname, id(sem))
            if waited.get(key, 0) >= val:
                return
            waited[key] = val
            self.eng[engname].wait_ge(sem, val)

        for i, o in enumerate(ops):
            e = o["eng"]
            wl = {}
            for p in need[i]:
                s, v = semval[p]
                if id(s) not in wl or wl[id(s)][1] < v:
                    wl[id(s)] = (s, v)
            if o["kind"] == "d":
                j = dcnt[e]
                sem = dma_sems[e][j % N_DMA_SEM]
                if j >= N_DMA_SEM:
                    prev = 16 * (j // N_DMA_SEM)
                    if id(sem) not in wl or wl[id(sem)][1] < prev:
                        wl[id(sem)] = (sem, prev)
            for s, v in wl.values():
                do_wait(e, s, v)
            ins = o["fn"]()
            if o["kind"] == "d":
                dcnt[e] = j + 1
                v = 16 * (j // N_DMA_SEM + 1)
                ins.then_inc(sem, 16)
                semval[i] = (sem, v)
            elif signal[i]:
                cnt[e] += 1
                ins.then_inc(comp_sem[e], 1)
                semval[i] = (comp_sem[e], cnt[e])
        for e, sems in dma_sems.items():
            for j, s in enumerate(sems):
                tot = dcnt[e]
                k = (tot - j + N_DMA_SEM - 1) // N_DMA_SEM if tot > j else 0
                if k > 0:
                    nc.sync.wait_ge(s, 16 * k)
        return dict(n_ops=n, cnt=cnt, dcnt=dcnt)
TOK = 4096
TT = 512
NT = TOK // TT
D = 1024
KC = D // 128
INC = 3584


class Ctx:
    def __init__(self, same_engine_sync=True):
        self.nc = bass.Bass("TRN2", target_bir_lowering=False)
        self.st = ExitStack()
        self.S = Sched(self.nc, same_engine_sync=same_engine_sync)
        self.nps = 0

    def sb(self, name, shape, dt):
        return self.st.enter_context(self.nc.sbuf_tensor(name, list(shape), dt))

    def ps(self, name, shape=(128, 512), dt=F32):
        return self.st.enter_context(self.nc.psum_tensor(name, list(shape), dt))

    def din(self, name, shape, dt):
        return self.nc.dram_tensor(name, list(shape), dt, kind="ExternalInput").ap()

    def dout(self, name, shape, dt):
        return self.nc.dram_tensor(name, list(shape), dt, kind="ExternalOutput").ap()

    def finish(self):
        info = self.S.emit(self.st)
        self.st.close()
        return self.nc, info


def emit_mod(C, wada, bada_t, ct, col0, nchunk, modT, tagp, wtiles):
    nc, S = C.nc, C.S
    csb = C.sb(tagp + "c", [128, KC], F32)
    csl = C.sb(tagp + "cs", [128, KC], F32)
    bsb = C.sb(tagp + "b", [128, 48], F32)
    S.dma("sp", csb[:], ct, writes=[tagp + "c"])
    S.dma("sp", bsb[:], bada_t, writes=[tagp + "b"])
    S.op("act", lambda: nc.scalar.activation(out=csl[:], in_=csb[:], func=AF.Silu), reads=[tagp + "c"], writes=[tagp + "cs"])
    WB = 256
    pm = C.ps(tagp + "pm", [128, 64], F32)
    wv = wada.rearrange("(k p) n -> p k n", p=128)
    ngrp = (nchunk * 128) // WB
    for g in range(ngrp):
        tt_, key = wtiles[g % len(wtiles)]
        t = tt_[:, :, 0:WB]
        S.dma("sp", t, wv[:, :, col0 + g * WB: col0 + (g + 1) * WB], writes=[key])
        for jj in range(WB // 128):
            j = g * (WB // 128) + jj
            for k in range(KC):
                S.op("pe", (lambda t=t, jj=jj, j=j, k=k: nc.tensor.matmul(
                    pm[:, j:j + 1], t[:, k, jj * 128:(jj + 1) * 128], csl[:, k:k + 1],
                    start=(k == 0), stop=(k == KC - 1))),
                    reads=[key, tagp + "cs"], writes=[(tagp + "pm", j)])
    jb = col0 // 128
    S.op("dve", lambda: nc.vector.tensor_tensor(out=modT[:, 0:nchunk], in0=pm[:, 0:nchunk], in1=bsb[:, jb:jb + nchunk], op=ALU.add),
         reads=[(tagp + "pm", j) for j in range(nchunk)] + [tagp + "b"], writes=[tagp + "mod"])


def emit_rstd(C, S, nc, ps_stat, rstd, key_ps, key_rstd, ncols, dim, tmp):
    S.op("act", lambda: nc.scalar.activation(out=tmp[:, :ncols], in_=ps_stat[:, :ncols], func=AF.Sqrt, scale=1.0 / dim, bias=C.eps_t[:, 0:1]),
         reads=[key_ps], writes=[key_rstd + "_t"])
    S.op("dve", lambda: nc.vector.reciprocal(out=rstd[:, :ncols], in_=tmp[:, :ncols]), reads=[key_rstd + "_t"], writes=[key_rstd])


def build_A(layer):
    C = Ctx()
    nc, S = C.nc, C.S
    xT = C.din("xT", [D, TOK], F32)
    ct = C.din("ct", [128, KC], F32)
    wada = C.din("wada", [D, 6144], F32)
    bada = C.din("bada", [128, 48], F32)
    n1w = C.din("n1w", [128, KC], F32)
    win = C.din("win", [D, INC], F32)
    lbg = C.din("lbg", [128, 2 * 768], F32)
    o_aT = C.dout("o_aT", [256, TOK], BF16)
    o_qkT = C.dout("o_qkT", [768, TOK], BF16)
    o_gT = C.dout("o_gT", [384, TOK], F32)
    o_v = C.dout("o_v", [TOK, 384], BF16)
    o_qr = C.dout("o_qr", [TOK, 384], F32)
    o_lf = C.dout("o_lf", [TOK, 768], F32)
    o_kk = C.dout("o_kk", [TOK, 768], F32)
    o_ir = C.dout("o_ir", [TOK, 384], BF16)

    C.eps_t = C.sb("eps", [128, 1], F32)
    S.op("dve", lambda: nc.vector.memset(C.eps_t[:], EPS), writes=["eps"])
    ones = C.sb("ones", [128, 128], BF16)
    S.op("dve", lambda: nc.vector.memset(ones[:], 1.0), writes=["ones"])

    wsb = C.sb("wsb", [128, KC, INC], BF16)
    wv = win.rearrange("(k p) n -> p k n", p=128)
    for k in range(KC):
        S.dma("pool", wsb[:, k, :], wv[:, k, :], writes=[("wsb", k)])
    WKEYS = [("wsb", k) for k in range(KC)]

    modT = C.sb("modT", [128, 16], F32)
    xt = [C.sb("xt%d" % i, [128, KC, TT], F32) for i in range(2)]
    emit_mod(C, wada, bada, ct, 0, 16, modT, "m_", [(xt[0], "xt0"), (xt[1], "xt1")])
    nw = C.sb("nw", [128, KC], F32)
    S.dma("sp", nw[:], n1w, writes=["nw"])
    scl = C.sb("scl", [128, KC], F32)
    S.op("dve", lambda: nc.vector.scalar_tensor_tensor(out=scl[:], in0=modT[:, 8:16], scalar=1.0, in1=nw[:], op0=ALU.add, op1=ALU.mult),
         reads=["m_mod", "nw"], writes=["scl"])

    lbt = C.sb("lbt", [128, 768], F32)
    oml = C.sb("oml", [128, 768], F32)
    if layer == 0:
        S.op("dve", lambda: nc.vector.memset(lbt[:], 0.0), writes=["lbt"])
        S.op("dve", lambda: nc.vector.memset(oml[:], 1.0), writes=["oml"])
    else:
        lg = C.sb("lg", [128, 2 * 768], F32)
        S.dma("sp", lg[:], lbg, writes=["lg"])
        ex = C.sb("lbex", [128, 2 * 768], F32)
        S.op("act", lambda: nc.scalar.activation(out=ex[:], in_=lg[:], func=AF.Exp), reads=["lg"], writes=["lbex"])
        sm = C.sb("lbsm", [128, 768], F32)
        S.op("dve", lambda: nc.vector.tensor_tensor(out=sm[:], in0=ex[:, 0:768], in1=ex[:, 768:1536], op=ALU.add), reads=["lbex"], writes=["lbsm"])
        S.op("dve", lambda: nc.vector.reciprocal(out=sm[:], in_=sm[:]), reads=["lbsm"], writes=["lbsm"])
        S.op("dve", lambda: nc.vector.tensor_tensor(out=lbt[:], in0=ex[:, 768:1536], in1=sm[:], op=ALU.mult), reads=["lbex", "lbsm"], writes=["lbt"])
        S.op("dve", lambda: nc.vector.tensor_scalar(out=oml[:], in0=lbt[:], scalar1=-1.0, scalar2=1.0, op0=ALU.mult, op1=ALU.add), reads=["lbt"], writes=["oml"])

    hs = [C.sb("hs%d" % i, [128, KC, TT], BF16) for i in range(2)]
    sq = C.sb("sq", [128, KC, TT], BF16)
    rstd = C.sb("rstd", [128, TT], F32)
    rtmp = C.sb("rtmp", [128, TT], F32)
    ps_stat = C.ps("ps_stat")
    psf = [C.ps("psf%d" % i) for i in range(3)]
    pst = [C.ps("pst%d" % i) for i in range(3)]
    NST = 3
    stf_b = [C.sb("stfb%d" % i, [128, TT], BF16) for i in range(NST)]
    stf_f = [C.sb("stff%d" % i, [128, TT], F32) for i in range(NST)]
    sg = [C.sb("sg%d" % i, [128, TT], F32) for i in range(2)]
    st_v = [C.sb("stv%d" % i, [128, 384], BF16) for i in range(2)]
    st_qr = [C.sb("stqr%d" % i, [128, 384], F32) for i in range(2)]
    st_s = [C.sb("sts%d" % i, [128, 768], F32) for i in range(2)]
    st_lf = [C.sb("stlf%d" % i, [128, 768], F32) for i in range(2)]
    st_kk = [C.sb("stkk%d" % i, [128, 768], F32) for i in range(2)]
    st_ir = [C.sb("stir%d" % i, [128, 384], BF16) for i in range(2)]
    xv = xT.rearrange("(k p) t -> p k t", p=128)
    cnt = {"f": 0, "t": 0, "st": 0, "tm": 0}

    def prep(i):
        b = i % 2
        x = xt[b]
        S.dma("sp", x[:], xv[:, :, i * TT:(i + 1) * TT], writes=["xt%d" % b])
        S.op("act", lambda: nc.scalar.activation(out=sq[:], in_=x[:], func=AF.Square), reads=["xt%d" % b], writes=["sq"])
        for k in range(KC):
            S.op("pe", (lambda k=k: nc.tensor.matmul(ps_stat[:], ones[:], sq[:, k, :], start=(k == 0), stop=(k == KC - 1))),
                 reads=["sq", "ones"], writes=["ps_stat"])
        emit_rstd(C, S, nc, ps_stat, rstd, "ps_stat", "rstd", TT, D, rtmp)
        for k in range(KC):
            S.op("dve", (lambda k=k: nc.vector.scalar_tensor_tensor(out=x[:, k, :], in0=x[:, k, :], scalar=scl[:, k:k + 1], in1=rstd[:],
                                                                    op0=ALU.mult, op1=ALU.mult)),
                 reads=["xt%d" % b, "scl", "rstd"], writes=["xt%d" % b])
        for k in range(KC):
            S.op("act", (lambda k=k: nc.scalar.activation(out=hs[b][:, k, :], in_=x[:, k, :], func=AF.Identity, bias=modT[:, k:k + 1])),
                 reads=["xt%d" % b, "m_mod"], writes=["hs%d" % b])

    def fm_group(i, colchunk):
        b = i % 2
        p = cnt["f"] % 3
        cnt["f"] += 1
        pt = psf[p]
        for k in range(KC):
            S.op("pe", (lambda k=k: nc.tensor.matmul(pt[:], wsb[:, k, colchunk * 128:(colchunk + 1) * 128], hs[b][:, k, :],
                                                     start=(k == 0), stop=(k == KC - 1))),
                 reads=WKEYS + ["hs%d" % b], writes=["psf%d" % p])
        return pt, "psf%d" % p

    def main(i):
        b = i % 2
        t0 = i * TT
        for c in range(2):
            pg, kg = fm_group(i, 2 + c)
            s = cnt["st"] % 2
            cnt["st"] += 1
            S.op("act", (lambda pg=pg, s=s: nc.scalar.activation(out=sg[s][:], in_=pg[:], func=AF.Sigmoid)), reads=[kg], writes=["sg%d" % s])
            pv, kv = fm_group(i, c)
            q = cnt["tm"] % NST
            cnt["tm"] += 1
            S.op("dve", (lambda pv=pv, s=s, q=q: nc.vector.tensor_tensor(out=stf_b[q][:], in0=pv[:], in1=sg[s][:], op=ALU.mult)),
                 reads=[kv, "sg%d" % s], writes=["stfb%d" % q])
            S.dma("pool", o_aT[c * 128:(c + 1) * 128, t0:t0 + TT], stf_b[q][:], reads=["stfb%d" % q])
        for c in range(6):
            pq, kq = fm_group(i, 4 + c)
            q = cnt["tm"] % NST
            cnt["tm"] += 1
            S.op("act", (lambda pq=pq, q=q: nc.scalar.copy(out=stf_b[q][:], in_=pq[:])), reads=[kq], writes=["stfb%d" % q])
            S.dma("pool", o_qkT[c * 128:(c + 1) * 128, t0:t0 + TT], stf_b[q][:], reads=["stfb%d" % q])
        for c in range(3):
            pq, kq = fm_group(i, 25 + c)
            q = cnt["tm"] % NST
            cnt["tm"] += 1
            S.op("act", (lambda pq=pq, q=q: nc.scalar.activation(out=stf_f[q][:], in_=pq[:], func=AF.Silu)), reads=[kq], writes=["stff%d" % q])
            S.dma("pool", o_gT[c * 128:(c + 1) * 128, t0:t0 + TT], stf_f[q][:], reads=["stff%d" % q])
        def sub(su):
            r0 = t0 + su * 128
            sbi = cnt["t"] % 2
            cnt["t"] += 1

            def tm_group(col0, ncol):
                p = cnt["tm"] % 3
                cnt["tm"] += 1
                pt = pst[p]
                for k in range(KC):
                    S.op("pe", (lambda k=k, pt=pt: nc.tensor.matmul(pt[:, :ncol], hs[b][:, k, su * 128:(su + 1) * 128], wsb[:, k, col0:col0 + ncol],
                                                                    start=(k == 0), stop=(k == KC - 1))),
                         reads=WKEYS + ["hs%d" % b], writes=["pst%d" % p])
                return pt, "pst%d" % p
            pt, kp = tm_group(1280, 384)
            S.op("act", (lambda pt=pt: nc.scalar.copy(out=st_v[sbi][:], in_=pt[:, :384])), reads=[kp], writes=["stv%d" % sbi])
            S.dma("pool", o_v[r0:r0 + 128, :], st_v[sbi][:], reads=["stv%d" % sbi])
            pt, kp = tm_group(1664, 384)
            S.op("act", (lambda pt=pt: nc.scalar.activation(out=st_qr[sbi][:], in_=pt[:, :384], func=AF.Silu)), reads=[kp], writes=["stqr%d" % sbi])
            S.dma("pool", o_qr[r0:r0 + 128, :], st_qr[sbi][:], reads=["stqr%d" % sbi])
            for dd in range(2):
                pt, kp = tm_group(2048 + dd * 384, 384)
                sl = slice(dd * 384, (dd + 1) * 384)
                S.op("act", (lambda pt=pt, sl=sl: nc.scalar.activation(out=st_s[sbi][:, sl], in_=pt[:, :384], func=AF.Sigmoid)),
                     reads=[kp], writes=[("sts%d" % sbi, dd)])
            ks = "sts%d" % sbi
            S.op("dve", lambda: nc.vector.tensor_tensor(out=st_s[sbi][:], in0=st_s[sbi][:], in1=oml[:], op=ALU.mult),
                 reads=[(ks, 0), (ks, 1), "oml"], writes=[(ks, 0), (ks, 1)])
            S.op("dve", lambda: nc.vector.scalar_tensor_tensor(out=st_s[sbi][:], in0=st_s[sbi][:], scalar=F_TINY, in1=lbt[:], op0=ALU.max, op1=ALU.add),
                 reads=[(ks, 0), (ks, 1), "lbt"], writes=[(ks, 0), (ks, 1)])
            S.op("act", lambda: nc.scalar.activation(out=st_lf[sbi][:], in_=st_s[sbi][:], func=AF.Ln), reads=[(ks, 0), (ks, 1)], writes=["stlf%d" % sbi])
            S.op("dve", lambda: nc.vector.tensor_scalar(out=st_kk[sbi][:], in0=st_s[sbi][:], scalar1=-1.0, scalar2=1.0, op0=ALU.mult, op1=ALU.add),
                 reads=[(ks, 0), (ks, 1)], writes=["stkk%d" % sbi])
            S.dma("pool", o_lf[r0:r0 + 128, :], st_lf[sbi][:], reads=["stlf%d" % sbi])
            S.dma("pool", o_kk[r0:r0 + 128, :], st_kk[sbi][:], reads=["stkk%d" % sbi])
            pt, kp = tm_group(2816, 384)
            S.op("act", (lambda pt=pt: nc.scalar.copy(out=st_ir[sbi][:], in_=pt[:, :384])), reads=[kp], writes=["stir%d" % sbi])
            S.dma("pool", o_ir[r0:r0 + 128, :], st_ir[sbi][:], reads=["stir%d" % sbi])

        for su in range(TT // 128):
            sub(su)

    prep(0)
    for i in range(NT):
        if i + 1 < NT:
            prep(i + 1)
        main(i)
    return C.finish()
NPAD = TOK + 2
CT = 256
CSTEP = CT - 2
NCT = (TOK + CSTEP - 1) // CSTEP
DFF = 2816
NJ = DFF // 128


def build_CD(final):
    C = Ctx()
    nc, S = C.nc, C.S
    xT = C.din("xT", [D, NPAD], F32)
    acT = C.din("acT", [256, NPAD], BF16)
    atT = C.din("atT", [384, NPAD], BF16)
    ofT = C.din("ofT", [384, NPAD], F32)
    obT = C.din("obT", [384, NPAD], F32)
    gT = C.din("gT", [384, NPAD], F32)
    rnw = C.din("rnw", [128, 3], F32)
    ct = C.din("ct", [128, KC], F32)
    wada = C.din("wada", [D, 6144], F32)
    bada = C.din("bada", [128, 48], F32)
    n2w = C.din("n2w", [128, KC], F32)
    fnw = C.din("fnw", [128, KC], F32)
    wout = C.din("wout", [D, D], F32)
    wup = C.din("wup", [D, 2 * DFF], F32)
    wdown = C.din("wdown", [DFF, D], F32)
    cfw = C.din("cfw", [128, 44, 3], F32)
    flags = C.din("flags", [128, 2], F32)
    bdm = C.din("bdm", [128, 128], F32)
    o_xT = C.dout("o_xT", [D, TOK], F32)

    C.eps_t = C.sb("eps", [128, 1], F32)
    S.op("dve", lambda: nc.vector.memset(C.eps_t[:], EPS), writes=["eps"])
    ones = C.sb("ones", [128, 128], BF16)
    S.op("dve", lambda: nc.vector.memset(ones[:], 1.0), writes=["ones"])
    bd = C.sb("bd", [128, 128], BF16)
    S.dma("pool", bd[:], bdm, writes=["bd"])

    xt = C.sb("xt", [128, KC, CT], F32)
    modT = C.sb("modT", [128, 32], F32)
    emit_mod(C, wada, bada, ct, 2048, 32, modT, "m_", [(xt, "xt")])
    nw = C.sb("nw", [128, KC], F32)
    S.dma("sp", nw[:], n2w, writes=["nw"])
    scl = C.sb("scl", [128, KC], F32)
    S.op("dve", lambda: nc.vector.scalar_tensor_tensor(out=scl[:], in0=modT[:, 16:24], scalar=1.0, in1=nw[:], op0=ALU.add, op1=ALU.mult),
         reads=["m_mod", "nw"], writes=["scl"])
    fw = C.sb("fw", [128, KC], F32)
    S.dma("sp", fw[:], fnw, writes=["fw"])
    rw = C.sb("rw", [128, 3], F32)
    S.dma("sp", rw[:], rnw, writes=["rw"])
    cw = C.sb("cw", [128, 44, 3], F32)
    S.dma("sp", cw[:], cfw, writes=["cw"])
    fl = C.sb("fl", [128, 2], F32)
    S.dma("sp", fl[:], flags, writes=["fl"])

    wo = C.sb("wo", [128, KC, D], BF16)
    wu = C.sb("wu", [128, KC, 2 * DFF], BF16)
    wd = C.sb("wd", [128, NJ, D], BF16)
    wov = wout.rearrange("(k p) n -> p k n", p=128)
    wuv = wup.rearrange("(k p) n -> p k n", p=128)
    wdv = wdown.rearrange("(k p) n -> p k n", p=128)
    for k in range(KC):
        S.dma("pool", wo[:, k, :], wov[:, k, :], writes=[("wo", k)])
    for k in range(KC):
        S.dma("pool", wu[:, k, :], wuv[:, k, :], writes=[("wu", k)])
    for k in range(NJ):
        S.dma("pool", wd[:, k, :], wdv[:, k, :], writes=[("wd", k)])
    WO = [("wo", k) for k in range(KC)]
    WU = [("wu", k) for k in range(KC)]
    WD = [("wd", k) for k in range(NJ)]

    mx = C.sb("mx", [128, KC, CT], BF16)
    oft = C.sb("oft", [128, 3, CT], F32)
    obt = C.sb("obt", [128, 3, CT], F32)
    gt = C.sb("gt", [128, 3, CT], F32)
    h2 = C.sb("h2", [128, KC, CT], BF16)
    sq = C.sb("sq", [128, KC, CT], BF16)
    gv = C.sb("gv", [128, NJ, CT], BF16)
    rstd = C.sb("rstd", [128, CT], F32)
    rtmp = C.sb("rtmp", [128, CT], F32)
    tmpf = [C.sb("tmpf%d" % i, [128, CT], F32) for i in range(2)]
    ug = [C.sb("ug%d" % i, [128, CT], F32) for i in range(4)]
    c1 = [C.sb("c1%d" % i, [128, CT], F32) for i in range(4)]
    ost = [C.sb("ost%d" % i, [128, CT], F32) for i in range(2)]
    ps_stat = C.ps("ps_stat")
    psr = C.ps("psr")
    psm = [C.ps("psm%d" % i) for i in range(5)]
    cnt = {"p": 0, "u": 0, "g": 0, "t": 0, "o": 0}

    xv = xT.rearrange("(k p) t -> p k t", p=128)
    acv = acT.rearrange("(k p) t -> p k t", p=128)
    atv = atT.rearrange("(k p) t -> p k t", p=128)
    ofv = ofT.rearrange("(k p) t -> p k t", p=128)
    obv = obT.rearrange("(k p) t -> p k t", p=128)
    gv_ = gT.rearrange("(k p) t -> p k t", p=128)

    def newps():
        p = cnt["p"] % 5
        cnt["p"] += 1
        return psm[p], "psm%d" % p

    def do_tile(ti):
        c0 = ti * CSTEP
        w = min(CT, NPAD - c0)
        nout = w - 2
        cs = slice(c0, c0 + w)
        S.dma("sp", xt[:, :, :w], xv[:, :, cs], writes=["xt"])
        S.dma("sp", mx[:, 0:2, :w], acv[:, :, cs], writes=[("mx", 0)])
        S.dma("sp", mx[:, 2:5, :w], atv[:, :, cs], writes=[("mx", 1)])
        S.dma("sp", oft[:, :, :w], ofv[:, :, cs], writes=["oft"])
        S.dma("sp", obt[:, :, :w], obv[:, :, cs], writes=["obt"])
        S.dma("sp", gt[:, :, :w], gv_[:, :, cs], writes=["gt"])
        S.op("dve", lambda: nc.vector.tensor_tensor(out=oft[:, :, :w], in0=oft[:, :, :w], in1=obt[:, :, :w], op=ALU.add),
             reads=["oft", "obt"], writes=["oft"])
        S.op("act", lambda: nc.scalar.activation(out=sq[:, 0:3, :w], in_=oft[:, :, :w], func=AF.Square), reads=["oft"], writes=["sq"])
        for c in range(3):
            S.op("pe", (lambda c=c: nc.tensor.matmul(psr[:, :w], bd[:], sq[:, c, :w], start=True, stop=True)), reads=["sq", "bd"], writes=["psr"])
            emit_rstd(C, S, nc, psr, rstd, "psr", "rstd", w, 64, rtmp)
            S.op("dve", (lambda c=c: nc.vector.scalar_tensor_tensor(out=oft[:, c, :w], in0=oft[:, c, :w], scalar=rw[:, c:c + 1], in1=rstd[:, :w],
                                                                    op0=ALU.mult, op1=ALU.mult)),
                 reads=["oft", "rw", "rstd"], writes=["oft"])
            S.op("dve", (lambda c=c: nc.vector.tensor_tensor(out=mx[:, 5 + c, :w], in0=oft[:, c, :w], in1=gt[:, c, :w], op=ALU.mult)),
                 reads=["oft", "gt"], writes=[("mx", 2)])
        MX = [("mx", 0), ("mx", 1), ("mx", 2)]
        for oc in range(KC):
            pt, kp = newps()
            for k in range(KC):
                S.op("pe", (lambda k=k, pt=pt, oc=oc: nc.tensor.matmul(pt[:, :w], wo[:, k, oc * 128:(oc + 1) * 128], mx[:, k, :w],
                                                                      start=(k == 0), stop=(k == KC - 1))),
                     reads=WO + MX, writes=[kp])
            S.op("dve", (lambda pt=pt, oc=oc: nc.vector.scalar_tensor_tensor(out=xt[:, oc, :w], in0=pt[:, :w], scalar=modT[:, oc:oc + 1], in1=xt[:, oc, :w],
                                                                            op0=ALU.mult, op1=ALU.add)),
                 reads=[kp, "m_mod", "xt"], writes=["xt"])
        S.op("act", lambda: nc.scalar.activation(out=sq[:, :, :w], in_=xt[:, :, :w], func=AF.Square), reads=["xt"], writes=["sq"])
        for k in range(KC):
            S.op("pe", (lambda k=k: nc.tensor.matmul(ps_stat[:, :w], ones[:], sq[:, k, :w], start=(k == 0), stop=(k == KC - 1))),
                 reads=["sq", "ones"], writes=["ps_stat"])
        emit_rstd(C, S, nc, ps_stat, rstd, "ps_stat", "rstd", w, D, rtmp)
        for k in range(KC):
            q = cnt["t"] % 2
            cnt["t"] += 1
            S.op("dve", (lambda k=k, q=q: nc.vector.scalar_tensor_tensor(out=tmpf[q][:, :w], in0=xt[:, k, :w], scalar=scl[:, k:k + 1], in1=rstd[:, :w],
                                                                        op0=ALU.mult, op1=ALU.mult)),
                 reads=["xt", "scl", "rstd"], writes=["tmpf%d" % q])
            S.op("act", (lambda k=k, q=q: nc.scalar.activation(out=h2[:, k, :w], in_=tmpf[q][:, :w], func=AF.Identity, bias=modT[:, 8 + k:9 + k])),
                 reads=["tmpf%d" % q, "m_mod"], writes=["h2"])
        for j in range(NJ):
            res = []
            for part in range(2):
                col = j + part * NJ
                pt, kp = newps()
                for k in range(KC):
                    S.op("pe", (lambda k=k, pt=pt, col=col: nc.tensor.matmul(pt[:, :w], wu[:, k, col * 128:(col + 1) * 128], h2[:, k, :w],
                                                                            start=(k == 0), stop=(k == KC - 1))),
                         reads=WU + ["h2"], writes=[kp])
                u = cnt["u"] % 4
                cnt["u"] += 1
                S.op("act", (lambda pt=pt, u=u: nc.scalar.copy(out=ug[u][:, :w], in_=pt[:, :w])), reads=[kp], writes=["ug%d" % u])
                S.op("act", (lambda pt=pt, u=u, col=col: nc.scalar.activation(out=c1[u][:, :w], in_=pt[:, :w], func=AF.Copy, scale=cw[:, col, 1:2])),
                     reads=[kp, "cw"], writes=["c1%d" % u])
                if ti == 0:
                    S.op("dve", (lambda u=u: nc.vector.tensor_scalar(out=ug[u][:, 0:1], in0=ug[u][:, 0:1], scalar1=fl[:, 0:1], scalar2=None, op0=ALU.mult)),
                         reads=["ug%d" % u, "fl"], writes=["ug%d" % u])
                if ti == NCT - 1:
                    S.op("dve", (lambda u=u: nc.vector.tensor_scalar(out=ug[u][:, w - 1:w], in0=ug[u][:, w - 1:w], scalar1=fl[:, 1:2], scalar2=None, op0=ALU.mult)),
                         reads=["ug%d" % u, "fl"], writes=["ug%d" % u])
                S.op("dve", (lambda u=u, col=col: nc.vector.scalar_tensor_tensor(out=c1[u][:, 1:w - 1], in0=ug[u][:, 0:w - 2], scalar=cw[:, col, 0:1],
                                                                                in1=c1[u][:, 1:w - 1], op0=ALU.mult, op1=ALU.add)),
                     reads=["ug%d" % u, "c1%d" % u, "cw"], writes=["c1%d" % u])
                S.op("dve", (lambda u=u, col=col: nc.vector.scalar_tensor_tensor(out=c1[u][:, 1:w - 1], in0=ug[u][:, 2:w], scalar=cw[:, col, 2:3],
                                                                                in1=c1[u][:, 1:w - 1], op0=ALU.mult, op1=ALU.add)),
                     reads=["ug%d" % u, "c1%d" % u, "cw"], writes=["c1%d" % u])
                res.append(u)
            ugate, uval = res
            g = cnt["g"] % 2
            cnt["g"] += 1
            S.op("act", (lambda ugate=ugate: nc.scalar.activation(out=c1[ugate][:, 1:w - 1], in_=c1[ugate][:, 1:w - 1], func=AF.Gelu)),
                 reads=["c1%d" % ugate], writes=["c1%d" % ugate])
            S.op("dve", (lambda uval=uval, ugate=ugate, j=j: nc.vector.tensor_tensor(out=gv[:, j, 1:w - 1], in0=c1[ugate][:, 1:w - 1], in1=c1[uval][:, 1:w - 1], op=ALU.mult)),
                 reads=["c1%d" % ugate, "c1%d" % uval], writes=[("gv", j)])
        GV = [("gv", j) for j in range(NJ)]
        for oc in range(KC):
            pt, kp = newps()
            for j in range(NJ):
                S.op("pe", (lambda j=j, pt=pt, oc=oc: nc.tensor.matmul(pt[:, 1:w - 1], wd[:, j, oc * 128:(oc + 1) * 128], gv[:, j, 1:w - 1],
                                                                      start=(j == 0), stop=(j == NJ - 1))),
                     reads=WD + GV, writes=[kp])
            if not final:
                o = cnt["o"] % 2
                cnt["o"] += 1
                S.op("dve", (lambda pt=pt, oc=oc, o=o: nc.vector.scalar_tensor_tensor(out=ost[o][:, 1:w - 1], in0=pt[:, 1:w - 1], scalar=modT[:, 24 + oc:25 + oc],
                                                                                     in1=xt[:, oc, 1:w - 1], op0=ALU.mult, op1=ALU.add)),
                     reads=[kp, "m_mod", "xt"], writes=["ost%d" % o])
                S.dma("pool", o_xT[oc * 128:(oc + 1) * 128, c0:c0 + nout], ost[o][:, 1:w - 1], reads=["ost%d" % o])
            else:
                S.op("dve", (lambda pt=pt, oc=oc: nc.vector.scalar_tensor_tensor(out=xt[:, oc, 1:w - 1], in0=pt[:, 1:w - 1], scalar=modT[:, 24 + oc:25 + oc],
                                                                                in1=xt[:, oc, 1:w - 1], op0=ALU.mult, op1=ALU.add)),
                     reads=[kp, "m_mod", "xt"], writes=["xt"])
        if final:
            S.op("act", lambda: nc.scalar.activation(out=sq[:, :, 1:w - 1], in_=xt[:, :, 1:w - 1], func=AF.Square), reads=["xt"], writes=["sq"])
            for k in range(KC):
                S.op("pe", (lambda k=k: nc.tensor.matmul(ps_stat[:, 1:w - 1], ones[:], sq[:, k, 1:w - 1], start=(k == 0), stop=(k == KC - 1))),
                     reads=["sq", "ones"], writes=["ps_stat"])
            emit_rstd(C, S, nc, ps_stat, rstd, "ps_stat", "rstd", w, D, rtmp)
            for k in range(KC):
                o = cnt["o"] % 2
                cnt["o"] += 1
                S.op("dve", (lambda k=k, o=o: nc.vector.scalar_tensor_tensor(out=ost[o][:, 1:w - 1], in0=xt[:, k, 1:w - 1], scalar=fw[:, k:k + 1], in1=rstd[:, 1:w - 1],
                                                                            op0=ALU.mult, op1=ALU.mult)),
                     reads=["xt", "fw", "rstd"], writes=["ost%d" % o])
                S.dma("pool", o_xT[k * 128:(k + 1) * 128, c0:c0 + nout], ost[o][:, 1:w - 1], reads=["ost%d" % o])

    for ti in range(NCT):
        do_tile(ti)
    return C.finish()
CK = 31
CH = CK // 2


def build_Bc():
    C = Ctx()
    nc, S = C.nc, C.S
    aT = C.din("aT", [256, TOK + 2 * CH], BF16)
    cwd = C.din("cw_d", [128, 2, CK], F32)
    cbd = C.din("cb_d", [128, 2], F32)
    lwd = C.din("lw_d", [128, 2], F32)
    lbd = C.din("lb_d", [128, 2], F32)
    idd = C.din("ident_d", [128, 128], F32)
    o_ac = C.dout("o_ac", [256, TOK], BF16)

    C.eps_t = C.sb("eps", [128, 1], F32)
    S.op("dve", lambda: nc.vector.memset(C.eps_t[:], EPS), writes=["eps"])
    onesf = C.sb("onesf", [128, 128], F32)
    S.op("dve", lambda: nc.vector.memset(onesf[:], 1.0), writes=["onesf"])
    idt = C.sb("idt", [128, 128], F32)
    S.dma("sp", idt[:], idd, writes=["idt"])
    cw = C.sb("cw", [128, 2, CK], F32)
    S.dma("sp", cw[:], cwd, writes=["cw"])
    cb = C.sb("cb", [128, 2], F32)
    S.dma("sp", cb[:], cbd, writes=["cb"])
    lw = C.sb("lw", [128, 2], F32)
    S.dma("sp", lw[:], lwd, writes=["lw"])
    lb = C.sb("lb", [128, 2], F32)
    S.dma("sp", lb[:], lbd, writes=["lb"])
    dg = C.sb("dg", [128, 2, CK, 128], BF16)
    for c in range(2):
        for k in range(CK):
            S.op("dve", (lambda c=c, k=k: nc.vector.tensor_scalar(out=dg[:, c, k, :], in0=idt[:], scalar1=cw[:, c, k:k + 1], scalar2=None, op0=ALU.mult)),
                 reads=["idt", "cw"], writes=["dg"])
    at = [C.sb("at%d" % i, [128, 2, TT + 2 * CH], BF16) for i in range(2)]
    y = C.sb("y", [128, 2, TT], F32)
    ysq = C.sb("ysq", [128, 2, TT], F32)
    mean = C.sb("mean", [128, TT], F32)
    msq = C.sb("msq", [128, TT], F32)
    var = C.sb("var", [128, TT], F32)
    rstd = C.sb("rstd", [128, TT], F32)
    tmp = C.sb("tmp", [128, TT], F32)
    ob = [C.sb("ob%d" % i, [128, TT], BF16) for i in range(2)]
    pc = [C.ps("pc%d" % i) for i in range(2)]
    ps1 = C.ps("ps1")
    ps2 = C.ps("ps2")
    av = aT.rearrange("(c p) t -> p c t", p=128)
    cnt = {"o": 0}

    def load(i):
        b = i % 2
        S.dma("sp", at[b][:], av[:, :, i * TT:i * TT + TT + 2 * CH], writes=["at%d" % b])

    def do_tile(i):
        b = i % 2
        for c in range(2):
            for k in range(CK):
                S.op("pe", (lambda c=c, k=k: nc.tensor.matmul(pc[c][:], dg[:, c, k, :], at[b][:, c, k:k + TT], start=(k == 0), stop=(k == CK - 1))),
                     reads=["dg", "at%d" % b], writes=["pc%d" % c])
            S.op("act", (lambda c=c: nc.scalar.activation(out=y[:, c, :], in_=pc[c][:], func=AF.Identity, bias=cb[:, c:c + 1])),
                 reads=["pc%d" % c, "cb"], writes=[("y", c)])
            S.op("act", (lambda c=c: nc.scalar.activation(out=ysq[:, c, :], in_=y[:, c, :], func=AF.Square)), reads=[("y", c)], writes=[("ysq", c)])
        for c in range(2):
            S.op("pe", (lambda c=c: nc.tensor.matmul(ps1[:], onesf[:], y[:, c, :], start=(c == 0), stop=(c == 1))), reads=[("y", c), "onesf"], writes=["ps1"])
        for c in range(2):
            S.op("pe", (lambda c=c: nc.tensor.matmul(ps2[:], onesf[:], ysq[:, c, :], start=(c == 0), stop=(c == 1))), reads=[("ysq", c), "onesf"], writes=["ps2"])
        S.op("dve", lambda: nc.vector.tensor_scalar(out=mean[:], in0=ps1[:], scalar1=1.0 / 256, scalar2=None, op0=ALU.mult), reads=["ps1"], writes=["mean"])
        S.op("dve", lambda: nc.vector.tensor_tensor(out=msq[:], in0=mean[:], in1=mean[:], op=ALU.mult), reads=["mean"], writes=["msq"])
        S.op("dve", lambda: nc.vector.scalar_tensor_tensor(out=var[:], in0=ps2[:], scalar=1.0 / 256, in1=msq[:], op0=ALU.mult, op1=ALU.subtract),
             reads=["ps2", "msq"], writes=["var"])
        S.op("act", lambda: nc.scalar.activation(out=tmp[:], in_=var[:], func=AF.Sqrt, bias=C.eps_t[:, 0:1]), reads=["var", "eps"], writes=["tmp"])
        S.op("dve", lambda: nc.vector.reciprocal(out=rstd[:], in_=tmp[:]), reads=["tmp"], writes=["rstd"])
        for c in range(2):
            S.op("dve", (lambda c=c: nc.vector.tensor_tensor(out=y[:, c, :], in0=y[:, c, :], in1=mean[:], op=ALU.subtract)),
                 reads=[("y", c), "mean"], writes=[("y", c)])
            S.op("dve", (lambda c=c: nc.vector.scalar_tensor_tensor(out=y[:, c, :], in0=y[:, c, :], scalar=lw[:, c:c + 1], in1=rstd[:], op0=ALU.mult, op1=ALU.mult)),
                 reads=[("y", c), "lw", "rstd"], writes=[("y", c)])
            o = cnt["o"] % 2
            cnt["o"] += 1
            S.op("act", (lambda c=c, o=o: nc.scalar.activation(out=ob[o][:], in_=y[:, c, :], func=AF.Silu, bias=lb[:, c:c + 1])),
                 reads=[("y", c), "lb"], writes=["ob%d" % o])
            S.dma("pool", o_ac[c * 128:(c + 1) * 128, i * TT:(i + 1) * TT], ob[o][:], reads=["ob%d" % o])

    load(0)
    for i in range(NT):
        if i + 1 < NT:
            load(i + 1)
        do_tile(i)
    return C.finish()
SEQ = 8192
DILS = (1, 4, 16)
NU = 3


def build_Ba():
    C = Ctx()
    nc, S = C.nc, C.S
    qd, kd, vd = {}, {}, {}
    for r in DILS:
        L = SEQ // r
        qd[r] = C.din("q%d" % r, [NU, 64, r * L], BF16)
        kd[r] = C.din("k%d" % r, [NU, 64, r * (L + 128)], BF16)
        vd[r] = C.din("v%d" % r, [NU, 128, r * (L // 128 + 1), 64], BF16)
    Ed = C.din("E", [NU, 3, 128, 256], BF16)
    seld = C.din("sel", [128, 64], F32)
    o_at = C.dout("o_at", [NU, 64, SEQ], BF16)

    sel = C.sb("sel_s", [128, 64], F32)
    S.dma("sp", sel[:], seld, writes=["sel"])
    qs = [C.sb("qs%d" % i, [64, SEQ], BF16) for i in range(2)]
    ks = [C.sb("ks%d" % i, [64, SEQ + 128 * 16], BF16) for i in range(2)]
    vs = [C.sb("vs%d" % i, [128, SEQ // 128 + 16, 128], BF16) for i in range(2)]
    for i in range(2):
        S.op("dve", (lambda i=i: nc.vector.memset(vs[i][:, :, 64:128], 1.0)), writes=["vs%d" % i])
    Et = [C.sb("Et%d" % i, [128, 256], BF16) for i in range(2)]
    acc = C.sb("acc", [128, SEQ], F32)
    rd = [C.sb("rd%d" % i, [64, 512], F32) for i in range(2)]
    pt = [C.sb("pt%d" % i, [128, 256], BF16) for i in range(3)]
    ost = [C.sb("ost%d" % i, [64, 512], BF16) for i in range(2)]
    pss = [C.ps("pss%d" % i, [128, 256], F32) for i in range(3)]
    pso = [C.ps("pso%d" % i, [128, 128], F32) for i in range(3)]
    psel = C.ps("psel", [64, 512], F32)
    cnt = {"l": 0, "p": 0, "o": 0}

    def load(u, bi):
        r = DILS[bi]
        L = SEQ // r
        b = cnt["l"] % 2
        cnt["l"] += 1
        S.dma("sp", qs[b][:, :], qd[r][u, :, :], writes=["qs%d" % b])
        S.dma("sp", ks[b][:, :r * (L + 128)], kd[r][u, :, :], writes=["ks%d" % b])
        S.dma("sp", vs[b][:, :r * (L // 128 + 1), 0:64], vd[r][u, :, :, :], writes=["vs%d" % b])
        S.dma("sp", Et[b][:], Ed[u, bi, :, :], writes=["Et%d" % b])
        return b

    def branch(u, bi, b):
        r = DILS[bi]
        L = SEQ // r
        nb_n = L // 128
        its = [(rho, nb) for rho in range(r) for nb in range(nb_n)]
        pidx = {}

        def stage1(rho, nb):
            p = cnt["p"] % 3
            cnt["p"] += 1
            pidx[(rho, nb)] = p
            q0 = rho * L + nb * 128
            for c in range(2):
                k0 = rho * (L + 128) + (nb + c) * 128
                S.op("pe", (lambda c=c, k0=k0: nc.tensor.matmul(pss[p][:, c * 128:(c + 1) * 128], ks[b][:, k0:k0 + 128], qs[b][:, q0:q0 + 128],
                                                                 start=True, stop=True)),
                     reads=["ks%d" % b, "qs%d" % b], writes=["pss%d" % p])
            S.op("act", (lambda: nc.scalar.activation(out=pt[p][:], in_=pss[p][:], func=AF.Exp, scale=0.125)), reads=["pss%d" % p], writes=["pt%d" % p])
            S.op("dve", (lambda: nc.vector.tensor_tensor(out=pt[p][:], in0=pt[p][:], in1=Et[b][:], op=ALU.mult)),
                 reads=["pt%d" % p, "Et%d" % b], writes=["pt%d" % p])
            if nb == 0:
                S.op("dve", (lambda: nc.vector.memset(pt[p][0:64, 0:128], 0.0)), writes=["pt%d" % p])
            if nb == nb_n - 1:
                S.op("dve", (lambda: nc.vector.memset(pt[p][64:128, 128:256], 0.0)), writes=["pt%d" % p])

        def stage2(rho, nb):
            p = pidx[(rho, nb)]
            for c in range(2):
                vt = rho * (nb_n + 1) + nb + c
                S.op("pe", (lambda c=c, vt=vt: nc.tensor.matmul(pso[p][:, :], vs[b][:, vt, :], pt[p][:, c * 128:(c + 1) * 128], start=(c == 0), stop=(c == 1))),
                     reads=["vs%d" % b, "pt%d" % p], writes=["pso%d" % p])
            t0 = rho + r * 128 * nb
            sl = slice(t0, t0 + 127 * r + 1, r) if r > 1 else slice(t0, t0 + 128)
            last = (rho == r - 1 and nb == nb_n - 1)
            wk = [("acc", u, bi, rho, nb)] + ([("accdone", u, bi)] if last else [])
            rk = ["pso%d" % p] + ([("accdone", u, bi - 1)] if bi > 0 else ([("finished", u - 1)] if u > 0 else []))
            if bi == 0:
                S.op("act", (lambda: nc.scalar.copy(out=acc[:, sl], in_=pso[p][:, :])), reads=rk, writes=wk)
            else:
                S.op("dve", (lambda: nc.vector.tensor_tensor(out=acc[:, sl], in0=acc[:, sl], in1=pso[p][:, :], op=ALU.add)),
                     reads=rk, writes=wk)

        stage1(*its[0])
        if len(its) > 1:
            stage1(*its[1])
        for idx, it in enumerate(its):
            if idx + 2 < len(its):
                stage1(*its[idx + 2])
            stage2(*it)

    def finish_unit(u):
        for g in range(SEQ // 512):
            o = cnt["o"] % 2
            cnt["o"] += 1
            sl = slice(g * 512, (g + 1) * 512)
            S.op("pe", (lambda sl=sl: nc.tensor.matmul(psel[:, :], sel[:], acc[:, sl], start=True, stop=True)), reads=[("accdone", u, 2), "sel"], writes=["psel"])
            S.op("dve", (lambda o=o: nc.vector.reciprocal(out=rd[o][:], in_=psel[:, :])), reads=["psel"], writes=["rd%d" % o])
            S.op("dve", (lambda o=o, sl=sl: nc.vector.tensor_tensor(out=ost[o][:], in0=acc[0:64, sl], in1=rd[o][:], op=ALU.mult)),
                 reads=[("accdone", u, 2), "rd%d" % o], writes=["ost%d" % o] + ([("finished", u)] if g == SEQ // 512 - 1 else []))
            S.dma("pool", o_at[u, :, sl], ost[o][:], reads=["ost%d" % o])

    seqs = [(u, bi) for u in range(NU) for bi in range(3)]
    bnext = load(*seqs[0])
    for idx, (u, bi) in enumerate(seqs):
        bcur = bnext
        if idx + 1 < len(seqs):
            bnext = load(*seqs[idx + 1])
        branch(u, bi, bcur)
        if bi == 2:
            finish_unit(u)
    return C.finish()
RTILES = SEQ // 128
NCH = SEQ // 64


def rec_consts():
    s = np.arange(128)[:, None]
    t = np.arange(128)[None, :]
    same = (s // 64) == (t // 64)
    tri = same & (s <= t)
    refm = same & ((s % 64) <= 31)
    dm = tri.astype(np.float32) - refm.astype(np.float32)
    mask = tri.astype(np.uint32)
    ind = np.zeros((128, 6), np.float32)
    sl = np.arange(128)
    for ch in range(2):
        inch = (sl // 64) == ch
        ind[:, 3 * ch + 0] = inch & ((sl % 64) > 31)
        ind[:, 3 * ch + 1] = inch
        ind[:, 3 * ch + 2] = inch & ((sl % 64) <= 31)
    return dm, mask, ind


def build_Br(NU=NU, SEQ=SEQ, dbg=9):
    RTILES = SEQ // 128
    NCH = SEQ // 64
    C = Ctx()
    nc, S = C.nc, C.S
    qd = C.din("rq", [NU, SEQ, 128], F32)
    kd = C.din("rk", [NU, SEQ, 128], F32)
    ld = C.din("rl", [NU, SEQ, 128], F32)
    vd = C.din("rv", [NU, SEQ, 128], BF16)
    dmd = C.din("dm", [128, 128], F32)
    mkd = C.din("mask", [128, 128], mybir.dt.uint32)
    idd = C.din("ind", [128, 6], F32)
    ied = C.din("identb", [128, 128], BF16)
    o_r = C.dout("o_r", [NU, 128, SEQ], F32)

    dm = C.sb("dm_s", [128, 128], F32)
    S.dma("sp", dm[:], dmd, writes=["dm"])
    mk = C.sb("mk_s", [128, 128], mybir.dt.uint32)
    S.dma("sp", mk[:], mkd, writes=["mk"])
    ind = C.sb("ind_s", [128, 6], F32)
    S.dma("sp", ind[:], idd, writes=["ind"])
    idb = C.sb("idb", [128, 128], BF16)
    S.dma("sp", idb[:], ied, writes=["idb"])

    NB = 3
    qt = [C.sb("qt%d" % i, [128, 128], F32) for i in range(NB)]
    kt = [C.sb("kt%d" % i, [128, 128], F32) for i in range(NB)]
    lt = [C.sb("lt%d" % i, [128, 128], F32) for i in range(NB)]
    vt = [C.sb("vt%d" % i, [128, 128], BF16) for i in range(NB)]
    e1 = [C.sb("e1%d" % i, [128, 128], F32) for i in range(2)]
    e2 = [C.sb("e2%d" % i, [128, 128], F32) for i in range(2)]
    qtl = [C.sb("qtl%d" % i, [128, 128], BF16) for i in range(2)]
    ktl = [C.sb("ktl%d" % i, [128, 128], BF16) for i in range(2)]
    kfm = [C.sb("kfm%d" % i, [128, 128], BF16) for i in range(2)]
    qfm = C.sb("qfm", [128, SEQ], BF16)
    EX = C.sb("EX", [128, RTILES, 6], F32)
    AT = C.sb("AT", [128, RTILES, 2, 128], BF16)
    U = C.sb("U", [128, 64, NCH], F32)
    Sc = C.sb("Sc", [128, 64, NCH], F32)
    Sp = C.sb("Sp", [128, NCH, 64], BF16)
    Gc = C.sb("Gc", [128, NCH], F32)
    ER = C.sb("ER", [128, NCH], F32)
    ot = [C.sb("ot%d" % i, [64, 2, 128], F32) for i in range(2)]
    bk0 = C.ps("bk0", [128, 256], F32)
    pd = [bk0[:, 0:128], bk0[:, 0:128]]
    pst = bk0[:, 128:136]
    bk1 = C.ps("bk1", [128, 2, 128], BF16)
    ptq = bk1[:, 0, :]
    ptk = bk1[:, 1, :]
    paA = C.ps("paA", [128, 128], F32)
    paB = C.ps("paB", [128, 128], F32)
    pah = [paA, paB]
    puA = C.ps("puA", [128, 128], F32)
    puB = C.ps("puB", [128, 128], F32)
    puc = [puA, puB]
    poA = C.ps("poA", [64, 128], F32)
    poB = C.ps("poB", [64, 128], F32)
    poh = [poA, poB]

    S.op("dve", lambda: nc.vector.memset(AT[:], 0.0), writes=["ATz"] + [("AT", m_, h_) for m_ in range(RTILES) for h_ in range(2)])

    def load1(u, m):
        b = m % NB
        r = slice(m * 128, (m + 1) * 128)
        S.dma("sp", qt[b][:], qd[u, r, :], writes=["qt%d" % b])
        S.dma("sp", kt[b][:], kd[u, r, :], writes=["kt%d" % b])
        S.dma("sp", lt[b][:], ld[u, r, :], writes=["lt%d" % b])
        S.dma("sp", vt[b][:], vd[u, r, :], writes=["vt%d" % b])

    def pass1(u, m):
        b = m % NB
        a = m % 2
        S.op("pe", lambda: nc.tensor.matmul(pd[a], dm[:], lt[b][:], start=True, stop=True), reads=["dm", "lt%d" % b], writes=["b0"])
        S.op("act", lambda: nc.scalar.activation(out=e1[a][:], in_=pd[a], func=AF.Exp), reads=["b0"], writes=["e1%d" % a])
        S.op("act", lambda: nc.scalar.activation(out=e2[a][:], in_=pd[a], func=AF.Exp, scale=-1.0), reads=["b0"], writes=["e2%d" % a])
        S.op("dve", lambda: nc.vector.tensor_tensor(out=qtl[a][:], in0=qt[b][:], in1=e1[a][:], op=ALU.mult), reads=["qt%d" % b, "e1%d" % a], writes=["qtl%d" % a])
        S.op("dve", lambda: nc.vector.tensor_tensor(out=ktl[a][:], in0=kt[b][:], in1=e2[a][:], op=ALU.mult), reads=["kt%d" % b, "e2%d" % a], writes=["ktl%d" % a])
        if dbg < 2:
            return
        S.op("pe", lambda: nc.tensor.transpose(ptq, qtl[a][:], idb[:]), reads=["qtl%d" % a, "idb"], writes=["b1"])
        S.op("pe", lambda: nc.tensor.transpose(ptk, ktl[a][:], idb[:]), reads=["ktl%d" % a, "idb"], writes=["b1"])
        S.op("act", lambda: nc.scalar.copy(out=qfm[:, m * 128:(m + 1) * 128], in_=ptq), reads=["b1"], writes=[("qfm", m)])
        S.op("act", lambda: nc.scalar.copy(out=kfm[a][:], in_=ptk), reads=["b1"], writes=["kfm%d" % a])
        if dbg < 3:
            return
        S.op("pe", lambda: nc.tensor.matmul(pst[:, 0:6], lt[b][:], ind[:], start=True, stop=True), reads=["lt%d" % b, "ind"], writes=["b0"])
        S.op("act", lambda: nc.scalar.activation(out=EX[:, m, :], in_=pst[:, 0:6], func=AF.Exp), reads=["b0"], writes=[("EX", m)])
        if dbg < 4:
            return
        for hh in range(2):
            hs_ = slice(64 * hh, 64 * hh + 64)
            S.op("pe", (lambda hh=hh, hs_=hs_: nc.tensor.matmul(pah[hh][:, :], kfm[a][hs_, :], qfm[hs_, m * 128:(m + 1) * 128], start=True, stop=True)),
                 reads=["kfm%d" % a, ("qfm", m)], writes=["pa%d" % hh])
        for hh in range(2):
            S.op("dve", (lambda hh=hh: nc.vector.copy_predicated(out=AT[:, m, hh, :], mask=mk[:], data=pah[hh][:, :])),
                 reads=["pa%d" % hh, "mk", "ATz"], writes=[("AT", m, hh)])
        if dbg < 5:
            return
        for ch in range(2):
            cs_ = slice(64 * ch, 64 * ch + 64)
            S.op("pe", (lambda ch=ch, cs_=cs_: nc.tensor.matmul(puc[ch][:, :], ktl[a][cs_, :], vt[b][cs_, :], start=True, stop=True)),
                 reads=["ktl%d" % a, "vt%d" % b], writes=["pu%d" % ch])
            n = 2 * m + ch
            for hh in range(2):
                hs_ = slice(64 * hh, 64 * hh + 64)
                eng = "dve"
                if eng == "dve":
                    S.op("dve", (lambda ch=ch, hs_=hs_, n=n: nc.vector.tensor_scalar(out=U[hs_, :, n], in0=puc[ch][hs_, hs_], scalar1=EX[hs_, m, 3 * ch:3 * ch + 1],
                                                                                    scalar2=None, op0=ALU.mult)),
                         reads=["pu%d" % ch, ("EX", m)], writes=[("U", n, hh)])
                else:
                    S.op("act", (lambda ch=ch, hs_=hs_, n=n: nc.scalar.activation(out=U[hs_, :, n], in_=puc[ch][hs_, hs_], func=AF.Copy, scale=EX[hs_, m, 3 * ch:3 * ch + 1])),
                         reads=["pu%d" % ch, ("EX", m)], writes=[("U", n, hh)])

    def scan(u):
        allU = [("U", n, hh) for n in range(NCH) for hh in range(2)]
        allEX = [("EX", m) for m in range(RTILES)]
        exv = EX[:, :, :].rearrange("p m (c k) -> p (m c) k", c=2)
        S.op("dve", lambda: nc.vector.tensor_copy(out=Gc[:], in_=exv[:, :, 1]), reads=allEX, writes=["Gc"])
        S.op("dve", lambda: nc.vector.tensor_copy(out=ER[:], in_=exv[:, :, 2]), reads=allEX, writes=["ER"])
        for dv in range(64):
            S.op("dve", (lambda dv=dv: nc.vector.tensor_tensor_scan(out=Sc[:, dv, :], data0=Gc[:], data1=U[:, dv, :], initial=0.0, op0=ALU.mult, op1=ALU.add)),
                 reads=allU + ["Gc"], writes=[("Sc", dv)])
        S.op("dve", lambda: nc.vector.memset(Sp[:, 0, :], 0.0), writes=[("Sp", -1)])
        for dv in range(64):
            S.op("dve", (lambda dv=dv: nc.vector.tensor_tensor(out=Sp[:, 1:NCH, dv], in0=Sc[:, dv, 0:NCH - 1], in1=ER[:, 1:NCH], op=ALU.mult)),
                 reads=[("Sc", dv), "ER"], writes=[("Sp", dv)])

    def load2(u, m):
        b = m % NB
        S.dma("sp", vt[b][:], vd[u, m * 128:(m + 1) * 128, :], writes=["vt%d" % b])

    def pass2(u, m):
        b = m % NB
        a = m % 2
        allSp = [("Sp", dv) for dv in range(-1, 64)]
        for hh in range(2):
            hs_ = slice(64 * hh, 64 * hh + 64)
            S.op("pe", (lambda hh=hh, hs_=hs_: nc.tensor.matmul(poh[hh][:, :], vt[b][:, hs_], AT[:, m, hh, :], start=True, stop=False)),
                 reads=["vt%d" % b, ("AT", m, hh)], writes=["po%d" % hh])
            for ch in range(2):
                cs_ = slice(64 * ch, 64 * ch + 64)
                n = 2 * m + ch
                S.op("pe", (lambda hh=hh, hs_=hs_, cs_=cs_, n=n, ch=ch: nc.tensor.matmul(poh[hh][:, cs_], Sp[hs_, n, :], qfm[hs_, m * 128 + 64 * ch: m * 128 + 64 * ch + 64],
                                                                                 start=False, stop=(ch == 1))),
                     reads=allSp + [("qfm", m)], writes=["po%d" % hh])
        for hh in range(2):
            S.op("act", (lambda hh=hh: nc.scalar.copy(out=ot[a][:, hh, :], in_=poh[hh][:, :])), reads=["po%d" % hh], writes=["ot%d" % a])
        S.dma("pool", o_r[u, :, m * 128:(m + 1) * 128].rearrange("(h d) t -> d h t", h=2), ot[a][:], reads=["ot%d" % a])

    for u in range(NU):
        load1(u, 0)
        load1(u, 1)
        for m in range(RTILES):
            if m + 2 < RTILES:
                load1(u, m + 2)
            pass1(u, m)
        if dbg < 6:
            continue
        scan(u)
        if dbg < 7:
            continue
        load2(u, 0)
        load2(u, 1)
        for m in range(RTILES):
            if m + 2 < RTILES:
                load2(u, m + 2)
            pass2(u, m)
    return C.finish()
def att_tables():
    slopes = 2.0 ** (-8.0 * np.arange(1, 7) / 6)
    p = np.arange(128)[:, None, None]
    c = np.arange(2)[None, :, None]
    i = np.arange(128)[None, None, :]
    rel = 128 * c + p - 64 - i
    out = np.zeros((6, 3, 128, 256), np.float32)
    for h in range(6):
        for bi, r in enumerate((1, 4, 16)):
            e = np.where(np.abs(rel) <= 64, np.exp(-slopes[h] * r * np.abs(rel)), 0.0)
            out[h, bi] = e.reshape(128, 256)
    return out.astype(NPBF)


def att_layout(q_fm, k_fm, v_tm, r):
    T = q_fm.shape[1]
    L = T // r
    nb = L // 128
    qr = q_fm.reshape(64, L, r).transpose(0, 2, 1).reshape(64, r * L)
    kr = np.zeros((64, r, L + 128), k_fm.dtype)
    kr[:, :, 64:64 + L] = k_fm.reshape(64, L, r).transpose(0, 2, 1)
    vr = np.zeros((r, L + 128, 64), v_tm.dtype)
    vr[:, 64:64 + L] = v_tm.reshape(L, r, 64).transpose(1, 0, 2)
    vr = vr.reshape(r, nb + 1, 128, 64).transpose(2, 0, 1, 3).reshape(128, r * (nb + 1), 64)
    return np.ascontiguousarray(qr), np.ascontiguousarray(kr.reshape(64, r * (L + 128))), np.ascontiguousarray(vr)
_PROG_CACHE = {}


def _prog(name, fn, *a):
    return fn(*a)[0]


def _t128(v):
    return np.ascontiguousarray(np.asarray(v, np.float32).reshape(-1, 128).T)


def _run(nc, maps):
    res = run_bass_kernel_spmd(nc, maps, core_ids=list(range(8)))
    return res.results


def kernel(x, c, w_ada, b_ada, norm1_w, w_in, conv_a_w, conv_a_b, ln_a_w, ln_a_b,
           lb_gamma, rec_norm_w, w_out, norm2_w, w_up, conv_f_w, w_down, final_norm_w):
    f32 = np.float32
    x = np.asarray(x, f32)
    B, T, _ = x.shape
    HALF = T // 2
    cores = [(b, h) for b in range(B) for h in range(2)]
    xT = [np.ascontiguousarray(x[b].T) for b in range(B)]
    lbg_rep = np.ascontiguousarray(np.broadcast_to(np.asarray(lb_gamma, f32).reshape(1, -1), (128, 1536)))
    E_all = att_tables()
    dmc, maskc, indc = rec_consts()
    identf = np.eye(128, dtype=f32)
    identb = identf.astype(NPBF)
    bdm = np.kron(np.eye(2), np.ones((64, 64))).astype(f32)
    selm = np.zeros((128, 64), f32)
    selm[64 + np.arange(64), np.arange(64)] = 1.0
    depth = w_in.shape[0]
    for l in range(depth):
        final = (l == depth - 1)
        maps = []
        for (b, h) in cores:
            maps.append({"xT": np.ascontiguousarray(xT[b][:, h * HALF:(h + 1) * HALF]), "ct": _t128(c[b]),
                         "wada": np.asarray(w_ada[l], f32), "bada": _t128(b_ada[l]), "n1w": _t128(norm1_w[l]),
                         "win": np.asarray(w_in[l], f32), "lbg": lbg_rep})
        ra = _run(_prog("A%d" % l, build_A, l), maps)

        def catT(name, b):
            return np.concatenate([ra[2 * b][name], ra[2 * b + 1][name]], axis=1)

        def catR(name, b):
            return np.concatenate([ra[2 * b][name], ra[2 * b + 1][name]], axis=0)
        aT = [catT("o_aT", b) for b in range(B)]
        qkT = [catT("o_qkT", b) for b in range(B)]
        gT = [catT("o_gT", b) for b in range(B)]
        vat = [catR("o_v", b) for b in range(B)]
        qr = [catR("o_qr", b) for b in range(B)]
        lf = [catR("o_lf", b) for b in range(B)]
        kk = [catR("o_kk", b) for b in range(B)]
        ir = [catR("o_ir", b) for b in range(B)]
        del ra
        maps = []
        cw = np.ascontiguousarray(np.asarray(conv_a_w[l], f32).T.reshape(2, 128, CK).transpose(1, 0, 2))

        def t2(v):
            return np.ascontiguousarray(np.asarray(v, f32).reshape(2, 128).T)
        for (b, h) in cores:
            pad = np.zeros((256, T + 2 * CH), aT[b].dtype)
            pad[:, CH:CH + T] = aT[b]
            maps.append({"aT": np.ascontiguousarray(pad[:, h * HALF:h * HALF + HALF + 2 * CH]), "cw_d": cw, "cb_d": t2(conv_a_b[l]),
                         "lw_d": t2(ln_a_w[l]), "lb_d": t2(ln_a_b[l]), "ident_d": identf})
        rc = _run(_prog("Bc", build_Bc), maps)
        acT = [np.concatenate([rc[2 * b]["o_ac"], rc[2 * b + 1]["o_ac"]], axis=1) for b in range(B)]
        del rc
        units = [(b, h) for b in range(B) for h in range(6)]
        maps = []
        for ci in range(8):
            us = units[3 * ci:3 * ci + 3]
            m = {"E": np.ascontiguousarray(np.stack([E_all[h] for (_, h) in us])), "sel": selm}
            for r in DILS:
                ql, kl, vl = [], [], []
                for (b, h) in us:
                    q_fm = qkT[b][h * 64:(h + 1) * 64, :]
                    k_fm = qkT[b][384 + h * 64:384 + (h + 1) * 64, :]
                    v_tm = vat[b][:, h * 64:(h + 1) * 64]
                    a_, b_, c_ = att_layout(np.ascontiguousarray(q_fm), np.ascontiguousarray(k_fm), np.ascontiguousarray(v_tm), r)
                    ql.append(a_)
                    kl.append(b_)
                    vl.append(c_)
                m["q%d" % r] = np.stack(ql)
                m["k%d" % r] = np.stack(kl)
                m["v%d" % r] = np.stack(vl)
            maps.append(m)
        rb = _run(_prog("Ba", build_Ba), maps)
        atT = [np.zeros((384, T), NPBF) for _ in range(B)]
        for ui, (b, h) in enumerate(units):
            atT[b][h * 64:(h + 1) * 64, :] = rb[ui // 3]["o_at"][ui % 3]
        del rb
        runits = [(b, d, hp) for b in range(B) for d in range(2) for hp in range(3)]
        maps = []
        for ci in range(8):
            us = runits[3 * ci:3 * ci + 3]
            ql, kl, ll, vl = [], [], [], []
            for (b, d, hp) in us:
                cs = slice(hp * 128, (hp + 1) * 128)
                cs2 = slice(d * 384 + hp * 128, d * 384 + (hp + 1) * 128)
                q_, k_, l_, v_ = qr[b][:, cs], kk[b][:, cs2], lf[b][:, cs2], ir[b][:, cs]
                if d == 1:
                    q_, k_, l_, v_ = q_[::-1], k_[::-1], l_[::-1], v_[::-1]
                ql.append(np.ascontiguousarray(q_))
                kl.append(np.ascontiguousarray(k_))
                ll.append(np.ascontiguousarray(l_))
                vl.append(np.ascontiguousarray(v_))
            maps.append({"rq": np.stack(ql), "rk": np.stack(kl), "rl": np.stack(ll), "rv": np.stack(vl),
                         "dm": dmc, "mask": maskc, "ind": indc, "identb": identb})
        rr = _run(_prog("Br", build_Br), maps)
        ofT = [np.zeros((384, T), f32) for _ in range(B)]
        obT = [np.zeros((384, T), f32) for _ in range(B)]
        for ui, (b, d, hp) in enumerate(runits):
            o = rr[ui // 3]["o_r"][ui % 3]
            if d == 0:
                ofT[b][hp * 128:(hp + 1) * 128, :] = o
            else:
                obT[b][hp * 128:(hp + 1) * 128, :] = o[:, ::-1]
        del rr
        cfw = np.ascontiguousarray(np.asarray(conv_f_w[l], f32).T.reshape(44, 128, 3).transpose(1, 0, 2))

        def padcols(a, h):
            p = np.zeros((a.shape[0], T + 2), a.dtype)
            p[:, 1:T + 1] = a
            return np.ascontiguousarray(p[:, h * HALF:h * HALF + HALF + 2])
        maps = []
        for (b, h) in cores:
            flags = np.ones((128, 2), f32)
            if h == 0:
                flags[:, 0] = 0.0
            if h == 1:
                flags[:, 1] = 0.0
            maps.append({"xT": padcols(xT[b], h), "acT": padcols(acT[b], h), "atT": padcols(atT[b], h), "ofT": padcols(ofT[b], h),
                         "obT": padcols(obT[b], h), "gT": padcols(gT[b], h), "rnw": _t128(rec_norm_w[l]), "ct": _t128(c[b]),
                         "wada": np.asarray(w_ada[l], f32), "bada": _t128(b_ada[l]), "n2w": _t128(norm2_w[l]), "fnw": _t128(final_norm_w),
                         "wout": np.asarray(w_out[l], f32), "wup": np.asarray(w_up[l], f32), "wdown": np.asarray(w_down[l], f32),
                         "cfw": cfw, "flags": flags, "bdm": bdm})
        rd = _run(_prog("CD%d" % final, build_CD, final), maps)
        xT = [np.concatenate([rd[2 * b]["o_xT"], rd[2 * b + 1]["o_xT"]], axis=1) for b in range(B)]
        del rd
    out = np.stack([np.ascontiguousarray(xT[b].T) for b in range(B)]).astype(f32)
    return out
```

```python
import numpy as np
import concourse.bass as bass
import concourse.mybir as mybir

from contextlib import ExitStack
import ml_dtypes


from concourse.bass_utils import run_bass_kernel_spmd

EPS = 1e-6
F_TINY = 1e-30
NPBF = ml_dtypes.bfloat16


F32 = mybir.dt.float32
BF16 = mybir.dt.bfloat16
AF = mybir.ActivationFunctionType
ALU = mybir.AluOpType

N_DMA_SEM = 8


class Sched:
    def __init__(self, nc, same_engine_sync=True):
        self.nc = nc
        self.ops = []
        self.same_engine_sync = same_engine_sync
        self.eng = {
            "pe": nc.tensor, "dve": nc.vector, "act": nc.scalar,
            "pool": nc.gpsimd, "sp": nc.sync,
        }

    def op(self, eng, fn, reads=(), writes=(), kind="c", fence=False):
        self.ops.append(dict(eng=eng, fn=fn, reads=tuple(reads), writes=tuple(writes), kind=kind, fence=fence))

    def dma(self, eng, out, in_, reads=(), writes=(), **kw):
        e = self.eng[eng]
        self.op(eng, lambda: e.dma_start(out=out, in_=in_, **kw), reads, writes, kind="d")

    def emit(self, stack):
        nc = self.nc
        ops = self.ops
        n = len(ops)
        last_w = {}
        readers = {}
        deps = [set() for _ in range(n)]
        for i, o in enumerate(ops):
            for k in o["reads"]:
                if k in last_w:
                    deps[i].add(last_w[k])
            for k in o["writes"]:
                if k in last_w:
                    deps[i].add(last_w[k])
                for r in readers.get(k, ()):
                    if r != i:
                        deps[i].add(r)
            for k in o["reads"]:
                readers.setdefault(k, []).append(i)
            for k in o["writes"]:
                last_w[k] = i
                readers[k] = []
        need = [[] for _ in range(n)]
        signal = [False] * n
        last_on = {}
        for i, o in enumerate(ops):
            if o.get("fence") and o["eng"] in last_on:
                p = last_on[o["eng"]]
                need[i].append(p)
                signal[p] = True
            if o["kind"] == "c":
                last_on[o["eng"]] = i
        for i, o in enumerate(ops):
            for p in deps[i]:
                po = ops[p]
                if po["kind"] == "d":
                    need[i].append(p)
                    continue
                if po["eng"] == o["eng"]:
                    if o["kind"] == "c" and (o["eng"] == "pe" or not self.same_engine_sync):
                        continue
                need[i].append(p)
                signal[p] = True
        comp_sem = {}
        for e in ("pe", "dve", "act", "pool"):
            comp_sem[e] = stack.enter_context(nc.semaphore("s_" + e))
        dma_sems = {}
        for e in ("sp", "pool", "act"):
            dma_sems[e] = [stack.enter_context(nc.semaphore("d_%s_%d" % (e, j))) for j in range(N_DMA_SEM)]
        cnt = {e: 0 for e in comp_sem}
        dcnt = {e: 0 for e in dma_sems}
        semval = [None] * n
        waited = {}
        sem_objs = {}

        def do_wait(engname, sem, val):
            key = (engname, id(sem))
            if waited.get(key, 0) >= val:
                return
            waited[key] = val
            self.eng[engname].wait_ge(sem, val)

        for i, o in enumerate(ops):
            e = o["eng"]
            wl = {}
            for p in need[i]:
                s, v = semval[p]
                if id(s) not in wl or wl[id(s)][1] < v:
                    wl[id(s)] = (s, v)
            if o["kind"] == "d":
                j = dcnt[e]
                sem = dma_sems[e][j % N_DMA_SEM]
                if j >= N_DMA_SEM:
                    prev = 16 * (j // N_DMA_SEM)
                    if id(sem) not in wl or wl[id(sem)][1] < prev:
                        wl[id(sem)] = (sem, prev)
            for s, v in wl.values():
                do_wait(e, s, v)
            ins = o["fn"]()
            if o["kind"] == "d":
                dcnt[e] = j + 1
                v = 16 * (j // N_DMA_SEM + 1)
                ins.then_inc(sem, 16)
                semval[i] = (sem, v)
            elif signal[i]:
                cnt[e] += 1
                ins.then_inc(comp_sem[e], 1)
                semval[i] = (comp_sem[e], cnt[e])
        for e, sems in dma_sems.items():
            for j, s in enumerate(sems):
                tot = dcnt[e]
                k = (tot - j + N_DMA_SEM - 1) // N_DMA_SEM if tot > j else 0
                if k > 0:
                    nc.sync.wait_ge(s, 16 * k)
        return dict(n_ops=n, cnt=cnt, dcnt=dcnt)
TOK = 4096
TT = 512
NT = TOK // TT
D = 1024
KC = D // 128
INC = 3584


class Ctx:
    def __init__(self, same_engine_sync=True):
        self.nc = bass.Bass("TRN2", target_bir_lowering=False)
        self.st = ExitStack()
        self.S = Sched(self.nc, same_engine_sync=same_engine_sync)
        self.nps = 0

    def sb(self, name, shape, dt):
        return self.st.enter_context(self.nc.sbuf_tensor(name, list(shape), dt))

    def ps(self, name, shape=(128, 512), dt=F32):
        return self.st.enter_context(self.nc.psum_tensor(name, list(shape), dt))

    def din(self, name, shape, dt):
        return self.nc.dram_tensor(name, list(shape), dt, kind="ExternalInput").ap()

    def dout(self, name, shape, dt):
        return self.nc.dram_tensor(name, list(shape), dt, kind="ExternalOutput").ap()

    def finish(self):
        info = self.S.emit(self.st)
        self.st.close()
        return self.nc, info


def emit_mod(C, wada, bada_t, ct, col0, nchunk, modT, tagp, wtiles):
    nc, S = C.nc, C.S
    csb = C.sb(tagp + "c", [128, KC], F32)
    csl = C.sb(tagp + "cs", [128, KC], F32)
    bsb = C.sb(tagp + "b", [128, 48], F32)
    S.dma("sp", csb[:], ct, writes=[tagp + "c"])
    S.dma("sp", bsb[:], bada_t, writes=[tagp + "b"])
    S.op("act", lambda: nc.scalar.activation(out=csl[:], in_=csb[:], func=AF.Silu), reads=[tagp + "c"], writes=[tagp + "cs"])
    WB = 256
    pm = C.ps(tagp + "pm", [128, 64], F32)
    wv = wada.rearrange("(k p) n -> p k n", p=128)
    ngrp = (nchunk * 128) // WB
    for g in range(ngrp):
        tt_, key = wtiles[g % len(wtiles)]
        t = tt_[:, :, 0:WB]
        S.dma("sp", t, wv[:, :, col0 + g * WB: col0 + (g + 1) * WB], writes=[key])
        for jj in range(WB // 128):
            j = g * (WB // 128) + jj
            for k in range(KC):
                S.op("pe", (lambda t=t, jj=jj, j=j, k=k: nc.tensor.matmul(
                    pm[:, j:j + 1], t[:, k, jj * 128:(jj + 1) * 128], csl[:, k:k + 1],
                    start=(k == 0), stop=(k == KC - 1))),
                    reads=[key, tagp + "cs"], writes=[(tagp + "pm", j)])
    jb = col0 // 128
    S.op("dve", lambda: nc.vector.tensor_tensor(out=modT[:, 0:nchunk], in0=pm[:, 0:nchunk], in1=bsb[:, jb:jb + nchunk], op=ALU.add),
         reads=[(tagp + "pm", j) for j in range(nchunk)] + [tagp + "b"], writes=[tagp + "mod"])


def emit_rstd(C, S, nc, ps_stat, rstd, key_ps, key_rstd, ncols, dim, tmp):
    S.op("act", lambda: nc.scalar.activation(out=tmp[:, :ncols], in_=ps_stat[:, :ncols], func=AF.Sqrt, scale=1.0 / dim, bias=C.eps_t[:, 0:1]),
         reads=[key_ps], writes=[key_rstd + "_t"])
    S.op("dve", lambda: nc.vector.reciprocal(out=rstd[:, :ncols], in_=tmp[:, :ncols]), reads=[key_rstd + "_t"], writes=[key_rstd])


def build_A(layer):
    C = Ctx()
    nc, S = C.nc, C.S
    xT = C.din("xT", [D, TOK], F32)
    ct = C.din("ct", [128, KC], F32)
    wada = C.din("wada", [D, 6144], F32)
    bada = C.din("bada", [128, 48], F32)
    n1w = C.din("n1w", [128, KC], F32)
    win = C.din("win", [D, INC], F32)
    lbg = C.din("lbg", [128, 2 * 768], F32)
    o_aT = C.dout("o_aT", [256, TOK], BF16)
    o_qkT = C.dout("o_qkT", [768, TOK], BF16)
    o_gT = C.dout("o_gT", [384, TOK], F32)
    o_v = C.dout("o_v", [TOK, 384], BF16)
    o_qr = C.dout("o_qr", [TOK, 384], F32)
    o_lf = C.dout("o_lf", [TOK, 768], F32)
    o_kk = C.dout("o_kk", [TOK, 768], F32)
    o_ir = C.dout("o_ir", [TOK, 384], BF16)

    C.eps_t = C.sb("eps", [128, 1], F32)
    S.op("dve", lambda: nc.vector.memset(C.eps_t[:], EPS), writes=["eps"])
    ones = C.sb("ones", [128, 128], BF16)
    S.op("dve", lambda: nc.vector.memset(ones[:], 1.0), writes=["ones"])

    wsb = C.sb("wsb", [128, KC, INC], BF16)
    wv = win.rearrange("(k p) n -> p k n", p=128)
    for k in range(KC):
        S.dma("pool", wsb[:, k, :], wv[:, k, :], writes=[("wsb", k)])
    WKEYS = [("wsb", k) for k in range(KC)]

    modT = C.sb("modT", [128, 16], F32)
    xt = [C.sb("xt%d" % i, [128, KC, TT], F32) for i in range(2)]
    emit_mod(C, wada, bada, ct, 0, 16, modT, "m_", [(xt[0], "xt0"), (xt[1], "xt1")])
    nw = C.sb("nw", [128, KC], F32)
    S.dma("sp", nw[:], n1w, writes=["nw"])
    scl = C.sb("scl", [128, KC], F32)
    S.op("dve", lambda: nc.vector.scalar_tensor_tensor(out=scl[:], in0=modT[:, 8:16], scalar=1.0, in1=nw[:], op0=ALU.add, op1=ALU.mult),
         reads=["m_mod", "nw"], writes=["scl"])

    lbt = C.sb("lbt", [128, 768], F32)
    oml = C.sb("oml", [128, 768], F32)
    if layer == 0:
        S.op("dve", lambda: nc.vector.memset(lbt[:], 0.0), writes=["lbt"])
        S.op("dve", lambda: nc.vector.memset(oml[:], 1.0), writes=["oml"])
    else:
        lg = C.sb("lg", [128, 2 * 768], F32)
        S.dma("sp", lg[:], lbg, writes=["lg"])
        ex = C.sb("lbex", [128, 2 * 768], F32)
        S.op("act", lambda: nc.scalar.activation(out=ex[:], in_=lg[:], func=AF.Exp), reads=["lg"], writes=["lbex"])
        sm = C.sb("lbsm", [128, 768], F32)
        S.op("dve", lambda: nc.vector.tensor_tensor(out=sm[:], in0=ex[:, 0:768], in1=ex[:, 768:1536], op=ALU.add), reads=["lbex"], writes=["lbsm"])
        S.op("dve", lambda: nc.vector.reciprocal(out=sm[:], in_=sm[:]), reads=["lbsm"], writes=["lbsm"])
        S.op("dve", lambda: nc.vector.tensor_tensor(out=lbt[:], in0=ex[:, 768:1536], in1=sm[:], op=ALU.mult), reads=["lbex", "lbsm"], writes=["lbt"])
        S.op("dve", lambda: nc.vector.tensor_scalar(out=oml[:], in0=lbt[:], scalar1=-1.0, scalar2=1.0, op0=ALU.mult, op1=ALU.add), reads=["lbt"], writes=["oml"])

    hs = [C.sb("hs%d" % i, [128, KC, TT], BF16) for i in range(2)]
    sq = C.sb("sq", [128, KC, TT], BF16)
    rstd = C.sb("rstd", [128, TT], F32)
    rtmp = C.sb("rtmp", [128, TT], F32)
    ps_stat = C.ps("ps_stat")
    psf = [C.ps("psf%d" % i) for i in range(3)]
    pst = [C.ps("pst%d" % i) for i in range(3)]
    NST = 3
    stf_b = [C.sb("stfb%d" % i, [128, TT], BF16) for i in range(NST)]
    stf_f = [C.sb("stff%d" % i, [128, TT], F32) for i in range(NST)]
    sg = [C.sb("sg%d" % i, [128, TT], F32) for i in range(2)]
    st_v = [C.sb("stv%d" % i, [128, 384], BF16) for i in range(2)]
    st_qr = [C.sb("stqr%d" % i, [128, 384], F32) for i in range(2)]
    st_s = [C.sb("sts%d" % i, [128, 768], F32) for i in range(2)]
    st_lf = [C.sb("stlf%d" % i, [128, 768], F32) for i in range(2)]
    st_kk = [C.sb("stkk%d" % i, [128, 768], F32) for i in range(2)]
    st_ir = [C.sb("stir%d" % i, [128, 384], BF16) for i in range(2)]
    xv = xT.rearrange("(k p) t -> p k t", p=128)
    cnt = {"f": 0, "t": 0, "st": 0, "tm": 0}

    def prep(i):
        b = i % 2
        x = xt[b]
        S.dma("sp", x[:], xv[:, :, i * TT:(i + 1) * TT], writes=["xt%d" % b])
        S.op("act", lambda: nc.scalar.activation(out=sq[:], in_=x[:], func=AF.Square), reads=["xt%d" % b], writes=["sq"])
        for k in range(KC):
            S.op("pe", (lambda k=k: nc.tensor.matmul(ps_stat[:], ones[:], sq[:, k, :], start=(k == 0), stop=(k == KC - 1))),
                 reads=["sq", "ones"], writes=["ps_stat"])
        emit_rstd(C, S, nc, ps_stat, rstd, "ps_stat", "rstd", TT, D, rtmp)
        for k in range(KC):
            S.op("dve", (lambda k=k: nc.vector.scalar_tensor_tensor(out=x[:, k, :], in0=x[:, k, :], scalar=scl[:, k:k + 1], in1=rstd[:],
                                                                    op0=ALU.mult, op1=ALU.mult)),
                 reads=["xt%d" % b, "scl", "rstd"], writes=["xt%d" % b])
        for k in range(KC):
            S.op("act", (lambda k=k: nc.scalar.activation(out=hs[b][:, k, :], in_=x[:, k, :], func=AF.Identity, bias=modT[:, k:k + 1])),
                 reads=["xt%d" % b, "m_mod"], writes=["hs%d" % b])

    def fm_group(i, colchunk):
        b = i % 2
        p = cnt["f"] % 3
        cnt["f"] += 1
        pt = psf[p]
        for k in range(KC):
            S.op("pe", (lambda k=k: nc.tensor.matmul(pt[:], wsb[:, k, colchunk * 128:(colchunk + 1) * 128], hs[b][:, k, :],
                                                     start=(k == 0), stop=(k == KC - 1))),
                 reads=WKEYS + ["hs%d" % b], writes=["psf%d" % p])
        return pt, "psf%d" % p

    def main(i):
        b = i % 2
        t0 = i * TT
        for c in range(2):
            pg, kg = fm_group(i, 2 + c)
            s = cnt["st"] % 2
            cnt["st"] += 1
            S.op("act", (lambda pg=pg, s=s: nc.scalar.activation(out=sg[s][:], in_=pg[:], func=AF.Sigmoid)), reads=[kg], writes=["sg%d" % s])
            pv, kv = fm_group(i, c)
            q = cnt["tm"] % NST
            cnt["tm"] += 1
            S.op("dve", (lambda pv=pv, s=s, q=q: nc.vector.tensor_tensor(out=stf_b[q][:], in0=pv[:], in1=sg[s][:], op=ALU.mult)),
                 reads=[kv, "sg%d" % s], writes=["stfb%d" % q])
            S.dma("pool", o_aT[c * 128:(c + 1) * 128, t0:t0 + TT], stf_b[q][:], reads=["stfb%d" % q])
        for c in range(6):
            pq, kq = fm_group(i, 4 + c)
            q = cnt["tm"] % NST
            cnt["tm"] += 1
            S.op("act", (lambda pq=pq, q=q: nc.scalar.copy(out=stf_b[q][:], in_=pq[:])), reads=[kq], writes=["stfb%d" % q])
            S.dma("pool", o_qkT[c * 128:(c + 1) * 128, t0:t0 + TT], stf_b[q][:], reads=["stfb%d" % q])
        for c in range(3):
            pq, kq = fm_group(i, 25 + c)
            q = cnt["tm"] % NST
            cnt["tm"] += 1
            S.op("act", (lambda pq=pq, q=q: nc.scalar.activation(out=stf_f[q][:], in_=pq[:], func=AF.Silu)), reads=[kq], writes=["stff%d" % q])
            S.dma("pool", o_gT[c * 128:(c + 1) * 128, t0:t0 + TT], stf_f[q][:], reads=["stff%d" % q])
        def sub(su):
            r0 = t0 + su * 128
            sbi = cnt["t"] % 2
            cnt["t"] += 1

            def tm_group(col0, ncol):
                p = cnt["tm"] % 3
                cnt["tm"] += 1
                pt = pst[p]
                for k in range(KC):
                    S.op("pe", (lambda k=k, pt=pt: nc.tensor.matmul(pt[:, :ncol], hs[b][:, k, su * 128:(su + 1) * 128], wsb[:, k, col0:col0 + ncol],
                                                                    start=(k == 0), stop=(k == KC - 1))),
                         reads=WKEYS + ["hs%d" % b], writes=["pst%d" % p])
                return pt, "pst%d" % p
            pt, kp = tm_group(1280, 384)
            S.op("act", (lambda pt=pt: nc.scalar.copy(out=st_v[sbi][:], in_=pt[:, :384])), reads=[kp], writes=["stv%d" % sbi])
            S.dma("pool", o_v[r0:r0 + 128, :], st_v[sbi][:], reads=["stv%d" % sbi])
            pt, kp = tm_group(1664, 384)
            S.op("act", (lambda pt=pt: nc.scalar.activation(out=st_qr[sbi][:], in_=pt[:, :384], func=AF.Silu)), reads=[kp], writes=["stqr%d" % sbi])
            S.dma("pool", o_qr[r0:r0 + 128, :], st_qr[sbi][:], reads=["stqr%d" % sbi])
            for dd in range(2):
                pt, kp = tm_group(2048 + dd * 384, 384)
                sl = slice(dd * 384, (dd + 1) * 384)
                S.op("act", (lambda pt=pt, sl=sl: nc.scalar.activation(out=st_s[sbi][:, sl], in_=pt[:, :384], func=AF.Sigmoid)),
                     reads=[kp], writes=[("sts%d" % sbi, dd)])
            ks = "sts%d" % sbi
            S.op("dve", lambda: nc.vector.tensor_tensor(out=st_s[sbi][:], in0=st_s[sbi][:], in1=oml[:], op=ALU.mult),
                 reads=[(ks, 0), (ks, 1), "oml"], writes=[(ks, 0), (ks, 1)])
            S.op("dve", lambda: nc.vector.scalar_tensor_tensor(out=st_s[sbi][:], in0=st_s[sbi][:], scalar=F_TINY, in1=lbt[:], op0=ALU.max, op1=ALU.add),
                 reads=[(ks, 0), (ks, 1), "lbt"], writes=[(ks, 0), (ks, 1)])
            S.op("act", lambda: nc.scalar.activation(out=st_lf[sbi][:], in_=st_s[sbi][:], func=AF.Ln), reads=[(ks, 0), (ks, 1)], writes=["stlf%d" % sbi])
            S.op("dve", lambda: nc.vector.tensor_scalar(out=st_kk[sbi][:], in0=st_s[sbi][:], scalar1=-1.0, scalar2=1.0, op0=ALU.mult, op1=ALU.add),
                 reads=[(ks, 0), (ks, 1)], writes=["stkk%d" % sbi])
            S.dma("pool", o_lf[r0:r0 + 128, :], st_lf[sbi][:], reads=["stlf%d" % sbi])
            S.dma("pool", o_kk[r0:r0 + 128, :], st_kk[sbi][:], reads=["stkk%d" % sbi])
            pt, kp = tm_group(2816, 384)
            S.op("act", (lambda pt=pt: nc.scalar.copy(out=st_ir[sbi][:], in_=pt[:, :384])), reads=[kp], writes=["stir%d" % sbi])
            S.dma("pool", o_ir[r0:r0 + 128, :], st_ir[sbi][:], reads=["stir%d" % sbi])

        for su in range(TT // 128):
            sub(su)

    prep(0)
    for i in range(NT):
        if i + 1 < NT:
            prep(i + 1)
        main(i)
    return C.finish()
NPAD = TOK + 2
CT = 256
CSTEP = CT - 2
NCT = (TOK + CSTEP - 1) // CSTEP
DFF = 2816
NJ = DFF // 128


def build_CD(final):
    C = Ctx()
    nc, S = C.nc, C.S
    xT = C.din("xT", [D, NPAD], F32)
    acT = C.din("acT", [256, NPAD], BF16)
    atT = C.din("atT", [384, NPAD], BF16)
    ofT = C.din("ofT", [384, NPAD], F32)
    obT = C.din("obT", [384, NPAD], F32)
    gT = C.din("gT", [384, NPAD], F32)
    rnw = C.din("rnw", [128, 3], F32)
    ct = C.din("ct", [128, KC], F32)
    wada = C.din("wada", [D, 6144], F32)
    bada = C.din("bada", [128, 48], F32)
    n2w = C.din("n2w", [128, KC], F32)
    fnw = C.din("fnw", [128, KC], F32)
    wout = C.din("wout", [D, D], F32)
    wup = C.din("wup", [D, 2 * DFF], F32)
    wdown = C.din("wdown", [DFF, D], F32)
    cfw = C.din("cfw", [128, 44, 3], F32)
    flags = C.din("flags", [128, 2], F32)
    bdm = C.din("bdm", [128, 128], F32)
    o_xT = C.dout("o_xT", [D, TOK], F32)

    C.eps_t = C.sb("eps", [128, 1], F32)
    S.op("dve", lambda: nc.vector.memset(C.eps_t[:], EPS), writes=["eps"])
    ones = C.sb("ones", [128, 128], BF16)
    S.op("dve", lambda: nc.vector.memset(ones[:], 1.0), writes=["ones"])
    bd = C.sb("bd", [128, 128], BF16)
    S.dma("pool", bd[:], bdm, writes=["bd"])

    xt = C.sb("xt", [128, KC, CT], F32)
    modT = C.sb("modT", [128, 32], F32)
    emit_mod(C, wada, bada, ct, 2048, 32, modT, "m_", [(xt, "xt")])
    nw = C.sb("nw", [128, KC], F32)
    S.dma("sp", nw[:], n2w, writes=["nw"])
    scl = C.sb("scl", [128, KC], F32)
    S.op("dve", lambda: nc.vector.scalar_tensor_tensor(out=scl[:], in0=modT[:, 16:24], scalar=1.0, in1=nw[:], op0=ALU.add, op1=ALU.mult),
         reads=["m_mod", "nw"], writes=["scl"])
    fw = C.sb("fw", [128, KC], F32)
    S.dma("sp", fw[:], fnw, writes=["fw"])
    rw = C.sb("rw", [128, 3], F32)
    S.dma("sp", rw[:], rnw, writes=["rw"])
    cw = C.sb("cw", [128, 44, 3], F32)
    S.dma("sp", cw[:], cfw, writes=["cw"])
    fl = C.sb("fl", [128, 2], F32)
    S.dma("sp", fl[:], flags, writes=["fl"])

    wo = C.sb("wo", [128, KC, D], BF16)
    wu = C.sb("wu", [128, KC, 2 * DFF], BF16)
    wd = C.sb("wd", [128, NJ, D], BF16)
    wov = wout.rearrange("(k p) n -> p k n", p=128)
    wuv = wup.rearrange("(k p) n -> p k n", p=128)
    wdv = wdown.rearrange("(k p) n -> p k n", p=128)
    for k in range(KC):
        S.dma("pool", wo[:, k, :], wov[:, k, :], writes=[("wo", k)])
    for k in range(KC):
        S.dma("pool", wu[:, k, :], wuv[:, k, :], writes=[("wu", k)])
    for k in range(NJ):
        S.dma("pool", wd[:, k, :], wdv[:, k, :], writes=[("wd", k)])
    WO = [("wo", k) for k in range(KC)]
    WU = [("wu", k) for k in range(KC)]
    WD = [("wd", k) for k in range(NJ)]

    mx = C.sb("mx", [128, KC, CT], BF16)
    oft = C.sb("oft", [128, 3, CT], F32)
    obt = C.sb("obt", [128, 3, CT], F32)
    gt = C.sb("gt", [128, 3, CT], F32)
    h2 = C.sb("h2", [128, KC, CT], BF16)
    sq = C.sb("sq", [128, KC, CT], BF16)
    gv = C.sb("gv", [128, NJ, CT], BF16)
    rstd = C.sb("rstd", [128, CT], F32)
    rtmp = C.sb("rtmp", [128, CT], F32)
    tmpf = [C.sb("tmpf%d" % i, [128, CT], F32) for i in range(2)]
    ug = [C.sb("ug%d" % i, [128, CT], F32) for i in range(4)]
    c1 = [C.sb("c1%d" % i, [128, CT], F32) for i in range(4)]
    ost = [C.sb("ost%d" % i, [128, CT], F32) for i in range(2)]
    ps_stat = C.ps("ps_stat")
    psr = C.ps("psr")
    psm = [C.ps("psm%d" % i) for i in range(5)]
    cnt = {"p": 0, "u": 0, "g": 0, "t": 0, "o": 0}

    xv = xT.rearrange("(k p) t -> p k t", p=128)
    acv = acT.rearrange("(k p) t -> p k t", p=128)
    atv = atT.rearrange("(k p) t -> p k t", p=128)
    ofv = ofT.rearrange("(k p) t -> p k t", p=128)
    obv = obT.rearrange("(k p) t -> p k t", p=128)
    gv_ = gT.rearrange("(k p) t -> p k t", p=128)

    def newps():
        p = cnt["p"] % 5
        cnt["p"] += 1
        return psm[p], "psm%d" % p

    def do_tile(ti):
        c0 = ti * CSTEP
        w = min(CT, NPAD - c0)
        nout = w - 2
        cs = slice(c0, c0 + w)
        S.dma("sp", xt[:, :, :w], xv[:, :, cs], writes=["xt"])
        S.dma("sp", mx[:, 0:2, :w], acv[:, :, cs], writes=[("mx", 0)])
        S.dma("sp", mx[:, 2:5, :w], atv[:, :, cs], writes=[("mx", 1)])
        S.dma("sp", oft[:, :, :w], ofv[:, :, cs], writes=["oft"])
        S.dma("sp", obt[:, :, :w], obv[:, :, cs], writes=["obt"])
        S.dma("sp", gt[:, :, :w], gv_[:, :, cs], writes=["gt"])
        S.op("dve", lambda: nc.vector.tensor_tensor(out=oft[:, :, :w], in0=oft[:, :, :w], in1=obt[:, :, :w], op=ALU.add),
             reads=["oft", "obt"], writes=["oft"])
        S.op("act", lambda: nc.scalar.activation(out=sq[:, 0:3, :w], in_=oft[:, :, :w], func=AF.Square), reads=["oft"], writes=["sq"])
        for c in range(3):
            S.op("pe", (lambda c=c: nc.tensor.matmul(psr[:, :w], bd[:], sq[:, c, :w], start=True, stop=True)), reads=["sq", "bd"], writes=["psr"])
            emit_rstd(C, S, nc, psr, rstd, "psr", "rstd", w, 64, rtmp)
            S.op("dve", (lambda c=c: nc.vector.scalar_tensor_tensor(out=oft[:, c, :w], in0=oft[:, c, :w], scalar=rw[:, c:c + 1], in1=rstd[:, :w],
                                                                    op0=ALU.mult, op1=ALU.mult)),
                 reads=["oft", "rw", "rstd"], writes=["oft"])
            S.op("dve", (lambda c=c: nc.vector.tensor_tensor(out=mx[:, 5 + c, :w], in0=oft[:, c, :w], in1=gt[:, c, :w], op=ALU.mult)),
                 reads=["oft", "gt"], writes=[("mx", 2)])
        MX = [("mx", 0), ("mx", 1), ("mx", 2)]
        for oc in range(KC):
            pt, kp = newps()
            for k in range(KC):
                S.op("pe", (lambda k=k, pt=pt, oc=oc: nc.tensor.matmul(pt[:, :w], wo[:, k, oc * 128:(oc + 1) * 128], mx[:, k, :w],
                                                                      start=(k == 0), stop=(k == KC - 1))),
                     reads=WO + MX, writes=[kp])
            S.op("dve", (lambda pt=pt, oc=oc: nc.vector.scalar_tensor_tensor(out=xt[:, oc, :w], in0=pt[:, :w], scalar=modT[:, oc:oc + 1], in1=xt[:, oc, :w],
                                                                            op0=ALU.mult, op1=ALU.add)),
                 reads=[kp, "m_mod", "xt"], writes=["xt"])
        S.op("act", lambda: nc.scalar.activation(out=sq[:, :, :w], in_=xt[:, :, :w], func=AF.Square), reads=["xt"], writes=["sq"])
        for k in range(KC):
            S.op("pe", (lambda k=k: nc.tensor.matmul(ps_stat[:, :w], ones[:], sq[:, k, :w], start=(k == 0), stop=(k == KC - 1))),
                 reads=["sq", "ones"], writes=["ps_stat"])
        emit_rstd(C, S, nc, ps_stat, rstd, "ps_stat", "rstd", w, D, rtmp)
        for k in range(KC):
            q = cnt["t"] % 2
            cnt["t"] += 1
            S.op("dve", (lambda k=k, q=q: nc.vector.scalar_tensor_tensor(out=tmpf[q][:, :w], in0=xt[:, k, :w], scalar=scl[:, k:k + 1], in1=rstd[:, :w],
                                                                        op0=ALU.mult, op1=ALU.mult)),
                 reads=["xt", "scl", "rstd"], writes=["tmpf%d" % q])
            S.op("act", (lambda k=k, q=q: nc.scalar.activation(out=h2[:, k, :w], in_=tmpf[q][:, :w], func=AF.Identity, bias=modT[:, 8 + k:9 + k])),
                 reads=["tmpf%d" % q, "m_mod"], writes=["h2"])
        for j in range(NJ):
            res = []
            for part in range(2):
                col = j + part * NJ
                pt, kp = newps()
                for k in range(KC):
                    S.op("pe", (lambda k=k, pt=pt, col=col: nc.tensor.matmul(pt[:, :w], wu[:, k, col * 128:(col + 1) * 128], h2[:, k, :w],
                                                                            start=(k == 0), stop=(k == KC - 1))),
                         reads=WU + ["h2"], writes=[kp])
                u = cnt["u"] % 4
                cnt["u"] += 1
                S.op("act", (lambda pt=pt, u=u: nc.scalar.copy(out=ug[u][:, :w], in_=pt[:, :w])), reads=[kp], writes=["ug%d" % u])
                S.op("act", (lambda pt=pt, u=u, col=col: nc.scalar.activation(out=c1[u][:, :w], in_=pt[:, :w], func=AF.Copy, scale=cw[:, col, 1:2])),
                     reads=[kp, "cw"], writes=["c1%d" % u])
                if ti == 0:
                    S.op("dve", (lambda u=u: nc.vector.tensor_scalar(out=ug[u][:, 0:1], in0=ug[u][:, 0:1], scalar1=fl[:, 0:1], scalar2=None, op0=ALU.mult)),
                         reads=["ug%d" % u, "fl"], writes=["ug%d" % u])
                if ti == NCT - 1:
                    S.op("dve", (lambda u=u: nc.vector.tensor_scalar(out=ug[u][:, w - 1:w], in0=ug[u][:, w - 1:w], scalar1=fl[:, 1:2], scalar2=None, op0=ALU.mult)),
                         reads=["ug%d" % u, "fl"], writes=["ug%d" % u])
                S.op("dve", (lambda u=u, col=col: nc.vector.scalar_tensor_tensor(out=c1[u][:, 1:w - 1], in0=ug[u][:, 0:w - 2], scalar=cw[:, col, 0:1],
                                                                                in1=c1[u][:, 1:w - 1], op0=ALU.mult, op1=ALU.add)),
                     reads=["ug%d" % u, "c1%d" % u, "cw"], writes=["c1%d" % u])
                S.op("dve", (lambda u=u, col=col: nc.vector.scalar_tensor_tensor(out=c1[u][:, 1:w - 1], in0=ug[u][:, 2:w], scalar=cw[:, col, 2:3],
                                                                                in1=c1[u][:, 1:w - 1], op0=ALU.mult, op1=ALU.add)),
                     reads=["ug%d" % u, "c1%d" % u, "cw"], writes=["c1%d" % u])
                res.append(u)
            ugate, uval = res
            g = cnt["g"] % 2
            cnt["g"] += 1
            S.op("act", (lambda ugate=ugate: nc.scalar.activation(out=c1[ugate][:, 1:w - 1], in_=c1[ugate][:, 1:w - 1], func=AF.Gelu)),
                 reads=["c1%d" % ugate], writes=["c1%d" % ugate])
            S.op("dve", (lambda uval=uval, ugate=ugate, j=j: nc.vector.tensor_tensor(out=gv[:, j, 1:w - 1], in0=c1[ugate][:, 1:w - 1], in1=c1[uval][:, 1:w - 1], op=ALU.mult)),
                 reads=["c1%d" % ugate, "c1%d" % uval], writes=[("gv", j)])
        GV = [("gv", j) for j in range(NJ)]
        for oc in range(KC):
            pt, kp = newps()
            for j in range(NJ):
                S.op("pe", (lambda j=j, pt=pt, oc=oc: nc.tensor.matmul(pt[:, 1:w - 1], wd[:, j, oc * 128:(oc + 1) * 128], gv[:, j, 1:w - 1],
                                                                      start=(j == 0), stop=(j == NJ - 1))),
                     reads=WD + GV, writes=[kp])
            if not final:
                o = cnt["o"] % 2
                cnt["o"] += 1
                S.op("dve", (lambda pt=pt, oc=oc, o=o: nc.vector.scalar_tensor_tensor(out=ost[o][:, 1:w - 1], in0=pt[:, 1:w - 1], scalar=modT[:, 24 + oc:25 + oc],
                                                                                     in1=xt[:, oc, 1:w - 1], op0=ALU.mult, op1=ALU.add)),
                     reads=[kp, "m_mod", "xt"], writes=["ost%d" % o])
                S.dma("pool", o_xT[oc * 128:(oc + 1) * 128, c0:c0 + nout], ost[o][:, 1:w - 1], reads=["ost%d" % o])
            else:
                S.op("dve", (lambda pt=pt, oc=oc: nc.vector.scalar_tensor_tensor(out=xt[:, oc, 1:w - 1], in0=pt[:, 1:w - 1], scalar=modT[:, 24 + oc:25 + oc],
                                                                                in1=xt[:, oc, 1:w - 1], op0=ALU.mult, op1=ALU.add)),
                     reads=[kp, "m_mod", "xt"], writes=["xt"])
        if final:
            S.op("act", lambda: nc.scalar.activation(out=sq[:, :, 1:w - 1], in_=xt[:, :, 1:w - 1], func=AF.Square), reads=["xt"], writes=["sq"])
            for k in range(KC):
                S.op("pe", (lambda k=k: nc.tensor.matmul(ps_stat[:, 1:w - 1], ones[:], sq[:, k, 1:w - 1], start=(k == 0), stop=(k == KC - 1))),
                     reads=["sq", "ones"], writes=["ps_stat"])
            emit_rstd(C, S, nc, ps_stat, rstd, "ps_stat", "rstd", w, D, rtmp)
            for k in range(KC):
                o = cnt["o"] % 2
                cnt["o"] += 1
                S.op("dve", (lambda k=k, o=o: nc.vector.scalar_tensor_tensor(out=ost[o][:, 1:w - 1], in0=xt[:, k, 1:w - 1], scalar=fw[:, k:k + 1], in1=rstd[:, 1:w - 1],
                                                                            op0=ALU.mult, op1=ALU.mult)),
                     reads=["xt", "fw", "rstd"], writes=["ost%d" % o])
                S.dma("pool", o_xT[k * 128:(k + 1) * 128, c0:c0 + nout], ost[o][:, 1:w - 1], reads=["ost%d" % o])

    for ti in range(NCT):
        do_tile(ti)
    return C.finish()
CK = 31
CH = CK // 2


def build_Bc():
    C = Ctx()
    nc, S = C.nc, C.S
    aT = C.din("aT", [256, TOK + 2 * CH], BF16)
    cwd = C.din("cw_d", [128, 2, CK], F32)
    cbd = C.din("cb_d", [128, 2], F32)
    lwd = C.din("lw_d", [128, 2], F32)
    lbd = C.din("lb_d", [128, 2], F32)
    idd = C.din("ident_d", [128, 128], F32)
    o_ac = C.dout("o_ac", [256, TOK], BF16)

    C.eps_t = C.sb("eps", [128, 1], F32)
    S.op("dve", lambda: nc.vector.memset(C.eps_t[:], EPS), writes=["eps"])
    onesf = C.sb("onesf", [128, 128], F32)
    S.op("dve", lambda: nc.vector.memset(onesf[:], 1.0), writes=["onesf"])
    idt = C.sb("idt", [128, 128], F32)
    S.dma("sp", idt[:], idd, writes=["idt"])
    cw = C.sb("cw", [128, 2, CK], F32)
    S.dma("sp", cw[:], cwd, writes=["cw"])
    cb = C.sb("cb", [128, 2], F32)
    S.dma("sp", cb[:], cbd, writes=["cb"])
    lw = C.sb("lw", [128, 2], F32)
    S.dma("sp", lw[:], lwd, writes=["lw"])
    lb = C.sb("lb", [128, 2], F32)
    S.dma("sp", lb[:], lbd, writes=["lb"])
    dg = C.sb("dg", [128, 2, CK, 128], BF16)
    for c in range(2):
        for k in range(CK):
            S.op("dve", (lambda c=c, k=k: nc.vector.tensor_scalar(out=dg[:, c, k, :], in0=idt[:], scalar1=cw[:, c, k:k + 1], scalar2=None, op0=ALU.mult)),
                 reads=["idt", "cw"], writes=["dg"])
    at = [C.sb("at%d" % i, [128, 2, TT + 2 * CH], BF16) for i in range(2)]
    y = C.sb("y", [128, 2, TT], F32)
    ysq = C.sb("ysq", [128, 2, TT], F32)
    mean = C.sb("mean", [128, TT], F32)
    msq = C.sb("msq", [128, TT], F32)
    var = C.sb("var", [128, TT], F32)
    rstd = C.sb("rstd", [128, TT], F32)
    tmp = C.sb("tmp", [128, TT], F32)
    ob = [C.sb("ob%d" % i, [128, TT], BF16) for i in range(2)]
    pc = [C.ps("pc%d" % i) for i in range(2)]
    ps1 = C.ps("ps1")
    ps2 = C.ps("ps2")
    av = aT.rearrange("(c p) t -> p c t", p=128)
    cnt = {"o": 0}

    def load(i):
        b = i % 2
        S.dma("sp", at[b][:], av[:, :, i * TT:i * TT + TT + 2 * CH], writes=["at%d" % b])

    def do_tile(i):
        b = i % 2
        for c in range(2):
            for k in range(CK):
                S.op("pe", (lambda c=c, k=k: nc.tensor.matmul(pc[c][:], dg[:, c, k, :], at[b][:, c, k:k + TT], start=(k == 0), stop=(k == CK - 1))),
                     reads=["dg", "at%d" % b], writes=["pc%d" % c])
            S.op("act", (lambda c=c: nc.scalar.activation(out=y[:, c, :], in_=pc[c][:], func=AF.Identity, bias=cb[:, c:c + 1])),
                 reads=["pc%d" % c, "cb"], writes=[("y", c)])
            S.op("act", (lambda c=c: nc.scalar.activation(out=ysq[:, c, :], in_=y[:, c, :], func=AF.Square)), reads=[("y", c)], writes=[("ysq", c)])
        for c in range(2):
            S.op("pe", (lambda c=c: nc.tensor.matmul(ps1[:], onesf[:], y[:, c, :], start=(c == 0), stop=(c == 1))), reads=[("y", c), "onesf"], writes=["ps1"])
        for c in range(2):
            S.op("pe", (lambda c=c: nc.tensor.matmul(ps2[:], onesf[:], ysq[:, c, :], start=(c == 0), stop=(c == 1))), reads=[("ysq", c), "onesf"], writes=["ps2"])
        S.op("dve", lambda: nc.vector.tensor_scalar(out=mean[:], in0=ps1[:], scalar1=1.0 / 256, scalar2=None, op0=ALU.mult), reads=["ps1"], writes=["mean"])
        S.op("dve", lambda: nc.vector.tensor_tensor(out=msq[:], in0=mean[:], in1=mean[:], op=ALU.mult), reads=["mean"], writes=["msq"])
        S.op("dve", lambda: nc.vector.scalar_tensor_tensor(out=var[:], in0=ps2[:], scalar=1.0 / 256, in1=msq[:], op0=ALU.mult, op1=ALU.subtract),
             reads=["ps2", "msq"], writes=["var"])
        S.op("act", lambda: nc.scalar.activation(out=tmp[:], in_=var[:], func=AF.Sqrt, bias=C.eps_t[:, 0:1]), reads=["var", "eps"], writes=["tmp"])
        S.op("dve", lambda: nc.vector.reciprocal(out=rstd[:], in_=tmp[:]), reads=["tmp"], writes=["rstd"])
        for c in range(2):
            S.op("dve", (lambda c=c: nc.vector.tensor_tensor(out=y[:, c, :], in0=y[:, c, :], in1=mean[:], op=ALU.subtract)),
                 reads=[("y", c), "mean"], writes=[("y", c)])
            S.op("dve", (lambda c=c: nc.vector.scalar_tensor_tensor(out=y[:, c, :], in0=y[:, c, :], scalar=lw[:, c:c + 1], in1=rstd[:], op0=ALU.mult, op1=ALU.mult)),
                 reads=[("y", c), "lw", "rstd"], writes=[("y", c)])
            o = cnt["o"] % 2
            cnt["o"] += 1
            S.op("act", (lambda c=c, o=o: nc.scalar.activation(out=ob[o][:], in_=y[:, c, :], func=AF.Silu, bias=lb[:, c:c + 1])),
                 reads=[("y", c), "lb"], writes=["ob%d" % o])
            S.dma("pool", o_ac[c * 128:(c + 1) * 128, i * TT:(i + 1) * TT], ob[o][:], reads=["ob%d" % o])

    load(0)
    for i in range(NT):
        if i + 1 < NT:
            load(i + 1)
        do_tile(i)
    return C.finish()
SEQ = 8192
DILS = (1, 4, 16)
NU = 3


def build_Ba():
    C = Ctx()
    nc, S = C.nc, C.S
    qd, kd, vd = {}, {}, {}
    for r in DILS:
        L = SEQ // r
        qd[r] = C.din("q%d" % r, [NU, 64, r * L], BF16)
        kd[r] = C.din("k%d" % r, [NU, 64, r * (L + 128)], BF16)
        vd[r] = C.din("v%d" % r, [NU, 128, r * (L // 128 + 1), 64], BF16)
    Ed = C.din("E", [NU, 3, 128, 256], BF16)
    seld = C.din("sel", [128, 64], F32)
    o_at = C.dout("o_at", [NU, 64, SEQ], BF16)

    sel = C.sb("sel_s", [128, 64], F32)
    S.dma("sp", sel[:], seld, writes=["sel"])
    qs = [C.sb("qs%d" % i, [64, SEQ], BF16) for i in range(2)]
    ks = [C.sb("ks%d" % i, [64, SEQ + 128 * 16], BF16) for i in range(2)]
    vs = [C.sb("vs%d" % i, [128, SEQ // 128 + 16, 128], BF16) for i in range(2)]
    for i in range(2):
        S.op("dve", (lambda i=i: nc.vector.memset(vs[i][:, :, 64:128], 1.0)), writes=["vs%d" % i])
    Et = [C.sb("Et%d" % i, [128, 256], BF16) for i in range(2)]
    acc = C.sb("acc", [128, SEQ], F32)
    rd = [C.sb("rd%d" % i, [64, 512], F32) for i in range(2)]
    pt = [C.sb("pt%d" % i, [128, 256], BF16) for i in range(3)]
    ost = [C.sb("ost%d" % i, [64, 512], BF16) for i in range(2)]
    pss = [C.ps("pss%d" % i, [128, 256], F32) for i in range(3)]
    pso = [C.ps("pso%d" % i, [128, 128], F32) for i in range(3)]
    psel = C.ps("psel", [64, 512], F32)
    cnt = {"l": 0, "p": 0, "o": 0}

    def load(u, bi):
        r = DILS[bi]
        L = SEQ // r
        b = cnt["l"] % 2
        cnt["l"] += 1
        S.dma("sp", qs[b][:, :], qd[r][u, :, :], writes=["qs%d" % b])
        S.dma("sp", ks[b][:, :r * (L + 128)], kd[r][u, :, :], writes=["ks%d" % b])
        S.dma("sp", vs[b][:, :r * (L // 128 + 1), 0:64], vd[r][u, :, :, :], writes=["vs%d" % b])
        S.dma("sp", Et[b][:], Ed[u, bi, :, :], writes=["Et%d" % b])
        return b

    def branch(u, bi, b):
        r = DILS[bi]
        L = SEQ // r
        nb_n = L // 128
        its = [(rho, nb) for rho in range(r) for nb in range(nb_n)]
        pidx = {}

        def stage1(rho, nb):
            p = cnt["p"] % 3
            cnt["p"] += 1
            pidx[(rho, nb)] = p
            q0 = rho * L + nb * 128
            for c in range(2):
                k0 = rho * (L + 128) + (nb + c) * 128
                S.op("pe", (lambda c=c, k0=k0: nc.tensor.matmul(pss[p][:, c * 128:(c + 1) * 128], ks[b][:, k0:k0 + 128], qs[b][:, q0:q0 + 128],
                                                                 start=True, stop=True)),
                     reads=["ks%d" % b, "qs%d" % b], writes=["pss%d" % p])
            S.op("act", (lambda: nc.scalar.activation(out=pt[p][:], in_=pss[p][:], func=AF.Exp, scale=0.125)), reads=["pss%d" % p], writes=["pt%d" % p])
            S.op("dve", (lambda: nc.vector.tensor_tensor(out=pt[p][:], in0=pt[p][:], in1=Et[b][:], op=ALU.mult)),
                 reads=["pt%d" % p, "Et%d" % b], writes=["pt%d" % p])
            if nb == 0:
                S.op("dve", (lambda: nc.vector.memset(pt[p][0:64, 0:128], 0.0)), writes=["pt%d" % p])
            if nb == nb_n - 1:
                S.op("dve", (lambda: nc.vector.memset(pt[p][64:128, 128:256], 0.0)), writes=["pt%d" % p])

        def stage2(rho, nb):
            p = pidx[(rho, nb)]
            for c in range(2):
                vt = rho * (nb_n + 1) + nb + c
                S.op("pe", (lambda c=c, vt=vt: nc.tensor.matmul(pso[p][:, :], vs[b][:, vt, :], pt[p][:, c * 128:(c + 1) * 128], start=(c == 0), stop=(c == 1))),
                     reads=["vs%d" % b, "pt%d" % p], writes=["pso%d" % p])
            t0 = rho + r * 128 * nb
            sl = slice(t0, t0 + 127 * r + 1, r) if r > 1 else slice(t0, t0 + 128)
            last = (rho == r - 1 and nb == nb_n - 1)
            wk = [("acc", u, bi, rho, nb)] + ([("accdone", u, bi)] if last else [])
            rk = ["pso%d" % p] + ([("accdone", u, bi - 1)] if bi > 0 else ([("finished", u - 1)] if u > 0 else []))
            if bi == 0:
                S.op("act", (lambda: nc.scalar.copy(out=acc[:, sl], in_=pso[p][:, :])), reads=rk, writes=wk)
            else:
                S.op("dve", (lambda: nc.vector.tensor_tensor(out=acc[:, sl], in0=acc[:, sl], in1=pso[p][:, :], op=ALU.add)),
                     reads=rk, writes=wk)

        stage1(*its[0])
        if len(its) > 1:
            stage1(*its[1])
        for idx, it in enumerate(its):
            if idx + 2 < len(its):
                stage1(*its[idx + 2])
            stage2(*it)

    def finish_unit(u):
        for g in range(SEQ // 512):
            o = cnt["o"] % 2
            cnt["o"] += 1
            sl = slice(g * 512, (g + 1) * 512)
            S.op("pe", (lambda sl=sl: nc.tensor.matmul(psel[:, :], sel[:], acc[:, sl], start=True, stop=True)), reads=[("accdone", u, 2), "sel"], writes=["psel"])
            S.op("dve", (lambda o=o: nc.vector.reciprocal(out=rd[o][:], in_=psel[:, :])), reads=["psel"], writes=["rd%d" % o])
            S.op("dve", (lambda o=o, sl=sl: nc.vector.tensor_tensor(out=ost[o][:], in0=acc[0:64, sl], in1=rd[o][:], op=ALU.mult)),
                 reads=[("accdone", u, 2), "rd%d" % o], writes=["ost%d" % o] + ([("finished", u)] if g == SEQ // 512 - 1 else []))
            S.dma("pool", o_at[u, :, sl], ost[o][:], reads=["ost%d" % o])

    seqs = [(u, bi) for u in range(NU) for bi in range(3)]
    bnext = load(*seqs[0])
    for idx, (u, bi) in enumerate(seqs):
        bcur = bnext
        if idx + 1 < len(seqs):
            bnext = load(*seqs[idx + 1])
        branch(u, bi, bcur)
        if bi == 2:
            finish_unit(u)
    return C.finish()
RTILES = SEQ // 128
NCH = SEQ // 64


def rec_consts():
    s = np.arange(128)[:, None]
    t = np.arange(128)[None, :]
    same = (s // 64) == (t // 64)
    tri = same & (s <= t)
    refm = same & ((s % 64) <= 31)
    dm = tri.astype(np.float32) - refm.astype(np.float32)
    mask = tri.astype(np.uint32)
    ind = np.zeros((128, 6), np.float32)
    sl = np.arange(128)
    for ch in range(2):
        inch = (sl // 64) == ch
        ind[:, 3 * ch + 0] = inch & ((sl % 64) > 31)
        ind[:, 3 * ch + 1] = inch
        ind[:, 3 * ch + 2] = inch & ((sl % 64) <= 31)
    return dm, mask, ind


def build_Br(NU=NU, SEQ=SEQ, dbg=9):
    RTILES = SEQ // 128
    NCH = SEQ // 64
    C = Ctx()
    nc, S = C.nc, C.S
    qd = C.din("rq", [NU, SEQ, 128], F32)
    kd = C.din("rk", [NU, SEQ, 128], F32)
    ld = C.din("rl", [NU, SEQ, 128], F32)
    vd = C.din("rv", [NU, SEQ, 128], BF16)
    dmd = C.din("dm", [128, 128], F32)
    mkd = C.din("mask", [128, 128], mybir.dt.uint32)
    idd = C.din("ind", [128, 6], F32)
    ied = C.din("identb", [128, 128], BF16)
    o_r = C.dout("o_r", [NU, 128, SEQ], F32)

    dm = C.sb("dm_s", [128, 128], F32)
    S.dma("sp", dm[:], dmd, writes=["dm"])
    mk = C.sb("mk_s", [128, 128], mybir.dt.uint32)
    S.dma("sp", mk[:], mkd, writes=["mk"])
    ind = C.sb("ind_s", [128, 6], F32)
    S.dma("sp", ind[:], idd, writes=["ind"])
    idb = C.sb("idb", [128, 128], BF16)
    S.dma("sp", idb[:], ied, writes=["idb"])

    NB = 5
    qt = [C.sb("qt%d" % i, [128, 128], F32) for i in range(NB)]
    kt = [C.sb("kt%d" % i, [128, 128], F32) for i in range(NB)]
    lt = [C.sb("lt%d" % i, [128, 128], F32) for i in range(NB)]
    vt = [C.sb("vt%d" % i, [128, 128], BF16) for i in range(NB)]
    e1 = [C.sb("e1%d" % i, [128, 128], F32) for i in range(3)]
    e2 = [C.sb("e2%d" % i, [128, 128], F32) for i in range(3)]
    qtl = [C.sb("qtl%d" % i, [128, 128], BF16) for i in range(3)]
    ktl = [C.sb("ktl%d" % i, [128, 128], BF16) for i in range(3)]
    kfm = [C.sb("kfm%d" % i, [128, 128], BF16) for i in range(3)]
    qfm = C.sb("qfm", [128, SEQ], BF16)
    EX = C.sb("EX", [128, RTILES, 6], F32)
    AT = C.sb("AT", [128, RTILES, 2, 128], BF16)
    U = C.sb("U", [128, 64, NCH], F32)
    Sc = C.sb("Sc", [128, 64, NCH], F32)
    Sp = C.sb("Sp", [128, NCH, 64], BF16)
    Gc = C.sb("Gc", [128, NCH], F32)
    ER = C.sb("ER", [128, NCH], F32)
    ot = [C.sb("ot%d" % i, [64, 2, 128], F32) for i in range(2)]
    bk0 = C.ps("bk0", [128, 256], F32)
    pd = [bk0[:, 0:128], bk0[:, 0:128], bk0[:, 0:128]]
    pst = bk0[:, 128:136]
    bk1 = C.ps("bk1", [128, 2, 128], BF16)
    ptq = bk1[:, 0, :]
    ptk = bk1[:, 1, :]
    paA = C.ps("paA", [128, 128], F32)
    paB = C.ps("paB", [128, 128], F32)
    pah = [paA, paB]
    puA = C.ps("puA", [128, 128], F32)
    puB = C.ps("puB", [128, 128], F32)
    puc = [puA, puB]
    poA = C.ps("poA", [64, 128], F32)
    poB = C.ps("poB", [64, 128], F32)
    poh = [poA, poB]

    S.op("dve", lambda: nc.vector.memset(AT[:], 0.0), writes=["ATz"] + [("AT", m_, h_) for m_ in range(RTILES) for h_ in range(2)])

    def load1(u, m):
        b = m % NB
        r = slice(m * 128, (m + 1) * 128)
        S.dma("sp", qt[b][:], qd[u, r, :], writes=["qt%d" % b])
        S.dma("sp", kt[b][:], kd[u, r, :], writes=["kt%d" % b])
        S.dma("sp", lt[b][:], ld[u, r, :], writes=["lt%d" % b])
        S.dma("sp", vt[b][:], vd[u, r, :], writes=["vt%d" % b])

    def pass1(u, m, stage):
        b = m % NB
        a = m % 3
        if stage == 2:
            return pass1_s2(u, m, b, a)
        if stage == 3:
            return pass1_s3(u, m, b, a)
        S.op("pe", lambda: nc.tensor.matmul(pd[a], dm[:], lt[b][:], start=True, stop=True), reads=["dm", "lt%d" % b], writes=["b0"])
        S.op("act", lambda: nc.scalar.activation(out=e1[a][:], in_=pd[a], func=AF.Exp), reads=["b0"], writes=["e1%d" % a])
        S.op("act", lambda: nc.scalar.activation(out=e2[a][:], in_=pd[a], func=AF.Exp, scale=-1.0), reads=["b0"], writes=["e2%d" % a])
        S.op("dve", lambda: nc.vector.tensor_tensor(out=qtl[a][:], in0=qt[b][:], in1=e1[a][:], op=ALU.mult), reads=["qt%d" % b, "e1%d" % a], writes=["qtl%d" % a])
        S.op("dve", lambda: nc.vector.tensor_tensor(out=ktl[a][:], in0=kt[b][:], in1=e2[a][:], op=ALU.mult), reads=["kt%d" % b, "e2%d" % a], writes=["ktl%d" % a])
        return

    def pass1_s2(u, m, b, a):
        S.op("pe", lambda: nc.tensor.transpose(ptq, qtl[a][:], idb[:]), reads=["qtl%d" % a, "idb"], writes=["b1"])
        S.op("pe", lambda: nc.tensor.transpose(ptk, ktl[a][:], idb[:]), reads=["ktl%d" % a, "idb"], writes=["b1"])
        S.op("act", lambda: nc.scalar.copy(out=qfm[:, m * 128:(m + 1) * 128], in_=ptq), reads=["b1"], writes=[("qfm", m)])
        S.op("act", lambda: nc.scalar.copy(out=kfm[a][:], in_=ptk), reads=["b1"], writes=["kfm%d" % a])
        S.op("pe", lambda: nc.tensor.matmul(pst[:, 0:6], lt[b][:], ind[:], start=True, stop=True), reads=["lt%d" % b, "ind"], writes=["b0"])
        S.op("act", lambda: nc.scalar.activation(out=EX[:, m, :], in_=pst[:, 0:6], func=AF.Exp), reads=["b0"], writes=[("EX", m)])
        return

    def pass1_s3(u, m, b, a):
        for hh in range(2):
            hs_ = slice(64 * hh, 64 * hh + 64)
            S.op("pe", (lambda hh=hh, hs_=hs_: nc.tensor.matmul(pah[hh][:, :], kfm[a][hs_, :], qfm[hs_, m * 128:(m + 1) * 128], start=True, stop=True)),
                 reads=["kfm%d" % a, ("qfm", m)], writes=["pa%d" % hh])
        for hh in range(2):
            S.op("dve", (lambda hh=hh: nc.vector.copy_predicated(out=AT[:, m, hh, :], mask=mk[:], data=pah[hh][:, :])),
                 reads=["pa%d" % hh, "mk", "ATz"], writes=[("AT", m, hh)])
        for ch in range(2):
            cs_ = slice(64 * ch, 64 * ch + 64)
            S.op("pe", (lambda ch=ch, cs_=cs_: nc.tensor.matmul(puc[ch][:, :], ktl[a][cs_, :], vt[b][cs_, :], start=True, stop=True)),
                 reads=["ktl%d" % a, "vt%d" % b], writes=["pu%d" % ch])
            n = 2 * m + ch
            for hh in range(2):
                hs_ = slice(64 * hh, 64 * hh + 64)
                eng = "dve"
                if eng == "dve":
                    S.op("dve", (lambda ch=ch, hs_=hs_, n=n: nc.vector.tensor_scalar(out=U[hs_, :, n], in0=puc[ch][hs_, hs_], scalar1=EX[hs_, m, 3 * ch:3 * ch + 1],
                                                                                    scalar2=None, op0=ALU.mult)),
                         reads=["pu%d" % ch, ("EX", m)], writes=[("U", n, hh)])
                else:
                    S.op("act", (lambda ch=ch, hs_=hs_, n=n: nc.scalar.activation(out=U[hs_, :, n], in_=puc[ch][hs_, hs_], func=AF.Copy, scale=EX[hs_, m, 3 * ch:3 * ch + 1])),
                         reads=["pu%d" % ch, ("EX", m)], writes=[("U", n, hh)])

    def scan(u):
        allU = [("U", n, hh) for n in range(NCH) for hh in range(2)]
        allEX = [("EX", m) for m in range(RTILES)]
        exv = EX[:, :, :].rearrange("p m (c k) -> p (m c) k", c=2)
        S.op("dve", lambda: nc.vector.tensor_copy(out=Gc[:], in_=exv[:, :, 1]), reads=allEX, writes=["Gc"])
        S.op("dve", lambda: nc.vector.tensor_copy(out=ER[:], in_=exv[:, :, 2]), reads=allEX, writes=["ER"])
        for dv in range(64):
            S.op("dve", (lambda dv=dv: nc.vector.tensor_tensor_scan(out=Sc[:, dv, :], data0=Gc[:], data1=U[:, dv, :], initial=0.0, op0=ALU.mult, op1=ALU.add)),
                 reads=allU + ["Gc"], writes=[("Sc", dv)])
        S.op("dve", lambda: nc.vector.memset(Sp[:, 0, :], 0.0), writes=[("Sp", -1)])
        for dv in range(64):
            S.op("dve", (lambda dv=dv: nc.vector.tensor_tensor(out=Sp[:, 1:NCH, dv], in0=Sc[:, dv, 0:NCH - 1], in1=ER[:, 1:NCH], op=ALU.mult)),
                 reads=[("Sc", dv), "ER"], writes=[("Sp", dv)])

    def load2(u, m):
        b = m % NB
        S.dma("sp", vt[b][:], vd[u, m * 128:(m + 1) * 128, :], writes=["vt%d" % b])

    def pass2(u, m):
        b = m % NB
        a = m % 2
        allSp = [("Sp", dv) for dv in range(-1, 64)]
        for hh in range(2):
            hs_ = slice(64 * hh, 64 * hh + 64)
            S.op("pe", (lambda hh=hh, hs_=hs_: nc.tensor.matmul(poh[hh][:, :], vt[b][:, hs_], AT[:, m, hh, :], start=True, stop=False)),
                 reads=["vt%d" % b, ("AT", m, hh)], writes=["po%d" % hh])
            for ch in range(2):
                cs_ = slice(64 * ch, 64 * ch + 64)
                n = 2 * m + ch
                S.op("pe", (lambda hh=hh, hs_=hs_, cs_=cs_, n=n, ch=ch: nc.tensor.matmul(poh[hh][:, cs_], Sp[hs_, n, :], qfm[hs_, m * 128 + 64 * ch: m * 128 + 64 * ch + 64],
                                                                                 start=False, stop=(ch == 1))),
                     reads=allSp + [("qfm", m)], writes=["po%d" % hh])
        for hh in range(2):
            S.op("act", (lambda hh=hh: nc.scalar.copy(out=ot[a][:, hh, :], in_=poh[hh][:, :])), reads=["po%d" % hh], writes=["ot%d" % a])
        S.dma("pool", o_r[u, :, m * 128:(m + 1) * 128].rearrange("(h d) t -> d h t", h=2), ot[a][:], reads=["ot%d" % a])

    for u in range(NU):
        for m0 in range(min(4, RTILES)):
            load1(u, m0)
        pass1(u, 0, 1)
        if RTILES > 1:
            pass1(u, 1, 1)
        pass1(u, 0, 2)
        for m in range(RTILES):
            if m + 4 < RTILES:
                load1(u, m + 4)
            if m + 2 < RTILES:
                pass1(u, m + 2, 1)
            if m + 1 < RTILES:
                pass1(u, m + 1, 2)
            pass1(u, m, 3)
        if dbg < 6:
            continue
        scan(u)
        if dbg < 7:
            continue
        load2(u, 0)
        load2(u, 1)
        for m in range(RTILES):
            if m + 2 < RTILES:
                load2(u, m + 2)
            pass2(u, m)
    return C.finish()
def att_tables():
    slopes = 2.0 ** (-8.0 * np.arange(1, 7) / 6)
    p = np.arange(128)[:, None, None]
    c = np.arange(2)[None, :, None]
    i = np.arange(128)[None, None, :]
    rel = 128 * c + p - 64 - i
    out = np.zeros((6, 3, 128, 256), np.float32)
    for h in range(6):
        for bi, r in enumerate((1, 4, 16)):
            e = np.where(np.abs(rel) <= 64, np.exp(-slopes[h] * r * np.abs(rel)), 0.0)
            out[h, bi] = e.reshape(128, 256)
    return out.astype(NPBF)


def att_layout(q_fm, k_fm, v_tm, r):
    T = q_fm.shape[1]
    L = T // r
    nb = L // 128
    qr = q_fm.reshape(64, L, r).transpose(0, 2, 1).reshape(64, r * L)
    kr = np.zeros((64, r, L + 128), k_fm.dtype)
    kr[:, :, 64:64 + L] = k_fm.reshape(64, L, r).transpose(0, 2, 1)
    vr = np.zeros((r, L + 128, 64), v_tm.dtype)
    vr[:, 64:64 + L] = v_tm.reshape(L, r, 64).transpose(1, 0, 2)
    vr = vr.reshape(r, nb + 1, 128, 64).transpose(2, 0, 1, 3).reshape(128, r * (nb + 1), 64)
    return np.ascontiguousarray(qr), np.ascontiguousarray(kr.reshape(64, r * (L + 128))), np.ascontiguousarray(vr)
_PROG_CACHE = {}


def _prog(name, fn, *a):
    return fn(*a)[0]


def _t128(v):
    return np.ascontiguousarray(np.asarray(v, np.float32).reshape(-1, 128).T)


def _run(nc, maps):
    res = run_bass_kernel_spmd(nc, maps, core_ids=list(range(8)))
    return res.results


def kernel(x, c, w_ada, b_ada, norm1_w, w_in, conv_a_w, conv_a_b, ln_a_w, ln_a_b,
           lb_gamma, rec_norm_w, w_out, norm2_w, w_up, conv_f_w, w_down, final_norm_w):
    f32 = np.float32
    x = np.asarray(x, f32)
    B, T, _ = x.shape
    HALF = T // 2
    cores = [(b, h) for b in range(B) for h in range(2)]
    xT = [np.ascontiguousarray(x[b].T) for b in range(B)]
    lbg_rep = np.ascontiguousarray(np.broadcast_to(np.asarray(lb_gamma, f32).reshape(1, -1), (128, 1536)))
    E_all = att_tables()
    dmc, maskc, indc = rec_consts()
    identf = np.eye(128, dtype=f32)
    identb = identf.astype(NPBF)
    bdm = np.kron(np.eye(2), np.ones((64, 64))).astype(f32)
    selm = np.zeros((128, 64), f32)
    selm[64 + np.arange(64), np.arange(64)] = 1.0
    depth = w_in.shape[0]
    for l in range(depth):
        final = (l == depth - 1)
        maps = []
        for (b, h) in cores:
            maps.append({"xT": np.ascontiguousarray(xT[b][:, h * HALF:(h + 1) * HALF]), "ct": _t128(c[b]),
                         "wada": np.asarray(w_ada[l], f32), "bada": _t128(b_ada[l]), "n1w": _t128(norm1_w[l]),
                         "win": np.asarray(w_in[l], f32), "lbg": lbg_rep})
        ra = _run(_prog("A%d" % l, build_A, l), maps)

        def catT(name, b):
            return np.concatenate([ra[2 * b][name], ra[2 * b + 1][name]], axis=1)

        def catR(name, b):
            return np.concatenate([ra[2 * b][name], ra[2 * b + 1][name]], axis=0)
        aT = [catT("o_aT", b) for b in range(B)]
        qkT = [catT("o_qkT", b) for b in range(B)]
        gT = [catT("o_gT", b) for b in range(B)]
        vat = [catR("o_v", b) for b in range(B)]
        qr = [catR("o_qr", b) for b in range(B)]
        lf = [catR("o_lf", b) for b in range(B)]
        kk = [catR("o_kk", b) for b in range(B)]
        ir = [catR("o_ir", b) for b in range(B)]
        del ra
        maps = []
        cw = np.ascontiguousarray(np.asarray(conv_a_w[l], f32).T.reshape(2, 128, CK).transpose(1, 0, 2))

        def t2(v):
            return np.ascontiguousarray(np.asarray(v, f32).reshape(2, 128).T)
        for (b, h) in cores:
            pad = np.zeros((256, T + 2 * CH), aT[b].dtype)
            pad[:, CH:CH + T] = aT[b]
            maps.append({"aT": np.ascontiguousarray(pad[:, h * HALF:h * HALF + HALF + 2 * CH]), "cw_d": cw, "cb_d": t2(conv_a_b[l]),
                         "lw_d": t2(ln_a_w[l]), "lb_d": t2(ln_a_b[l]), "ident_d": identf})
        rc = _run(_prog("Bc", build_Bc), maps)
        acT = [np.concatenate([rc[2 * b]["o_ac"], rc[2 * b + 1]["o_ac"]], axis=1) for b in range(B)]
        del rc
        units = [(b, h) for b in range(B) for h in range(6)]
        maps = []
        for ci in range(8):
            us = units[3 * ci:3 * ci + 3]
            m = {"E": np.ascontiguousarray(np.stack([E_all[h] for (_, h) in us])), "sel": selm}
            for r in DILS:
                ql, kl, vl = [], [], []
                for (b, h) in us:
                    q_fm = qkT[b][h * 64:(h + 1) * 64, :]
                    k_fm = qkT[b][384 + h * 64:384 + (h + 1) * 64, :]
                    v_tm = vat[b][:, h * 64:(h + 1) * 64]
                    a_, b_, c_ = att_layout(np.ascontiguousarray(q_fm), np.ascontiguousarray(k_fm), np.ascontiguousarray(v_tm), r)
                    ql.append(a_)
                    kl.append(b_)
                    vl.append(c_)
                m["q%d" % r] = np.stack(ql)
                m["k%d" % r] = np.stack(kl)
                m["v%d" % r] = np.stack(vl)
            maps.append(m)
        rb = _run(_prog("Ba", build_Ba), maps)
        atT = [np.zeros((384, T), NPBF) for _ in range(B)]
        for ui, (b, h) in enumerate(units):
            atT[b][h * 64:(h + 1) * 64, :] = rb[ui // 3]["o_at"][ui % 3]
        del rb
        runits = [(b, d, hp) for b in range(B) for d in range(2) for hp in range(3)]
        maps = []
        for ci in range(8):
            us = runits[3 * ci:3 * ci + 3]
            ql, kl, ll, vl = [], [], [], []
            for (b, d, hp) in us:
                cs = slice(hp * 128, (hp + 1) * 128)
                cs2 = slice(d * 384 + hp * 128, d * 384 + (hp + 1) * 128)
                q_, k_, l_, v_ = qr[b][:, cs], kk[b][:, cs2], lf[b][:, cs2], ir[b][:, cs]
                if d == 1:
                    q_, k_, l_, v_ = q_[::-1], k_[::-1], l_[::-1], v_[::-1]
                ql.append(np.ascontiguousarray(q_))
                kl.append(np.ascontiguousarray(k_))
                ll.append(np.ascontiguousarray(l_))
                vl.append(np.ascontiguousarray(v_))
            maps.append({"rq": np.stack(ql), "rk": np.stack(kl), "rl": np.stack(ll), "rv": np.stack(vl),
                         "dm": dmc, "mask": maskc, "ind": indc, "identb": identb})
        rr = _run(_prog("Br", build_Br), maps)
        ofT = [np.zeros((384, T), f32) for _ in range(B)]
        obT = [np.zeros((384, T), f32) for _ in range(B)]
        for ui, (b, d, hp) in enumerate(runits):
            o = rr[ui // 3]["o_r"][ui % 3]
            if d == 0:
                ofT[b][hp * 128:(hp + 1) * 128, :] = o
            else:
                obT[b][hp * 128:(hp + 1) * 128, :] = o[:, ::-1]
        del rr
        cfw = np.ascontiguousarray(np.asarray(conv_f_w[l], f32).T.reshape(44, 128, 3).transpose(1, 0, 2))

        def padcols(a, h):
            p = np.zeros((a.shape[0], T + 2), a.dtype)
            p[:, 1:T + 1] = a
            return np.ascontiguousarray(p[:, h * HALF:h * HALF + HALF + 2])
        maps = []
        for (b, h) in cores:
            flags = np.ones((128, 2), f32)
            if h == 0:
                flags[:, 0] = 0.0
            if h == 1:
                flags[:, 1] = 0.0
            maps.append({"xT": padcols(xT[b], h), "acT": padcols(acT[b], h), "atT": padcols(atT[b], h), "ofT": padcols(ofT[b], h),
                         "obT": padcols(obT[b], h), "gT": padcols(gT[b], h), "rnw": _t128(rec_norm_w[l]), "ct": _t128(c[b]),
                         "wada": np.asarray(w_ada[l], f32), "bada": _t128(b_ada[l]), "n2w": _t128(norm2_w[l]), "fnw": _t128(final_norm_w),
                         "wout": np.asarray(w_out[l], f32), "wup": np.asarray(w_up[l], f32), "wdown": np.asarray(w_down[l], f32),
                         "cfw": cfw, "flags": flags, "bdm": bdm})
        rd = _run(_prog("CD%d" % final, build_CD, final), maps)
        xT = [np.concatenate([rd[2 * b]["o_xT"], rd[2 * b + 1]["o_xT"]], axis=1) for b in range(B)]
        del rd
    out = np.stack([np.ascontiguousarray(xT[b].T) for b in range(B)]).astype(f32)
    return out
```
